# Optimizing a Trainium2 kernel written in Bass

```python
import jax, jax.numpy as jnp
from jax import lax
import numpy as np

D_MODEL = 1024
BATCH = 4
SEQ = 4096
DEPTH = 4
DEC_BATCH = 32
DEC_SEQ = 8
PAST_LEN = 8192
PAGE_SIZE = 128

D_MIX = D_MODEL
D_ATT = D_MIX // 2
D_CONV = D_MIX - D_ATT
N_HEADS = 8
HEAD_DIM = D_ATT // N_HEADS
N_KV = 2
GROUP = N_HEADS // N_KV
D_KV = N_KV * HEAD_DIM
CMP_BLOCK = 32
SEL_BLOCK = 64
CMP_PER_SEL = SEL_BLOCK // CMP_BLOCK
TOP_N = 16
WINDOW = 512
CONV_W = 3
D_PLE = 256
Q_BLOCK = 128
N_KV_SLOTS = 4
EPS = 1e-6
NEG = -1e30
INVALID = -1e9
FORCE_BONUS = 1e3
IN_SPLITS = (D_ATT, D_KV, D_KV, D_KV, D_KV, D_KV, D_KV, 3 * N_HEADS, D_ATT,
             D_CONV, D_CONV, D_CONV, D_CONV)
D_IN = 2 * D_ATT + 6 * D_KV + 3 * N_HEADS + 4 * D_CONV

kernel_name = 'hymba_nsa_shortconv_ple_step'


def rmsnorm(x, g):
    xf = x.astype(jnp.float32)
    y = xf * lax.rsqrt(jnp.mean(xf * xf, -1, keepdims=True) + EPS)
    return (y * g.astype(jnp.float32)).astype(x.dtype)


def alibi_slopes():
    h = jnp.arange(1, N_HEADS + 1, dtype=jnp.float32)
    return jnp.exp2(-8.0 * h / N_HEADS).reshape(N_KV, GROUP)


def masked_softmax(s, mask):
    s = jnp.where(mask, s, NEG)
    e = jnp.where(mask, jnp.exp(s - jnp.max(s, -1, keepdims=True)), 0.0)
    return e / jnp.maximum(jnp.sum(e, -1, keepdims=True), 1e-30)


def split_cols(a):
    outs, o = [], 0
    for n in IN_SPLITS:
        outs.append(a[..., o:o + n])
        o += n
    return outs


def in_proj(x, g_norm, w_in, g_q, g_k):
    bn, L = x.shape[:2]
    h = rmsnorm(x, g_norm)
    q, kc, vc, ks, vs, kw, vw, gl, za, bg, cg, hv, zb = split_cols(h @ w_in)
    heads = lambda a, n: a.reshape(bn, L, n, HEAD_DIM)
    q = rmsnorm(heads(q, N_HEADS), g_q).reshape(bn, L, N_KV, GROUP, HEAD_DIM)
    ks = rmsnorm(heads(ks, N_KV), g_k[1])
    kw = rmsnorm(heads(kw, N_KV), g_k[2])
    gates = jax.nn.sigmoid(gl.reshape(bn, L, N_KV, GROUP, 3))
    u = cg * hv
    return (q, heads(kc, N_KV), heads(vc, N_KV), ks, heads(vs, N_KV), kw, heads(vw, N_KV),
            gates, za, bg, u, zb)


def compress(rows, pos, w):
    bn, T = rows.shape[:2]
    blocks = rows.reshape(bn, T // CMP_BLOCK, CMP_BLOCK, N_KV, HEAD_DIM) + pos[:, None, :]
    return jnp.einsum('bclkd,lde->bcke', blocks, w)


def global_keys(kc, vc, ks, vs, cmp_pos, w_phi, g_kc):
    bn, T = kc.shape[:2]
    t_pad = -(-T // SEL_BLOCK) * SEL_BLOCK
    pad = lambda a: jnp.pad(a, ((0, 0), (0, t_pad - T), (0, 0), (0, 0)))
    kcmp = rmsnorm(compress(pad(kc), cmp_pos[0], w_phi[0]), g_kc)
    vcmp = compress(pad(vc), cmp_pos[1], w_phi[1])
    nb = t_pad // SEL_BLOCK
    blk = lambda a: pad(a).reshape(bn, nb, SEL_BLOCK, N_KV, HEAD_DIM).transpose(0, 3, 1, 2, 4)
    return kcmp, vcmp, blk(ks), blk(vs)


def nsa_core(q, t_pos, gates, kcmp, vcmp, ksb, vsb, kw, vw, k_pos):
    f32 = jnp.float32
    bn, nq = q.shape[:2]
    m = alibi_slopes()
    qf = q.astype(f32) * HEAD_DIM ** -0.5
    tp = t_pos[:, None]
    nc = kcmp.shape[1]
    c_end = jnp.arange(nc) * CMP_BLOCK + (CMP_BLOCK - 1)
    dist = tp - c_end[None, :]
    s = (jnp.einsum('bqkgd,bckd->bqkgc', qf, kcmp.astype(f32))
         - m[:, :, None] * dist[:, None, None, :].astype(f32))
    p_cmp = masked_softmax(s, (dist >= 0)[:, None, None, :])
    o_cmp = jnp.einsum('bqkgc,bckd->bqkgd', p_cmp, vcmp.astype(f32))
    nb = ksb.shape[2]
    imp = p_cmp.sum(3).reshape(bn, nq, N_KV, nb, CMP_PER_SEL).sum(-1)
    blk = jnp.arange(nb)[None, :]
    cur = tp // SEL_BLOCK
    valid = blk * SEL_BLOCK <= tp
    forced = (blk == 0) | (blk == cur) | (blk == cur - 1)
    score = jnp.where(valid[None, :, None, :],
                      imp + jnp.where(forced, FORCE_BONUS, 0.0)[None, :, None, :], INVALID)
    _, idx = lax.top_k(score, min(TOP_N, nb))
    n = idx.shape[-1]
    bi = jnp.arange(bn)[:, None, None, None]
    hi = jnp.arange(N_KV)[None, None, :, None]
    kb = ksb[bi, hi, idx].astype(f32)
    vb = vsb[bi, hi, idx].astype(f32)
    key_pos = idx[..., None] * SEL_BLOCK + jnp.arange(SEL_BLOCK)
    dist = t_pos[None, :, None, None, None] - key_pos
    s = (jnp.einsum('bqkgd,bqknsd->bqkgns', qf, kb)
         - m[None, None, :, :, None, None] * dist[:, :, :, None].astype(f32))
    s = s.reshape(bn, nq, N_KV, GROUP, n * SEL_BLOCK)
    mask = (dist >= 0).reshape(bn, nq, N_KV, 1, n * SEL_BLOCK)
    p = masked_softmax(s, mask).reshape(bn, nq, N_KV, GROUP, n, SEL_BLOCK)
    o_sel = jnp.einsum('bqkgns,bqknsd->bqkgd', p, vb)
    dist = tp - k_pos[None, :]
    mask = ((dist >= 0) & (dist < WINDOW) & (k_pos[None, :] >= 0))[:, None, None, :]
    s = (jnp.einsum('bqkgd,blkd->bqkgl', qf, kw.astype(f32))
         - m[:, :, None] * dist[:, None, None, :].astype(f32))
    o_win = jnp.einsum('bqkgl,blkd->bqkgd', masked_softmax(s, mask), vw.astype(f32))
    g = gates.astype(f32)
    return g[..., 0:1] * o_cmp + g[..., 1:2] * o_sel + g[..., 2:3] * o_win


def short_conv(u, prev, w):
    L = u.shape[1]
    up = jnp.concatenate([prev.astype(u.dtype), u], 1)
    y = sum(w[k] * up[:, k:k + L] for k in range(CONV_W))
    return y, up[:, up.shape[1] - (CONV_W - 1):]


def mix_out(o_att, za, bg, y_conv, zb, w_out):
    o = jnp.concatenate([o_att * jax.nn.silu(za), bg * y_conv * jax.nn.silu(zb)], -1)
    return o @ w_out


def ple_add(x, p, w_ple, w_pg, g_ple):
    return x + (p @ w_ple) * jax.nn.sigmoid(rmsnorm(x, g_ple) @ w_pg)


def prompt_layer(x, p, g_norm, w_in, g_q, g_k, cmp_pos, w_phi, conv_w, w_out, w_ple, w_pg, g_ple):
    bn, S = x.shape[:2]
    q, kc, vc, ks, vs, kw, vw, gates, za, bg, u, zb = in_proj(x, g_norm, w_in, g_q, g_k)
    kcmp, vcmp, ksb, vsb = global_keys(kc, vc, ks, vs, cmp_pos, w_phi, g_k[0])
    kvw = jnp.stack([kw, vw], 2)
    kvw_pad = jnp.pad(kvw, ((0, 0), (WINDOW, 0), (0, 0), (0, 0), (0, 0)))

    def qblock(qi):
        q0 = qi * Q_BLOCK
        sl = lambda a: lax.dynamic_slice_in_dim(a, q0, Q_BLOCK, 1)
        band = lax.dynamic_slice_in_dim(kvw_pad, q0, WINDOW + Q_BLOCK, 1)
        t_pos = q0 + jnp.arange(Q_BLOCK)
        k_pos = q0 - WINDOW + jnp.arange(WINDOW + Q_BLOCK)
        return nsa_core(sl(q), t_pos, sl(gates), kcmp, vcmp, ksb, vsb,
                        band[:, :, 0], band[:, :, 1], k_pos)

    o = lax.map(qblock, jnp.arange(S // Q_BLOCK))
    o = jnp.moveaxis(o, 0, 1).reshape(bn, S, D_ATT).astype(x.dtype)
    yc, conv_state = short_conv(u, jnp.zeros((bn, CONV_W - 1, D_CONV), u.dtype), conv_w)
    x = x + mix_out(o, za, bg, yc, zb, w_out)
    x = ple_add(x, p, w_ple, w_pg, g_ple)
    new_kv = jnp.stack([kc, vc, ks, vs], 2)
    new_win = kvw[:, S - min(WINDOW, S):]
    return x, new_kv, new_win, conv_state


def sample_layer(x, p, cache_kv_l, cache_win_l, state_conv_l, page_table,
                 g_norm, w_in, g_q, g_k, cmp_pos, w_phi, conv_w, w_out, w_ple, w_pg, g_ple):
    bn, L = x.shape[:2]
    past = cache_kv_l[page_table]
    past = past.reshape(bn, -1, N_KV_SLOTS, N_KV, HEAD_DIM)
    P = past.shape[1]
    q, kc, vc, ks, vs, kw, vw, gates, za, bg, u, zb = in_proj(x, g_norm, w_in, g_q, g_k)
    cat = lambda a, j: jnp.concatenate([past[:, :, j].astype(a.dtype), a], 1)
    kcmp, vcmp, ksb, vsb = global_keys(cat(kc, 0), cat(vc, 1), cat(ks, 2), cat(vs, 3),
                                       cmp_pos, w_phi, g_k[0])
    wb = cache_win_l.shape[1]
    kvw = jnp.concatenate([cache_win_l.astype(x.dtype), jnp.stack([kw, vw], 2)], 1)
    t_pos = P + jnp.arange(L)
    k_pos = P - wb + jnp.arange(wb + L)
    o = nsa_core(q, t_pos, gates, kcmp, vcmp, ksb, vsb, kvw[:, :, 0], kvw[:, :, 1], k_pos)
    o = o.reshape(bn, L, D_ATT).astype(x.dtype)
    yc, conv_state = short_conv(u, state_conv_l, conv_w)
    x = x + mix_out(o, za, bg, yc, zb, w_out)
    x = ple_add(x, p, w_ple, w_pg, g_ple)
    new_kv = jnp.stack([kc, vc, ks, vs], 2)
    return x, new_kv, kvw[:, L:], conv_state


def setup_inputs(seed: int = 0) -> dict:
    key = jax.random.key(seed)
    ks = jax.random.split(key, 20)
    n_pages = PAST_LEN // PAGE_SIZE
    n_used = DEC_BATCH * n_pages
    n_pool = n_used + max(1, n_used // 4)
    page_table = jax.random.permutation(ks[0], n_pool)[:n_used].astype(jnp.int32)
    page_table = page_table.reshape(DEC_BATCH, n_pages)
    wb = min(WINDOW, PAST_LEN)
    nrm = lambda k, shape, s=1.0: s * jax.random.normal(k, shape, jnp.float32)
    return {
        'x_prompt': nrm(ks[1], (BATCH, SEQ, D_MODEL)),
        'x_sample': nrm(ks[2], (DEC_BATCH, DEC_SEQ, D_MODEL)),
        'cache_kv': nrm(ks[3], (DEPTH, n_pool, PAGE_SIZE, N_KV_SLOTS, N_KV, HEAD_DIM)),
        'cache_win': nrm(ks[4], (DEPTH, DEC_BATCH, wb, 2, N_KV, HEAD_DIM)),
        'state_conv': nrm(ks[5], (DEPTH, DEC_BATCH, CONV_W - 1, D_CONV)),
        'page_table': page_table,
        'p_prompt': nrm(ks[6], (DEPTH, BATCH, SEQ, D_PLE)),
        'p_sample': nrm(ks[7], (DEPTH, DEC_BATCH, DEC_SEQ, D_PLE)),
        'g_norm': 1.0 + nrm(ks[8], (DEPTH, D_MODEL), 0.02),
        'w_in': nrm(ks[9], (DEPTH, D_MODEL, D_IN), D_MODEL ** -0.5),
        'g_q': 1.0 + nrm(ks[10], (DEPTH, HEAD_DIM), 0.02),
        'g_k': 1.0 + nrm(ks[11], (DEPTH, 3, HEAD_DIM), 0.02),
        'cmp_pos': nrm(ks[12], (DEPTH, 2, CMP_BLOCK, HEAD_DIM), 0.02),
        'w_phi': nrm(ks[13], (DEPTH, 2, CMP_BLOCK, HEAD_DIM, HEAD_DIM), (CMP_BLOCK * HEAD_DIM) ** -0.5),
        'conv_w': nrm(ks[14], (DEPTH, CONV_W, D_CONV), CONV_W ** -0.5),
        'w_out': nrm(ks[15], (DEPTH, D_MIX, D_MODEL), D_MIX ** -0.5),
        'w_ple': nrm(ks[16], (DEPTH, D_PLE, D_MODEL), D_PLE ** -0.5),
        'w_pg': nrm(ks[17], (DEPTH, D_MODEL, D_MODEL), D_MODEL ** -0.5),
        'g_ple': 1.0 + nrm(ks[18], (DEPTH, D_MODEL), 0.02),
    }


def reference(x_prompt, x_sample, cache_kv, cache_win, state_conv, page_table, p_prompt, p_sample,
              g_norm, w_in, g_q, g_k, cmp_pos, w_phi, conv_w, w_out, w_ple, w_pg, g_ple):
    xp, xs = x_prompt, x_sample
    kv_p, win_p, conv_p, kv_s, win_s, conv_s = [], [], [], [], [], []
    for i in range(DEPTH):
        w = (g_norm[i], w_in[i], g_q[i], g_k[i], cmp_pos[i], w_phi[i], conv_w[i], w_out[i],
             w_ple[i], w_pg[i], g_ple[i])
        xp, a, b, c = prompt_layer(xp, p_prompt[i], *w)
        xs, d, e, f = sample_layer(xs, p_sample[i], cache_kv[i], cache_win[i], state_conv[i],
                                   page_table, *w)
        kv_p.append(a); win_p.append(b); conv_p.append(c)
        kv_s.append(d); win_s.append(e); conv_s.append(f)
    return (xp, xs, jnp.stack(kv_p), jnp.stack(win_p), jnp.stack(conv_p),
            jnp.stack(kv_s), jnp.stack(win_s), jnp.stack(conv_s))
```

```python
import numpy as np
import concourse.bass as bass
import concourse.mybir as mybir
from concourse.bass_utils import run_bass_kernel_spmd

F32 = mybir.dt.float32
BF16 = mybir.dt.bfloat16
I32 = mybir.dt.int32
AF = mybir.ActivationFunctionType
ALU = mybir.AluOpType
AX = mybir.AxisListType

N_DMA_SEMS = 40

D_MODEL = 1024
SEQ = 4096
DEPTH = 4
D_IN = 3864
D_PLE = 256
NEGB = -30000.0
EPS = 1e-6

C_Q, C_KC, C_VC, C_KS, C_VS, C_KW, C_VW, C_GL, C_ZA, C_BG, C_CG, C_HV, C_ZB = (
    0, 512, 640, 768, 896, 1024, 1152, 1280, 1304, 1816, 2328, 2840, 3352)
COL_GROUPS = [(0, 512, "pj_q"), (512, 1024, "pj_kv"), (1024, 1304, "pj_w"), (1304, 1816, "pj_za"),
              (1816, 2328, "pj_bg"), (2328, 2840, "pj_cg"), (2840, 3352, "pj_hv"), (3352, 3864, "pj_zb")]


class _Op:
    __slots__ = ("eng", "fn", "reads", "writes", "dma", "deps", "needs_inc", "cnt", "dsem", "dval")

    def __init__(self, eng, fn, reads, writes, dma):
        self.eng = eng
        self.fn = fn
        self.reads = reads
        self.writes = writes
        self.dma = dma
        self.deps = ()
        self.needs_inc = False
        self.cnt = 0
        self.dsem = -1
        self.dval = 0


class KB:
    def __init__(self, nc):
        self.nc = nc
        self.engs = {"pe": nc.tensor, "act": nc.scalar, "dve": nc.vector, "pool": nc.gpsimd, "sp": nc.sync}
        self.ops = []

    def op(self, eng, fn, reads=(), writes=(), dma=False):
        self.ops.append(_Op(eng, fn, tuple(reads), tuple(writes), dma))

    def dma(self, out, in_, reads, writes, q="sp", **kw):
        self.op(q, lambda e: e.dma_start(out=out, in_=in_, **kw), reads, writes, dma=True)

    def finalize(self):
        nc = self.nc
        ops = self.ops
        last_w = {}
        readers = {}
        for i, o in enumerate(ops):
            deps = set()
            for r in o.reads:
                j = last_w.get(r)
                if j is not None:
                    deps.add(j)
            for w in o.writes:
                j = last_w.get(w)
                if j is not None:
                    deps.add(j)
                for j in readers.get(w, ()):
                    deps.add(j)
            deps.discard(i)
            keep = []
            for j in deps:
                p = ops[j]
                if (not p.dma) and (not o.dma) and p.eng == o.eng:
                    if o.eng == "pe":
                        continue
                    raw = any(last_w.get(r) == j for r in o.reads) or any(last_w.get(w) == j for w in o.writes)
                    if not raw:
                        continue
                keep.append(j)
            o.deps = tuple(keep)
            for j in keep:
                if not ops[j].dma:
                    ops[j].needs_inc = True
            for r in o.reads:
                readers.setdefault(r, []).append(i)
            for w in o.writes:
                last_w[w] = i
                readers[w] = []
        cnt = {k: 0 for k in self.engs}
        ndma = 0
        dvals = [0] * N_DMA_SEMS
        for o in ops:
            if o.dma:
                k = ndma % N_DMA_SEMS
                ndma += 1
                dvals[k] += 16
                o.dsem = k
                o.dval = dvals[k]
            elif o.needs_inc:
                cnt[o.eng] += 1
                o.cnt = cnt[o.eng]
        esem = {k: nc.alloc_semaphore("es_" + k) for k in self.engs}
        dsem = [nc.alloc_semaphore("ds_%d" % k) for k in range(N_DMA_SEMS)]
        wm_e = {k: {k2: 0 for k2 in self.engs} for k in self.engs}
        wm_d = {k: [0] * N_DMA_SEMS for k in self.engs}
        n_wait = 0
        for o in ops:
            E = self.engs[o.eng]
            need_e = {}
            need_d = {}
            for j in o.deps:
                p = ops[j]
                if p.dma:
                    need_d[p.dsem] = max(need_d.get(p.dsem, 0), p.dval)
                else:
                    need_e[p.eng] = max(need_e.get(p.eng, 0), p.cnt)
            if o.dma and o.dval > 16:
                need_d[o.dsem] = max(need_d.get(o.dsem, 0), o.dval - 16)
            for f, c in need_e.items():
                if wm_e[o.eng][f] < c:
                    E.wait_ge(esem[f], c)
                    wm_e[o.eng][f] = c
                    n_wait += 1
            for k, v in need_d.items():
                if wm_d[o.eng][k] < v:
                    E.wait_ge(dsem[k], v)
                    wm_d[o.eng][k] = v
                    n_wait += 1
            ins = o.fn(E)
            if o.dma:
                ins.then_inc(dsem[o.dsem], 16)
            elif o.needs_inc:
                ins.then_inc(esem[o.eng], 1)
        sp = self.engs["sp"]
        for k in range(N_DMA_SEMS):
            if dvals[k] > 0:
                sp.wait_ge(dsem[k], dvals[k])
        self.stats = dict(n_ops=len(ops), n_wait=n_wait, n_dma=ndma, cnt=cnt)
        return self.stats


def _consts(n_tiles):
    c = {}
    c["ident"] = np.eye(128, dtype=np.float32)
    kpos = np.arange(SEQ)
    c["kaug"] = np.stack([128.0 * (kpos // 128), (kpos % 128).astype(np.float64), np.ones(SEQ)]).astype(np.float32)
    ce = np.arange(128) * 32 + 31
    c["kcaug"] = np.stack([128.0 * (ce // 128), (ce % 128).astype(np.float64), np.ones(128)]).astype(np.float32)
    m = 2.0 ** (-(np.arange(8) + 1.0))
    qa = np.zeros((3, 8, 128), np.float32)
    qa[0] = m[:, None]
    qa[1] = m[:, None]
    qa[2] = -64.0 * m[:, None]
    c["qaug"] = np.ascontiguousarray(qa[:, :, 0])
    qm = np.ones((3, 65), np.float32)
    qm[2] = 2 * np.arange(65) + 1
    c["qmul"] = qm
    k = np.arange(128)[:, None]
    t = np.arange(128)[None, :]
    cb = np.where(k <= t, 0.0, NEGB).astype(np.float32)
    wb = np.where(t < k, 0.0, NEGB).astype(np.float32)
    c["cmask"] = np.tile(cb, (1, 4))
    c["wmask"] = np.tile(wb, (1, 4))
    st = np.zeros((64, SEQ), np.float32)
    st[np.arange(SEQ) // 64, np.arange(SEQ)] = 1.0
    c["stair"] = st
    cp = np.arange(252)[None, :] - 124
    tl = np.arange(128)[:, None]
    c["mcmp"] = np.where(32 * cp + 31 <= tl, 0.0, NEGB).astype(np.float32)
    bon = np.zeros((32, 128, 64), np.float32)
    for qb in range(32):
        tt = (qb * 128 + np.arange(128))[:, None]
        bb = np.arange(64)[None, :]
        cur = tt // 64
        valid = bb * 64 <= tt
        forced = (bb == 0) | (bb == cur) | (bb == cur - 1)
        bon[qb] = np.where(valid, np.where(forced, 1e3, 0.0), -1e9)
    c["bonus"] = bon
    sh = np.zeros((4, 128, 128), np.float32)
    for tq in range(128):
        if tq - 2 >= 0:
            sh[0, tq - 2, tq] = 1.0
        if tq - 1 >= 0:
            sh[1, tq - 1, tq] = 1.0
    sh[2, 126, 0] = 1.0
    sh[2, 127, 1] = 1.0
    sh[3, 127, 0] = 1.0
    c["shm"] = sh
    c["iota"] = np.arange(128, dtype=np.float32).reshape(128, 1)
    c["knaug"] = np.stack([np.full(8, 8192.0), np.arange(8.0), np.ones(8)]).astype(np.float32)
    bs = np.full((8, 136), -1e9, np.float32)
    bs[:, 0:129] = 0.0
    bs[:, [0, 127, 128]] = 1e3
    c["bons"] = bs
    kk = np.arange(8)[:, None]
    qq = np.arange(8)[None, :]
    c["cmask8"] = np.tile(np.where(kk <= qq, 0.0, NEGB).astype(np.float32), (1, 4))
    return c


def build_program(n_layers=DEPTH, n_tiles=32, debug=False, n_pool=2560, n_sb=4):
    nc = bass.Bass("TRN2", target_bir_lowering=False)
    kb = KB(nc)
    NT = n_tiles
    S = NT * 128

    def din(name, shape, dt=F32):
        return nc.dram_tensor(name, list(shape), dt, kind="ExternalInput").ap()

    def dout(name, shape, dt=F32):
        return nc.dram_tensor(name, list(shape), dt, kind="ExternalOutput").ap()

    def sb(name, shape, dt=F32):
        return nc.alloc_sbuf_tensor(name, list(shape), dt).ap()

    def ps(name, shape, dt=F32):
        return nc.alloc_psum_tensor(name, list(shape), dt).ap()

    xp = din("xp", [S, D_MODEL])
    pp = din("pp", [n_layers, S, D_PLE])
    g_norm = din("g_norm", [n_layers, D_MODEL])
    w_in = din("w_in", [n_layers, D_MODEL, D_IN])
    g_q = din("g_q", [n_layers, 64])
    g_k = din("g_k", [n_layers, 3, 64])
    cmp_pos = din("cmp_pos", [n_layers, 2, 32, 64])
    w_phi = din("w_phi", [n_layers, 2, 32, 64, 64])
    conv_w = din("conv_w", [n_layers, 3, 512])
    w_out = din("w_out", [n_layers, 1024, 1024])
    w_ple = din("w_ple", [n_layers, 256, 1024])
    w_pg = din("w_pg", [n_layers, 1024, 1024])
    g_ple = din("g_ple", [n_layers, 1024])
    c_ident = din("c_ident", [128, 128])
    c_kaug = din("c_kaug", [3, SEQ])
    c_kcaug = din("c_kcaug", [3, 128])
    c_qaug = din("c_qaug", [3, 8])
    c_qmul = din("c_qmul", [3, 65])
    c_iota = din("c_iota", [128, 1])
    c_knaug = din("c_knaug", [3, 8])
    c_bons = din("c_bons", [8, 136])
    c_cmask8 = din("c_cmask8", [8, 32])
    c_cmask = din("c_cmask", [128, 512])
    c_wmask = din("c_wmask", [128, 512])
    c_stair = din("c_stair", [64, SEQ])
    c_mcmp = din("c_mcmp", [128, 252])
    c_bonus = din("c_bonus", [32, 128, 64])
    c_shm = din("c_shm", [4, 128, 128])

    xs_in = din("xs", [n_sb, 8, D_MODEL])
    pps = din("pps", [n_layers, n_sb, 8, D_PLE])
    pool = din("pool", [n_layers * n_pool * 128, 512])
    cw_in = din("cwin", [n_layers, n_sb, 512, 256])
    sc_in = din("scin", [n_layers, n_sb, 2, 512])
    ptab = din("ptab", [n_sb, 64], I32)
    y_s = dout("y_s", [n_sb, 8, D_MODEL])
    kv_s = dout("kv_s", [n_layers, n_sb, 8, 512])
    win_s = dout("win_s", [n_layers, n_sb, 512, 256])
    conv_s = dout("conv_s", [n_layers, n_sb, 2, 512])
    xs_scr = nc.dram_tensor("xs_scr", [n_sb, 8, D_MODEL], F32, kind="Internal").ap()
    y_p = dout("y_p", [S, D_MODEL])
    kv_p = dout("kv_p", [n_layers, S, 512])
    WROWS = min(512, S)
    win_p = dout("win_p", [n_layers, WROWS, 256])
    conv_p = dout("conv_p", [n_layers, 2, 512])
    xscr = nc.dram_tensor("xscr", [S, D_MODEL], F32, kind="Internal").ap()

    W_in = sb("W_in", [128, 8, D_IN], BF16)
    W_out = sb("W_out", [128, 8, 1024], BF16)
    W_pg = sb("W_pg", [128, 8, 1024], BF16)
    W_ple = sb("W_ple", [128, 2, 1024], BF16)
    W_phi = sb("W_phi", [64, 2, 32, 64], BF16)
    gn = sb("gn", [128, 8])
    gp = sb("gp", [128, 8])
    gq8 = sb("gq8", [128, 64])
    gk = sb("gk", [128, 3, 64])
    cw = sb("cw", [128, 3, 512])
    posrep = sb("posrep", [128, 2, 64])

    identb = sb("identb", [128, 128], BF16)
    shm = sb("shm", [128, 4, 128])
    cmaskb = sb("cmaskb", [128, 512], BF16)
    wmaskb = sb("wmaskb", [128, 512], BF16)
    stairb = sb("stairb", [64, SEQ], BF16)
    mcmp = sb("mcmp", [128, 252])
    qaugf = sb("qaugf", [128, 8])
    qmulf = sb("qmulf", [128, 65])

    KsT = sb("KsT", [128, 2, SEQ], BF16)
    KwT = sb("KwT", [128, 2, 6 * 128], BF16)
    Vs = sb("Vs", [128, 32, 2, 65], BF16)
    Vw = sb("Vw", [128, 6, 2, 65], BF16)
    KcT = sb("KcT", [128, 2, 256], BF16)
    Vc = sb("Vc", [128, 2, 2, 64], BF16)

    xt = [sb("xt0", [128, 1024])] * 2
    pt = [sb("pt0", [128, 256])] * 2
    bon = [sb("bon%d" % j, [128, 64]) for j in range(2)]
    st = sb("st", [128, 8])
    xT = sb("xT", [128, 1024], BF16)
    proj = sb("proj", [128, D_IN])
    tmpa = sb("tmpa", [128, 512])
    tmpb = sb("tmpb", [128, 512])
    stg = [tmpa, tmpb]
    sm = sb("sm", [128, 64])
    qnb = sb("qnb", [128, 512], BF16)
    QT = sb("QT", [128, 1024], BF16)
    knb = sb("knb", [128, 256], BF16)
    gates = sb("gates", [128, 24])
    kcp = sb("kcp", [128, 256], BF16)
    kcT = sb("kcT", [64, 4, 128], BF16)
    cst = sb("cst", [8, 8])
    kcn = sb("kcn", [8, 64])
    kcnb = sb("kcnb", [8, 64], BF16)
    vcnb = sb("vcnb", [8, 64], BF16)
    pcm = sb("pcm", [128, 512])
    score = sb("score", [128, 64])
    score2 = sb("score2", [128, 64])
    m8 = sb("m8", [128, 16])
    selb = sb("selb", [128, 64], BF16)
    selTg = [sb("selT%d" % j, [64, 512], BF16) for j in range(2)]
    PT = [sb("PT%d" % j, [128, 512], BF16) for j in range(2)]
    pnb = PT[0]
    pnT = PT[1]
    junkb = pcm.bitcast(BF16)
    PT2 = [sb("PTx%d" % j, [128, 512], BF16) for j in range(2)]
    iota = sb("iota", [128, 1])
    idxi = sb("idxi", [128, 64], I32)
    idxf = sb("idxf", [128, 64])
    KnT = sb("KnT", [128, 2, 2, 8], BF16)
    Vn = sb("Vn", [8, 2, 2, 65], BF16)
    scS = sb("scS", [8, 136])
    scS2 = sb("scS2", [8, 136])
    selbS = sb("selbS", [8, 128], BF16)
    selTS = sb("selTS", [64, 2, 2, 32], BF16)
    bonsS = sb("bonsS", [8, 136])
    cmask8 = sb("cmask8", [8, 32], BF16)
    oat = proj[:, C_Q:C_Q + 512]
    ocf = sb("ocf", [128, 16])
    omix = sb("omix", [128, 1024], BF16)
    xb = omix
    u = proj[:, C_CG:C_CG + 512]
    v0 = tmpa
    v1 = pcm
    carry = sb("carry", [128, 512])
    sg = tmpa
    pb = sb("pb", [128, 256], BF16)
    pTT = sb("pTT", [128, 256], BF16)

    pA = [ps("pA%d" % j, [128, 512]) for j in range(2)]
    pT = ps("pT", [128, 1024], BF16)
    pS = [ps("pS%d" % j, [128, 512]) for j in range(2)]
    pO = ps("pO", [128, 4, 65])
    pW = ps("pW", [128, 4, 65])
    pC = ps("pC", [128, 512])

    dumped = set()

    def dump(name, ap, res, shape, dt=F32):
        if not debug or name in dumped:
            return
        dumped.add(name)
        d = nc.dram_tensor("dbg_" + name, list(shape), dt, kind="ExternalOutput").ap()
        kb.dma(d, ap, list(res), [("dbg", name)])

    def mm(out, lhsT, rhs, start, stop, R, W, skip=False):
        if skip:
            kb.op("pe", lambda e: e.matmul(out, lhsT=lhsT, rhs=rhs, start=start, stop=stop, skip_group_check=True), R, W)
        else:
            kb.op("pe", lambda e: e.matmul(out, lhsT=lhsT, rhs=rhs, start=start, stop=stop), R, W)

    def tr(out, in_, ident, R, W):
        kb.op("pe", lambda e: e.transpose(out=out, in_=in_, identity=ident), R, W)

    def act(out, in_, func, R, W, **kw):
        kb.op("act", lambda e: e.activation(out=out, in_=in_, func=func, **kw), R, W)

    def cp(eng, out, in_, R, W):
        if eng == "act":
            kb.op("act", lambda e: e.copy(out=out, in_=in_), R, W)
        else:
            kb.op(eng, lambda e: e.tensor_copy(out=out, in_=in_), R, W)

    def tt(eng, out, in0, in1, op, R, W):
        kb.op(eng, lambda e: e.tensor_tensor(out=out, in0=in0, in1=in1, op=op), R, W)

    def ts(eng, out, in0, s1, s2, op0, op1, R, W):
        if op1 is None:
            kb.op(eng, lambda e: e.tensor_scalar(out=out, in0=in0, scalar1=s1, scalar2=None, op0=op0), R, W)
        else:
            kb.op(eng, lambda e: e.tensor_scalar(out=out, in0=in0, scalar1=s1, scalar2=s2, op0=op0, op1=op1), R, W)

    def red(eng, out, in_, axis, op, R, W):
        kb.op(eng, lambda e: e.tensor_reduce(out=out, in_=in_, axis=axis, op=op), R, W)

    def rstd(col_in, col_out, n, R):
        act(col_out, col_in, AF.Sqrt, R, R, scale=1.0 / n, bias=EPS)
        kb.op("dve", lambda e: e.reciprocal(out=col_out, in_=col_out), R, R)

    kb.dma(tmpa[:, 0:128], c_ident, ["tmpa"], ["tmpa"])
    cp("dve", identb, tmpa[:, 0:128], ["tmpa"], ["identb"])
    kb.dma(shm, c_shm.rearrange("k a b -> a k b"), [], ["shm"])
    kb.dma(mcmp, c_mcmp, [], ["mcmp"])
    kb.dma(qaugf[64:67, :], c_qaug, [], ["qaugf"])
    kb.dma(qmulf[64:67, :], c_qmul, [], ["qmulf"])
    kb.dma(stg[0][:, 0:512], c_cmask, ["tmpa"], ["tmpa"])
    cp("dve", cmaskb, stg[0][:, 0:512], ["tmpa"], ["cmaskb"])
    kb.dma(stg[1][:, 0:512], c_wmask, ["tmpb"], ["tmpb"])
    cp("dve", wmaskb, stg[1][:, 0:512], ["tmpb"], ["wmaskb"])
    for j in range(SEQ // 512):
        s_ = stg[j % 2]
        kb.dma(s_[0:64, :], c_stair[:, j * 512:(j + 1) * 512], [("tmpa", "tmpb")[j % 2]], [("tmpa", "tmpb")[j % 2]])
        cp("dve", stairb[:, j * 512:(j + 1) * 512], s_[0:64, :], [("tmpa", "tmpb")[j % 2]], ["stairb"])
    for j in range(SEQ // 512):
        s_ = stg[j % 2]
        kb.dma(s_[64:67, :], c_kaug[:, j * 512:(j + 1) * 512], [("tmpa", "tmpb")[j % 2]], [("tmpa", "tmpb")[j % 2]])
        for g in range(2):
            cp("dve", KsT[64:67, g, j * 512:(j + 1) * 512], s_[64:67, :], [("tmpa", "tmpb")[j % 2]], ["KsT"])
    kb.dma(stg[0][64:67, 0:128], c_kcaug, ["tmpa"], ["tmpa"])
    for g in range(2):
        for hf in range(2):
            cp("dve", KcT[64:67, g, hf * 128:(hf + 1) * 128], stg[0][64:67, 0:128], ["tmpa"], ["KcT"])
    kb.dma(iota, c_iota, [], ["iota"])
    kb.dma(bonsS, c_bons, [], ["bonsS"])
    kb.dma(stg[1][0:8, 0:32], c_cmask8, ["tmpb"], ["tmpb"])
    cp("dve", cmask8, stg[1][0:8, 0:32], ["tmpb"], ["cmask8"])
    kb.dma(stg[1][64:67, 0:8], c_knaug, ["tmpb"], ["tmpb"])
    for a_ in range(2):
        for g in range(2):
            cp("dve", KnT[64:67, a_, g, :], stg[1][64:67, 0:8], ["tmpb"], ["KnT"])
    kb.op("dve", lambda e: e.memset(Vn[:, :, :, 64:65], 1.0), [], ["Vn"])
    kb.op("dve", lambda e: e.memset(Vs[:, :, :, 64:65], 1.0), [], ["Vs"])
    kb.op("dve", lambda e: e.memset(Vw[:, :, :, 64:65], 1.0), [], ["Vw"])

    stg_ctr = [0]

    def load_cast(dst_ap_fn, src_ap_fn, nchunks, scale_col_fn, wname, parts=128):
        for j in range(nchunks):
            k = stg_ctr[0] % 2
            stg_ctr[0] += 1
            sname = ("tmpa", "tmpb")[k]
            src, width = src_ap_fn(j)
            kb.dma(stg[k][0:parts, 0:width], src, [sname], [sname])
            dst = dst_ap_fn(j)
            sc = scale_col_fn(j) if scale_col_fn is not None else None
            eng = "dve" if j % 2 == 0 else "pool"
            if sc is None:
                cp(eng, dst, stg[k][0:parts, 0:width], [sname], [wname])
            else:
                ts(eng, dst, stg[k][0:parts, 0:width], sc, None, ALU.mult, None, [sname, "gvec"], [wname])

    def load_layer(l):
        kb.dma(gn, g_norm[l].rearrange("(c p) -> p c", p=128), [], ["gvec"], allow_slow_non_contiguous=True)
        kb.dma(gp, g_ple[l].rearrange("(c p) -> p c", p=128), [], ["gvec"], allow_slow_non_contiguous=True)
        kb.dma(gq8, g_q[l].partition_broadcast(128), [], ["gsm"])
        kb.dma(gk, g_k[l].partition_broadcast(128), [], ["gsm"])
        kb.dma(cw, conv_w[l].partition_broadcast(128), [], ["gsm"])
        for c4 in range(4):
            kb.dma(posrep[c4 * 32:(c4 + 1) * 32, :, :], cmp_pos[l].rearrange("kv l d -> l kv d"), [], ["gsm"])
        ts("dve", gq8, gq8, 0.125, None, ALU.mult, None, ["gsm"], ["gsm"])
        wcols = [(j * 512, min(512, D_IN - j * 512)) for j in range(8)]
        for kc in range(8):
            load_cast(lambda j, kc=kc: W_in[:, kc, wcols[j][0]:wcols[j][0] + wcols[j][1]],
                      lambda j, kc=kc: (w_in[l, kc * 128:(kc + 1) * 128, wcols[j][0]:wcols[j][0] + wcols[j][1]], wcols[j][1]),
                      8, lambda j, kc=kc: gn[:, kc:kc + 1], "W_in")
        for kc in range(8):
            load_cast(lambda j, kc=kc: W_out[:, kc, j * 512:(j + 1) * 512],
                      lambda j, kc=kc: (w_out[l, kc * 128:(kc + 1) * 128, j * 512:(j + 1) * 512], 512), 2, None, "W_out")
        for kc in range(8):
            load_cast(lambda j, kc=kc: W_pg[:, kc, j * 512:(j + 1) * 512],
                      lambda j, kc=kc: (w_pg[l, kc * 128:(kc + 1) * 128, j * 512:(j + 1) * 512], 512), 2,
                      lambda j, kc=kc: gp[:, kc:kc + 1], "W_pg")
        for kc in range(2):
            load_cast(lambda j, kc=kc: W_ple[:, kc, j * 512:(j + 1) * 512],
                      lambda j, kc=kc: (w_ple[l, kc * 128:(kc + 1) * 128, j * 512:(j + 1) * 512], 512), 2, None, "W_ple")
        for kv in range(2):
            for hh in range(4):
                k = stg_ctr[0] % 2
                stg_ctr[0] += 1
                sname = ("tmpa", "tmpb")[k]
                kb.dma(stg[k][0:64, :].rearrange("d (l e) -> d l e", e=64),
                       w_phi[l, kv, hh * 8:(hh + 1) * 8].rearrange("l d e -> d l e"), [sname], [sname])
                cp("dve", W_phi[:, kv, hh * 8:(hh + 1) * 8, :],
                   stg[k][0:64, :].rearrange("d (l e) -> d l e", e=64), [sname], ["W_phi"])

    def load_tile_inputs(l, i):
        s = i % 2
        src = xp if l == 0 else xscr
        kb.dma(xt[s], src[i * 128:(i + 1) * 128, :], [("xd", i)], ["xt0"])
        kb.dma(pt[s], pp[l, i * 128:(i + 1) * 128, :], [], ["pt0"])
        kb.dma(bon[s], c_bonus[i], [], ["bon%d" % s])

    def prompt_tile(l, i):
        s = i % 2
        X = xt[s]
        xn = "xt0"
        last_layer = (l == n_layers - 1)
        cp("dve", xb, X, [xn], ["omix"])
        for c in range(8):
            tr(pT[:, c * 128:(c + 1) * 128], xb[:, c * 128:(c + 1) * 128], identb, ["omix", "identb"], ["pT"])
        cp("act", xT, pT, ["pT"], ["xT"])
        kb.op("dve", lambda e: e.memset(st, 0.0), ["st"], ["st"])
        act(junkb, X, AF.Square, [xn, "st"], ["pcm", "st"], accum_out=st[:, 0:1])
        rstd(st[:, 0:1], st[:, 1:2], 1024.0, ["st"])
        dump("xT", xT, ["xT"], [128, 1024], BF16)
        dump("xb", xb, ["omix"], [128, 1024], BF16)
        dump("W_in0", W_in[:, 0, 0:512], ["W_in"], [128, 512], BF16)
        dump("st", st, ["st"], [128, 8])
        for gi, (c0, c1, rn) in enumerate(COL_GROUPS):
            w = c1 - c0
            bank = pA[gi % 2]
            bn = "pA%d" % (gi % 2)
            for kc in range(8):
                mm(bank[:, 0:w], xT[:, kc * 128:(kc + 1) * 128], W_in[:, kc, c0:c1], kc == 0, kc == 7,
                   ["xT", "W_in"], [bn])
            act(proj[:, c0:c1], bank[:, 0:w], AF.Copy, [bn, "st"], [rn], scale=st[:, 1:2])
        dump("proj", proj, [r for _, _, r in COL_GROUPS], [128, D_IN])
        tt("dve", kcp.rearrange("p (a h d) -> p a h d", a=2, h=2), proj[:, C_KC:C_KC + 256].rearrange("p (a h d) -> p a h d", a=2, h=2),
           posrep.unsqueeze(2).to_broadcast([128, 2, 2, 64]), ALU.add, ["pj_kv", "gsm"], ["kcp"])
        for j in range(4):
            tr(pT[0:64, j * 128:(j + 1) * 128], kcp[:, j * 64:(j + 1) * 64], identb, ["kcp", "identb"], ["pT"])
        cp("act", kcT.rearrange("d a t -> d (a t)"), pT[0:64, 0:512], ["pT"], ["kcT"])
        for kv in range(2):
            for ll in range(32):
                lhs = kcT[:, 2 * kv:2 * kv + 2, :].rearrange("d h (c l) -> d h c l", l=32)[:, :, :, ll]
                mm(pC[0:8, kv * 64:(kv + 1) * 64], lhs, W_phi[:, kv, ll, :], ll == 0, ll == 31, ["kcT", "W_phi"], ["pC"])
        q3 = proj[:, C_Q:C_Q + 512].rearrange("p (h d) -> p h d", d=64)
        tt("dve", tmpa, proj[:, C_Q:C_Q + 512], proj[:, C_Q:C_Q + 512], ALU.mult, ["pj_q"], ["tmpa"])
        red("dve", sm[:, 0:8], tmpa.rearrange("p (h d) -> p h d", d=64), AX.X, ALU.add, ["tmpa"], ["sm"])
        rstd(sm[:, 0:8], sm[:, 0:8], 64.0, ["sm"])
        tt("dve", tmpa.rearrange("p (h d) -> p h d", d=64), q3, sm[:, 0:8].unsqueeze(2).to_broadcast([128, 8, 64]),
           ALU.mult, ["pj_q", "sm"], ["tmpa"])
        tt("pool", qnb.rearrange("p (h d) -> p h d", d=64), tmpa.rearrange("p (h d) -> p h d", d=64),
           gq8.unsqueeze(1).to_broadcast([128, 8, 64]), ALU.mult, ["tmpa", "gsm"], ["qnb"])
        for h in range(8):
            tr(pT[0:64, h * 128:(h + 1) * 128], qnb[:, h * 64:(h + 1) * 64], identb, ["qnb", "identb"], ["pT"])
        cp("act", QT[0:64, :], pT[0:64, :], ["pT"], ["QT"])
        ts("dve", QT[64:67, :].rearrange("p (h q) -> p h q", h=8), qaugf[64:67, :].unsqueeze(2).to_broadcast([3, 8, 128]),
           qmulf[64:67, i:i + 1], None, ALU.mult, None, ["qaugf", "qmulf"], ["QT"])
        for which, (c0, gi_) in enumerate(((C_KS, 1), (C_KW, 2))):
            rn = "pj_kv" if which == 0 else "pj_w"
            k3 = proj[:, c0:c0 + 128].rearrange("p (h d) -> p h d", d=64)
            tt("dve", tmpb[:, 0:128], proj[:, c0:c0 + 128], proj[:, c0:c0 + 128], ALU.mult, [rn], ["tmpb"])
            red("dve", sm[:, 8 + 2 * which:10 + 2 * which], tmpb[:, 0:128].rearrange("p (h d) -> p h d", d=64),
                AX.X, ALU.add, ["tmpb"], ["sm"])
            rstd(sm[:, 8 + 2 * which:10 + 2 * which], sm[:, 8 + 2 * which:10 + 2 * which], 64.0, ["sm"])
            tt("dve", k3, k3, sm[:, 8 + 2 * which:10 + 2 * which].unsqueeze(2).to_broadcast([128, 2, 64]),
               ALU.mult, [rn, "sm"], [rn])
            tt("dve", k3, k3, gk[:, gi_, :].unsqueeze(1).to_broadcast([128, 2, 64]), ALU.mult, [rn, "gsm"], [rn])
            cp("pool", knb[:, which * 128:(which + 1) * 128], proj[:, c0:c0 + 128], [rn], ["knb"])
        for j in range(4):
            tr(pT[0:64, j * 128:(j + 1) * 128], knb[:, j * 64:(j + 1) * 64], identb, ["knb", "identb"], ["pT"])
        ws = i % 6
        for g in range(2):
            cp("act", KsT[0:64, g, i * 128:(i + 1) * 128], pT[0:64, g * 128:(g + 1) * 128], ["pT"], [("KsT", i)])
            cp("act", KwT[0:64, g, ws * 128:(ws + 1) * 128], pT[0:64, (2 + g) * 128:(3 + g) * 128], ["pT"], [("KwT", ws)])
            cp("dve", KwT[64:67, g, ws * 128:(ws + 1) * 128], KsT[64:67, g, i * 128:(i + 1) * 128], ["KsT"], [("KwT", ws)])
        cp("pool", Vs[:, i, :, 0:64], proj[:, C_VS:C_VS + 128].rearrange("p (h d) -> p h d", d=64), ["pj_kv"], [("Vs", i)])
        cp("pool", Vw[:, ws, :, 0:64], proj[:, C_VW:C_VW + 128].rearrange("p (h d) -> p h d", d=64), ["pj_w"], [("Vw", ws)])
        kb.dma(kv_p[l, i * 128:(i + 1) * 128, :], proj[:, C_KC:C_KC + 512], ["pj_kv"], [("kv_p", l, i)])
        r0 = i * 128 - (S - WROWS)
        if r0 >= 0:
            kb.dma(win_p[l, r0:r0 + 128, :], proj[:, C_KW:C_KW + 256], ["pj_w"], [("win_p", l, i)])
        act(gates, proj[:, C_GL:C_GL + 24], AF.Sigmoid, ["pj_w"], ["gates"])
        kb.op("dve", lambda e: e.memset(cst, 0.0), ["cst"], ["cst"])
        act(kcn, pC[0:8, 0:64], AF.Square, ["pC", "cst"], ["kcn", "cst"], accum_out=cst[:, 0:1])
        rstd(cst[:, 0:1], cst[:, 1:2], 64.0, ["cst"])
        ts("dve", kcn, pC[0:8, 0:64], cst[:, 1:2], None, ALU.mult, None, ["pC", "cst"], ["kcn"])
        tt("dve", kcnb, kcn, gk[0:8, 0, :], ALU.mult, ["kcn", "gsm"], ["kcnb"])
        cp("act", vcnb, pC[0:8, 64:128], ["pC"], ["vcnb"])
        tr(pT[0:64, 0:8], kcnb, identb[0:8, 0:8], ["kcnb", "identb"], ["pT"])
        cp("act", KcT[0:64, :, 4 * i:4 * i + 4], pT[0:64, 0:8].rearrange("d (h c) -> d h c", h=2), ["pT"], ["KcT"])
        for g in range(2):
            kb.dma(Vc[4 * i:4 * i + 4, g, 0, :], vcnb[4 * g:4 * g + 4, :], ["vcnb", "Vc"], ["Vc"])

        def cmp_scores(g):
            for h in range(4):
                mm(pC[:, h * 128:(h + 1) * 128], QT[0:67, (4 * g + h) * 128:(4 * g + h + 1) * 128], KcT[0:67, g, 0:128],
                   True, True, ["QT", "KcT"], ["pC"])

        def cmp_chain(g):
            off = 124 - 4 * i
            tt("dve", tmpa.rearrange("p (h c) -> p h c", h=4), pC.rearrange("p (h c) -> p h c", h=4),
               mcmp[:, off:off + 128].unsqueeze(1).to_broadcast([128, 4, 128]), ALU.add, ["pC", "mcmp"], ["tmpa"])
            act(pcm, tmpa, AF.Exp, ["tmpa"], ["pcm"])
            red("dve", sm[:, 16:20], pcm.rearrange("p (h c) -> p h c", h=4), AX.X, ALU.add, ["pcm"], ["sm"])
            ts("dve", sm[:, 16:20], sm[:, 16:20], 1e-30, None, ALU.max, None, ["sm"], ["sm"])
            kb.op("dve", lambda e: e.reciprocal(out=sm[:, 16:20], in_=sm[:, 16:20]), ["sm"], ["sm"])
            tt("dve", pcm.rearrange("p (h c) -> p h c", h=4), pcm.rearrange("p (h c) -> p h c", h=4),
               sm[:, 16:20].unsqueeze(2).to_broadcast([128, 4, 128]), ALU.mult, ["pcm", "sm"], ["pcm"])
            cp("pool", pnb, pcm, ["pcm"], ["PT0"])
            for h in range(4):
                tr(pT[:, (4 + h) * 128:(5 + h) * 128], pnb[:, h * 128:(h + 1) * 128], identb, ["PT0", "identb"], ["pT"])
            cp("act", pnT, pT[:, 512:1024], ["pT"], ["PT1"])
            red("dve", score, pcm.rearrange("p (h b t) -> p b h t", h=4, t=2), AX.XY, ALU.add, ["pcm"], ["score"])
            tt("dve", score, score, bon[s], ALU.add, ["score", "bon%d" % s], ["score"])
            kb.op("dve", lambda e: e.max(out=m8[:, 0:8], in_=score), ["score"], ["m8"])
            kb.op("dve", lambda e: e.match_replace(out=score2, in_to_replace=m8[:, 0:8], in_values=score, imm_value=-1e30),
                  ["score", "m8"], ["score2"])
            kb.op("dve", lambda e: e.max(out=m8[:, 8:16], in_=score2), ["score2"], ["m8"])
            ts("dve", score2, score, m8[:, 15:16], None, ALU.is_ge, None, ["score", "m8"], ["score2"])
            ts("dve", selb, score2, -NEGB, NEGB, ALU.mult, ALU.add, ["score2"], ["selb"])
            tr(pT[0:64, 0:128], selb, identb, ["selb", "identb"], ["pT"])
            cp("act", selTg[g].rearrange("b (h q) -> b h q", h=4), pT[0:64, 0:128].unsqueeze(1).to_broadcast([64, 4, 128]),
               ["pT"], ["selT%d" % g])

        def o_cmp(g):
            for h in range(4):
                mm(pC[:, h * 64:(h + 1) * 64], pnT[:, h * 128:(h + 1) * 128], Vc[:, g, 0, :], True, True, ["PT1", "Vc"], ["pC"])
            oat3 = oat[:, g * 256:(g + 1) * 256].rearrange("p (h d) -> p h d", d=64)
            g3 = gates[:, g * 12:(g + 1) * 12].rearrange("p (h b) -> p h b", b=3)
            tt("dve", oat3, pC[:, 0:256].rearrange("p (h d) -> p h d", d=64),
               g3[:, :, 0:1].to_broadcast([128, 4, 64]), ALU.mult, ["pC", "gates"], ["pj_q"])

        def branch(g, br, pacc, pname):
            Qg = QT[0:67, g * 512:(g + 1) * 512]
            kts = list(range(0, i + 1)) if br == 0 else list(range(max(0, i - 4), i + 1))
            pend = None
            for idx, kt in enumerate(kts):
                b2 = idx % 2
                bank = pS[b2]
                bname = "pS%d" % b2
                if br == 0:
                    klhs = KsT[0:67, g, kt * 128:(kt + 1) * 128]
                    kres = ("KsT", kt)
                    vv = Vs[:, kt, g, :]
                    vres = ("Vs", kt)
                else:
                    wsl = kt % 6
                    klhs = KwT[0:67, g, wsl * 128:(wsl + 1) * 128]
                    kres = ("KwT", wsl)
                    vv = Vw[:, wsl, g, :]
                    vres = ("Vw", wsl)
                diag = (kt == i)
                wfirst = (br == 1 and kt == i - 4)
                nextra = (1 if br == 0 else 0) + (1 if diag else 0) + (1 if wfirst else 0)
                mm(bank, klhs, Qg, True, nextra == 0, [kres, "KsT", "KwT", "QT"], [bname])
                if br == 0:
                    nextra -= 1
                    mm(bank, stairb[:, kt * 128:(kt + 1) * 128], selTg[g], False, nextra == 0, ["stairb", "selT%d" % g], [bname])
                if diag:
                    nextra -= 1
                    mm(bank, identb, cmaskb, False, nextra == 0, ["identb", "cmaskb"], [bname])
                if wfirst:
                    nextra -= 1
                    mm(bank, identb, wmaskb, False, nextra == 0, ["identb", "wmaskb"], [bname])
                act(PT2[b2], bank, AF.Exp, [bname], ["PTx%d" % b2])
                if pend is not None:
                    pend()

                def pv(b2=b2, vv=vv, vres=vres, first=(idx == 0), lastk=(idx == len(kts) - 1)):
                    for h in range(4):
                        mm(pacc[:, h, :], PT2[b2][:, h * 128:(h + 1) * 128], vv, first and h == 0, lastk,
                           ["PTx%d" % b2, vres, "Vs", "Vw"], [pname], skip=True)
                pend = pv
            pend()

        def evac(g, br, pacc, pname):
            g3 = gates[:, g * 12:(g + 1) * 12].rearrange("p (h b) -> p h b", b=3)
            ts("dve", ocf[:, 0:4], pacc[:, :, 64], 1e-30, None, ALU.max, None, [pname], ["ocf"])
            kb.op("dve", lambda e: e.reciprocal(out=ocf[:, 0:4], in_=ocf[:, 0:4]), ["ocf"], ["ocf"])
            tt("dve", ocf[:, 4:8], ocf[:, 0:4], g3[:, :, 1 + br], ALU.mult, ["ocf", "gates"], ["ocf"])
            tt("dve", tmpb[:, 0:256].rearrange("p (h d) -> p h d", d=64), pacc[:, :, 0:64],
               ocf[:, 4:8].unsqueeze(2).to_broadcast([128, 4, 64]), ALU.mult, [pname, "ocf"], ["tmpb"])
            tt("pool", oat[:, g * 256:(g + 1) * 256], oat[:, g * 256:(g + 1) * 256], tmpb[:, 0:256], ALU.add,
               ["pj_q", "tmpb"], ["pj_q"])

        for g in range(2):
            cmp_scores(g)
            branch(g, 1, pW, "pW")
            cmp_chain(g)
            o_cmp(g)
            evac(g, 1, pW, "pW")
        branch(0, 0, pO, "pO")
        branch(1, 0, pW, "pW")
        evac(0, 0, pO, "pO")
        evac(1, 0, pW, "pW")
        act(tmpa, proj[:, C_ZA:C_ZA + 512], AF.Silu, ["pj_za"], ["tmpa"])
        tt("dve", omix[:, 0:512], oat, tmpa, ALU.mult, ["pj_q", "tmpa"], ["omix"])
        tt("pool", u, proj[:, C_CG:C_CG + 512], proj[:, C_HV:C_HV + 512], ALU.mult, ["pj_cg", "pj_hv"], ["pj_cg"])
        tt("pool", v0, u, cw[:, 0, :], ALU.mult, ["pj_cg", "gsm"], ["tmpa"])
        tt("pool", v1, u, cw[:, 1, :], ALU.mult, ["pj_cg", "gsm"], ["pcm"])
        tt("pool", tmpb, u, cw[:, 2, :], ALU.mult, ["pj_cg", "gsm"], ["tmpb"])
        mm(pA[0], shm[:, 0, :], v0, True, False, ["shm", "tmpa"], ["pA0"])
        mm(pA[0], shm[:, 1, :], v1, False, True, ["shm", "pcm"], ["pA0"])
        tt("dve", tmpb, pA[0], tmpb, ALU.add, ["pA0", "tmpb"], ["tmpb"])
        if i > 0:
            tt("dve", tmpb, tmpb, carry, ALU.add, ["tmpb", "carry"], ["tmpb"])
        if i < NT - 1:
            mm(pA[1], shm[:, 2, :], v0, True, False, ["shm", "tmpa"], ["pA1"])
            mm(pA[1], shm[:, 3, :], v1, False, True, ["shm", "pcm"], ["pA1"])
            cp("act", carry, pA[1], ["pA1"], ["carry"])
        tt("dve", tmpb, tmpb, proj[:, C_BG:C_BG + 512], ALU.mult, ["tmpb", "pj_bg"], ["tmpb"])
        act(tmpa, proj[:, C_ZB:C_ZB + 512], AF.Silu, ["pj_zb"], ["tmpa"])
        tt("dve", omix[:, 512:1024], tmpb, tmpa, ALU.mult, ["tmpb", "tmpa"], ["omix"])
        if i == NT - 1:
            kb.dma(conv_p[l], u[126:128, :], ["pj_cg"], [("conv_p", l)])
        for c in range(8):
            tr(pT[:, c * 128:(c + 1) * 128], omix[:, c * 128:(c + 1) * 128], identb, ["omix", "identb"], ["pT"])
        cp("act", xT, pT, ["pT"], ["xT"])
        for cg in range(2):
            bn = "pA%d" % cg
            for kc in range(8):
                mm(pA[cg], xT[:, kc * 128:(kc + 1) * 128], W_out[:, kc, cg * 512:(cg + 1) * 512], kc == 0, kc == 7,
                   ["xT", "W_out"], [bn])
            tt("dve", X[:, cg * 512:(cg + 1) * 512], pA[cg], X[:, cg * 512:(cg + 1) * 512], ALU.add, [bn, xn], [xn])
        cp("dve", xb, X, [xn], ["omix"])
        for c in range(8):
            tr(pT[:, c * 128:(c + 1) * 128], xb[:, c * 128:(c + 1) * 128], identb, ["omix", "identb"], ["pT"])
        cp("act", xT, pT, ["pT"], ["xT"])
        kb.op("dve", lambda e: e.memset(st[:, 2:4], 0.0), ["st"], ["st"])
        act(junkb, X, AF.Square, [xn, "st"], ["pcm", "st"], accum_out=st[:, 2:3])
        rstd(st[:, 2:3], st[:, 3:4], 1024.0, ["st"])
        cp("pool", pb, pt[s], ["pt0"], ["pb"])
        for c in range(2):
            tr(pT[:, c * 128:(c + 1) * 128], pb[:, c * 128:(c + 1) * 128], identb, ["pb", "identb"], ["pT"])
        cp("act", pTT, pT[:, 0:256], ["pT"], ["pTT"])
        for cg in range(2):
            for kc in range(8):
                mm(pA[0], xT[:, kc * 128:(kc + 1) * 128], W_pg[:, kc, cg * 512:(cg + 1) * 512], kc == 0, kc == 7,
                   ["xT", "W_pg"], ["pA0"])
            act(sg, pA[0], AF.Sigmoid, ["pA0", "st"], ["tmpa"], scale=st[:, 3:4])
            for kc in range(2):
                mm(pA[1], pTT[:, kc * 128:(kc + 1) * 128], W_ple[:, kc, cg * 512:(cg + 1) * 512], kc == 0, kc == 1,
                   ["pTT", "W_ple"], ["pA1"])
            tt("dve", sg, pA[1], sg, ALU.mult, ["pA1", "tmpa"], ["tmpa"])
            tt("pool", X[:, cg * 512:(cg + 1) * 512], X[:, cg * 512:(cg + 1) * 512], sg, ALU.add, [xn, "tmpa"], [xn])
        dst = y_p if last_layer else xscr
        kb.dma(dst[i * 128:(i + 1) * 128, :], X, [xn], [("xd", i)])

    pg = [tmpb[:, 0:256], tmpb[:, 256:512], pcm[:, 0:256], pcm[:, 256:512]]
    pgn = ["pgq0", "pgq1", "pgq2", "pgq3"]
    pool2 = pool.rearrange("r (two c) -> (r two) c", two=2)
    idxB = sb("idxB", [128, 64], I32)

    def fence():
        allr = ["tmpb", "pcm", "knb", "pT", "pS0", "pS1", "PT0", "PT1"] + pgn
        allr += [("pSq", g_, r_) for g_ in range(2) for r_ in range(4)] + [("PTq", g_, r_) for g_ in range(2) for r_ in range(4)]
        allr += [("knbq", 0), ("knbq", 1), ("pTq", 0), ("pTq", 1)]
        kb.op("dve", lambda e: e.memset(cst[0:1, 4:5], 0.0), allr, allr)
    QTv = QT.rearrange("p (h q) -> p h q", h=8)
    pTv = pT.rearrange("p (h q) -> p h q", h=8)

    def gather_page(l, j, k, second):
        buf = pg[k]
        src = pool2
        off = (idxB if second else idxi)[:, j:j + 1]
        kb.op("pool", lambda e: e.indirect_dma_start(out=buf, out_offset=None, in_=src,
                                                     in_offset=bass.IndirectOffsetOnAxis(ap=off, axis=0)),
              ["idxi"], [pgn[k]], dma=True)

    def sample_tile(l, bl):
        P = 8
        X = xt[0]
        xn = "xt0"
        last_layer = (l == n_layers - 1)
        src = xs_in if l == 0 else xs_scr
        kb.dma(X[0:P, :], src[bl], [("xsd", bl)], [xn])
        kb.dma(pt[0][0:P, :], pps[l, bl], [], ["pt0"])
        kb.dma(idxi, ptab[bl].partition_broadcast(128), [], ["idxi"])
        cp("dve", idxf, idxi, ["idxi"], ["idxf"])
        ts("dve", idxf, idxf, 128.0, iota[:, 0:1], ALU.mult, ALU.add, ["idxf", "iota"], ["idxf"])
        ts("dve", idxf, idxf, float(l * n_pool * 128), 2.0, ALU.add, ALU.mult, ["idxf"], ["idxf"])
        cp("dve", idxi, idxf, ["idxf"], ["idxi"])
        ts("dve", idxf, idxf, 1.0, None, ALU.add, None, ["idxf"], ["idxf"])
        cp("dve", idxB, idxf, ["idxf"], ["idxi"])
        kb.op("dve", lambda e: e.memset(st, 0.0), ["st"], ["st"])
        act(omix[0:P, :], X[0:P, :], AF.Square, [xn, "st"], ["omix", "st"], accum_out=st[0:P, 0:1])
        rstd(st[0:P, 0:1], st[0:P, 1:2], 1024.0, ["st"])
        cp("dve", xb[0:P, :], X[0:P, :], [xn], ["omix"])
        for c in range(8):
            tr(pT[:, c * 128:c * 128 + P], xb[0:P, c * 128:(c + 1) * 128], identb[0:P, 0:P], ["omix", "identb"], ["pT"])
        cp("act", xT, pT, ["pT"], ["xT"])
        for gi, (c0, c1, rn) in enumerate(COL_GROUPS):
            w = c1 - c0
            bank = pA[gi % 2]
            bn = "pA%d" % (gi % 2)
            for kc in range(8):
                mm(bank[0:P, 0:w], xT[:, kc * 128:kc * 128 + P], W_in[:, kc, c0:c1], kc == 0, kc == 7, ["xT", "W_in"], [bn])
            act(proj[0:P, c0:c1], bank[0:P, 0:w], AF.Copy, [bn, "st"], [rn], scale=st[0:P, 1:2])
        q3 = proj[0:P, C_Q:C_Q + 512].rearrange("p (h d) -> p h d", d=64)
        ta3 = tmpa[0:P, :].rearrange("p (h d) -> p h d", d=64)
        tt("dve", tmpa[0:P, :], proj[0:P, C_Q:C_Q + 512], proj[0:P, C_Q:C_Q + 512], ALU.mult, ["pj_q"], ["tmpa"])
        red("dve", sm[0:P, 0:8], ta3, AX.X, ALU.add, ["tmpa"], ["sm"])
        rstd(sm[0:P, 0:8], sm[0:P, 0:8], 64.0, ["sm"])
        tt("dve", ta3, q3, sm[0:P, 0:8].unsqueeze(2).to_broadcast([P, 8, 64]), ALU.mult, ["pj_q", "sm"], ["tmpa"])
        tt("pool", qnb[0:P, :].rearrange("p (h d) -> p h d", d=64), ta3, gq8[0:P, :].unsqueeze(1).to_broadcast([P, 8, 64]),
           ALU.mult, ["tmpa", "gsm"], ["qnb"])
        for h in range(8):
            tr(pT[0:64, h * 128:h * 128 + P], qnb[0:P, h * 64:(h + 1) * 64], identb[0:P, 0:P], ["qnb", "identb"], ["pT"])
        for vq, qb_ in ((0, 64), (1, 32)):
            cp("act", QTv[0:64, :, vq * 8:vq * 8 + 8], pTv[0:64, :, 0:8], ["pT"], ["QT"])
            ts("dve", QTv[64:67, :, vq * 8:vq * 8 + 8], qaugf[64:67, :].unsqueeze(2).to_broadcast([3, 8, 8]),
               qmulf[64:67, qb_:qb_ + 1], None, ALU.mult, None, ["qaugf", "qmulf"], ["QT"])
        for which, (c0, gi_) in enumerate(((C_KS, 1), (C_KW, 2))):
            rn = "pj_kv" if which == 0 else "pj_w"
            k3 = proj[0:P, c0:c0 + 128].rearrange("p (h d) -> p h d", d=64)
            tt("dve", tmpb[0:P, 0:128], proj[0:P, c0:c0 + 128], proj[0:P, c0:c0 + 128], ALU.mult, [rn], ["tmpb"])
            red("dve", sm[0:P, 8 + 2 * which:10 + 2 * which], tmpb[0:P, 0:128].rearrange("p (h d) -> p h d", d=64),
                AX.X, ALU.add, ["tmpb"], ["sm"])
            rstd(sm[0:P, 8 + 2 * which:10 + 2 * which], sm[0:P, 8 + 2 * which:10 + 2 * which], 64.0, ["sm"])
            tt("dve", k3, k3, sm[0:P, 8 + 2 * which:10 + 2 * which].unsqueeze(2).to_broadcast([P, 2, 64]), ALU.mult, [rn, "sm"], [rn])
            tt("dve", k3, k3, gk[0:P, gi_, :].unsqueeze(1).to_broadcast([P, 2, 64]), ALU.mult, [rn, "gsm"], [rn])
            cp("pool", knb[0:P, which * 128:(which + 1) * 128], proj[0:P, c0:c0 + 128], [rn], ["knb"])
        for j in range(4):
            tr(pT[0:64, j * 128:j * 128 + P], knb[0:P, j * 64:(j + 1) * 64], identb[0:P, 0:P], ["knb", "identb"], ["pT"])
        for which in range(2):
            for g in range(2):
                j = which * 2 + g
                cp("act", KnT[0:64, which, g, :], pT[0:64, j * 128:j * 128 + 8], ["pT"], ["KnT"])
        cp("pool", Vn[:, 0, :, 0:64], proj[0:P, C_VS:C_VS + 128].rearrange("p (h d) -> p h d", d=64), ["pj_kv"], ["Vn"])
        cp("pool", Vn[:, 1, :, 0:64], proj[0:P, C_VW:C_VW + 128].rearrange("p (h d) -> p h d", d=64), ["pj_w"], ["Vn"])
        kb.dma(kv_s[l, bl], proj[0:P, C_KC:C_KC + 512], ["pj_kv"], [("kv_s", l, bl)])
        kb.dma(win_s[l, bl, 504:512, :], proj[0:P, C_KW:C_KW + 256], ["pj_w"], [("win_s", l, bl)])
        kb.dma(win_s[l, bl, 0:504, :], cw_in[l, bl, 8:512, :], [], [("win_s2", l, bl)])
        act(gates[0:P, :], proj[0:P, C_GL:C_GL + 24], AF.Sigmoid, ["pj_w"], ["gates"])

        fence()
        for g in range(2):
            for half in range(2):
                for jj in range(32):
                    j = half * 32 + jj
                    k = jj % 4
                    gather_page(l, j, k, False)
                    pv = pg[k].rearrange("p (a h d) -> p a h d", a=2, h=2)[:, :, g, :]
                    tt("dve", kcp[:, 0:128].rearrange("p (a d) -> p a d", a=2), pv, posrep, ALU.add, [pgn[k], "gsm"], ["kcp"])
                    for a in range(2):
                        tr(pT[0:64, a * 128:(a + 1) * 128], kcp[:, a * 64:(a + 1) * 64], identb, ["kcp", "identb"], ["pT"])
                    cp("act", KsT[0:64, :, jj * 128:(jj + 1) * 128], pT[0:64, 0:256].rearrange("d (a t) -> d a t", a=2),
                       ["pT"], [("KsT", jj)])
                allk = [("KsT", jj) for jj in range(32)]
                for kv in range(2):
                    for ll in range(32):
                        lhs = KsT[0:64, kv, :].rearrange("d (c l) -> d c l", l=32)[:, :, ll]
                        mm(pC[:, kv * 64:(kv + 1) * 64], lhs, W_phi[:, kv, ll, :], ll == 0, ll == 31, allk + ["W_phi"], ["pC"])
                kb.op("dve", lambda e: e.memset(st[:, 4:6], 0.0), ["st"], ["st"])
                act(tmpa[:, 0:64], pC[:, 0:64], AF.Square, ["pC", "st"], ["tmpa", "st"], accum_out=st[:, 4:5])
                rstd(st[:, 4:5], st[:, 5:6], 64.0, ["st"])
                ts("dve", tmpa[:, 0:64], pC[:, 0:64], st[:, 5:6], None, ALU.mult, None, ["pC", "st"], ["tmpa"])
                tt("dve", kcp[:, 0:64], tmpa[:, 0:64], gk[:, 0, :], ALU.mult, ["tmpa", "gsm"], ["kcp"])
                tr(pT[0:64, 0:128], kcp[:, 0:64], identb, ["kcp", "identb"], ["pT"])
                cp("act", KcT[0:64, g, half * 128:(half + 1) * 128], pT[0:64, 0:128], ["pT"], ["KcT"])
                cp("act", Vc[:, g, half, :], pC[:, 64:128], ["pC"], ["Vc"])

        fence()
        for g in range(2):
            pbuf = [tmpa, pcm]
            pbn = ["tmpa", "pcm"]
            for hp in range(2):
                bank = pC if hp == 0 else pS[0]
                bn = "pC" if hp == 0 else "pS0"
                for hh in range(2):
                    h = hp * 2 + hh
                    for half in range(2):
                        mm(bank[0:P, (hh * 2 + half) * 128:(hh * 2 + half + 1) * 128],
                           QTv[0:67, 4 * g + h, half * 8:half * 8 + 8], KcT[0:67, g, half * 128:(half + 1) * 128],
                           True, True, ["QT", "KcT"], [bn])
                act(pbuf[hp][0:P, :], bank[0:P, :], AF.Exp, [bn], [pbn[hp]])
                red("dve", sm[0:P, 16 + 2 * hp:18 + 2 * hp], pbuf[hp][0:P, :].rearrange("p (h c) -> p h c", h=2), AX.X, ALU.add,
                    [pbn[hp]], ["sm"])
            ts("dve", sm[0:P, 16:20], sm[0:P, 16:20], 1e-30, None, ALU.max, None, ["sm"], ["sm"])
            kb.op("dve", lambda e: e.reciprocal(out=sm[0:P, 16:20], in_=sm[0:P, 16:20]), ["sm"], ["sm"])
            for hp in range(2):
                pv3 = pbuf[hp][0:P, :].rearrange("p (h c) -> p h c", h=2)
                tt("dve", pv3, pv3, sm[0:P, 16 + 2 * hp:18 + 2 * hp].unsqueeze(2).to_broadcast([P, 2, 256]), ALU.mult,
                   [pbn[hp], "sm"], [pbn[hp]])
                dst = scS if hp == 0 else scS2
                red("dve", dst[:, 0:128], pbuf[hp][0:P, :].rearrange("p (h b t) -> p b h t", h=2, t=2), AX.XY, ALU.add,
                    [pbn[hp]], ["scS" if hp == 0 else "scS2"])
                cp("pool", omix[0:P, hp * 512:(hp + 1) * 512], pbuf[hp][0:P, :], [pbn[hp]], ["omix"])
            tt("dve", scS[:, 0:128], scS[:, 0:128], scS2[:, 0:128], ALU.add, ["scS", "scS2"], ["scS"])
            tt("dve", scS[:, 0:128], scS[:, 0:128], bonsS[:, 0:128], ALU.add, ["scS", "bonsS"], ["scS"])
            cp("dve", scS[:, 128:136], bonsS[:, 128:136], ["bonsS", "scS"], ["scS"])
            kb.op("dve", lambda e: e.max(out=m8[0:P, 0:8], in_=scS), ["scS"], ["m8"])
            kb.op("dve", lambda e: e.match_replace(out=scS2, in_to_replace=m8[0:P, 0:8], in_values=scS, imm_value=-1e30),
                  ["scS", "m8"], ["scS2"])
            kb.op("dve", lambda e: e.max(out=m8[0:P, 8:16], in_=scS2), ["scS2"], ["m8"])
            ts("dve", scS2, scS, m8[0:P, 15:16], None, ALU.is_ge, None, ["scS", "m8"], ["scS2"])
            ts("dve", selbS, scS2[:, 0:128], -NEGB, NEGB, ALU.mult, ALU.add, ["scS2"], ["selbS"])
            for half in range(2):
                tr(pT[0:64, half * 128:half * 128 + P], selbS[:, half * 64:(half + 1) * 64], identb[0:P, 0:P],
                   ["selbS", "identb"], ["pT"])
            for half in range(2):
                cp("act", selTS[:, g, half, :].rearrange("b (h q) -> b h q", h=4),
                   pT[0:64, half * 128:half * 128 + 8].unsqueeze(1).to_broadcast([64, 4, 8]), ["pT"], ["selTS"])
            for hc in range(8):
                tr(pT[:, hc * 128:hc * 128 + P], omix[0:P, hc * 128:(hc + 1) * 128], identb[0:P, 0:P], ["omix", "identb"], ["pT"])
            cp("act", PT[1][:, 0:64].rearrange("c (a q) -> c a q", a=8), pTv[:, :, 0:8], ["pT"], ["PT1"])
            for h in range(4):
                for half in range(2):
                    hc = h * 2 + half
                    mm(pC[0:P, h * 64:(h + 1) * 64], PT[1][:, hc * 8:(hc + 1) * 8], Vc[:, g, half, :],
                       h == 0 and half == 0, half == 1, ["PT1", "Vc"], ["pC"], skip=True)
            oat3 = oat[0:P, g * 256:(g + 1) * 256].rearrange("p (h d) -> p h d", d=64)
            g3 = gates[0:P, g * 12:(g + 1) * 12].rearrange("p (h b) -> p h b", b=3)
            tt("dve", oat3, pC[0:P, 0:256].rearrange("p (h d) -> p h d", d=64), g3[:, :, 0:1].to_broadcast([P, 4, 64]),
               ALU.mult, ["pC", "gates"], ["pj_q"])

        paccs = [pO, pW]
        pnames = ["pO", "pW"]

        rot = [0]

        def key_tile_s(g, klhs, kres, qoff, extra, nk, r):
            bank = pS[g][0:nk, r * 32:(r + 1) * 32]
            bn = ("pSq", g, r)
            Qv = QTv[0:67, 4 * g:4 * g + 4, qoff:qoff + 8]
            mm(bank, klhs, Qv, True, len(extra) == 0, [kres, "KsT", "KwT", "KnT", "QT"], [bn])
            for xi, (el, er, eres) in enumerate(extra):
                mm(bank, el, er, False, xi == len(extra) - 1, list(eres), [bn])
            act(PT[g][0:nk, r * 32:(r + 1) * 32], bank, AF.Exp, [bn], [("PTq", g, r)])

        def key_tile_v(g, vv, vres, first, nk, r):
            for h in range(4):
                mm(paccs[g][0:P, h, :], PT[g][0:nk, r * 32 + h * 8:r * 32 + (h + 1) * 8], vv, first and h == 0, False,
                   [("PTq", g, r), vres, "Vs", "Vw", "Vn"], [pnames[g]], skip=True)

        def key_tile(g, klhs, kres, qoff, extra, vv, vres, first, nk):
            r = rot[0] % 4
            rot[0] += 1
            key_tile_s(g, klhs, kres, qoff, extra, nk, r)
            key_tile_v(g, vv, vres, first, nk, r)

        def combine(br):
            for g in range(2):
                pacc = paccs[g]
                g3 = gates[0:P, g * 12:(g + 1) * 12].rearrange("p (h b) -> p h b", b=3)
                ts("dve", ocf[0:P, 0:4], pacc[0:P, :, 64], 1e-30, None, ALU.max, None, [pnames[g]], ["ocf"])
                kb.op("dve", lambda e: e.reciprocal(out=ocf[0:P, 0:4], in_=ocf[0:P, 0:4]), ["ocf"], ["ocf"])
                tt("dve", ocf[0:P, 4:8], ocf[0:P, 0:4], g3[:, :, 1 + br], ALU.mult, ["ocf", "gates"], ["ocf"])
                tt("dve", tmpa[0:P, 0:256].rearrange("p (h d) -> p h d", d=64), pacc[0:P, :, 0:64],
                   ocf[0:P, 4:8].unsqueeze(2).to_broadcast([P, 4, 64]), ALU.mult, [pnames[g], "ocf"], ["tmpa"])
                tt("pool", oat[0:P, g * 256:(g + 1) * 256], oat[0:P, g * 256:(g + 1) * 256], tmpa[0:P, 0:256], ALU.add,
                   ["pj_q", "tmpa"], ["pj_q"])

        fence()
        for j in range(64):
            k = j % 4
            half = j // 32
            jj = j % 32
            slot = j % 6
            gather_page(l, j, k, True)
            kq = j % 2
            cp("dve", knb[:, kq * 128:(kq + 1) * 128], pg[k][:, 0:128], [pgn[k]], [("knbq", kq)])
            for g in range(2):
                tr(pT[0:64, kq * 256 + g * 128:kq * 256 + (g + 1) * 128], knb[:, kq * 128 + g * 64:kq * 128 + (g + 1) * 64], identb,
                   [("knbq", kq), "identb"], [("pTq", kq)])
            for g in range(2):
                cp("act", KwT[0:64, g, slot * 128:(slot + 1) * 128], pT[0:64, kq * 256 + g * 128:kq * 256 + (g + 1) * 128],
                   [("pTq", kq)], [("KwT", slot)])
                cp("dve", KwT[64:67, g, slot * 128:(slot + 1) * 128], KsT[64:67, g, jj * 128:(jj + 1) * 128], ["KsT"],
                   [("KwT", slot)])
            cp("pool", Vw[:, slot, :, 0:64], pg[k][:, 128:256].rearrange("p (h d) -> p h d", d=64), [pgn[k]], [("Vw", slot)])
            for g in range(2):
                key_tile_s(g, KwT[0:67, g, slot * 128:(slot + 1) * 128], ("KwT", slot), half * 8,
                           [(stairb[:, jj * 128:(jj + 1) * 128], selTS[:, g, half, :], ("stairb", "selTS"))], 128, j % 4)
            for g in range(2):
                key_tile_v(g, Vw[:, slot, g, :], ("Vw", slot), j == 0, 128, j % 4)
        for g in range(2):
            key_tile(g, KnT[0:67, 0, g, :], "KnT", 0, [(identb[0:8, 0:8], cmask8, ("identb", "cmask8"))],
                     Vn[:, 0, g, :], "Vn", False, 8)
        combine(0)
        wm8 = wmaskb.rearrange("k (h q) -> k h q", h=4)[:, :, 0:8]
        for r in range(4):
            k = r % 4
            slot = r
            kb.dma(pg[k], cw_in[l, bl, r * 128:(r + 1) * 128, :], [], [pgn[k]])
            kq = r % 2
            cp("dve", knb[:, kq * 128:(kq + 1) * 128], pg[k][:, 0:128], [pgn[k]], [("knbq", kq)])
            for g in range(2):
                tr(pT[0:64, kq * 256 + g * 128:kq * 256 + (g + 1) * 128], knb[:, kq * 128 + g * 64:kq * 128 + (g + 1) * 64], identb,
                   [("knbq", kq), "identb"], [("pTq", kq)])
            for g in range(2):
                cp("act", KwT[0:64, g, slot * 128:(slot + 1) * 128], pT[0:64, kq * 256 + g * 128:kq * 256 + (g + 1) * 128],
                   [("pTq", kq)], [("KwT", slot)])
                cp("dve", KwT[64:67, g, slot * 128:(slot + 1) * 128], KsT[64:67, g, (28 + r) * 128:(29 + r) * 128], ["KsT"],
                   [("KwT", slot)])
            cp("pool", Vw[:, slot, :, 0:64], pg[k][:, 128:256].rearrange("p (h d) -> p h d", d=64), [pgn[k]], [("Vw", slot)])
            for g in range(2):
                extra = [(identb, wm8, ("identb", "wmaskb"))] if r == 0 else []
                key_tile(g, KwT[0:67, g, slot * 128:(slot + 1) * 128], ("KwT", slot), 8, extra,
                         Vw[:, slot, g, :], ("Vw", slot), r == 0, 128)
        for g in range(2):
            key_tile(g, KnT[0:67, 1, g, :], "KnT", 0, [(identb[0:8, 0:8], cmask8, ("identb", "cmask8"))],
                     Vn[:, 1, g, :], "Vn", False, 8)
        combine(1)

        fence()
        act(tmpa[0:P, :], proj[0:P, C_ZA:C_ZA + 512], AF.Silu, ["pj_za"], ["tmpa"])
        tt("dve", omix[0:P, 0:512], oat[0:P, :], tmpa[0:P, :], ALU.mult, ["pj_q", "tmpa"], ["omix"])
        kb.op("pool", lambda e: e.memset(carry, 0.0), ["carry"], ["carry"])
        kb.dma(carry[126:128, :], sc_in[l, bl], ["carry"], ["carry"])
        tt("pool", u[0:P, :], proj[0:P, C_CG:C_CG + 512], proj[0:P, C_HV:C_HV + 512], ALU.mult, ["pj_cg", "pj_hv"], ["pj_cg"])
        tt("pool", tmpa[0:P, :], u[0:P, :], cw[0:P, 0, :], ALU.mult, ["pj_cg", "gsm"], ["tmpa"])
        tt("pool", pcm[0:P, :], u[0:P, :], cw[0:P, 1, :], ALU.mult, ["pj_cg", "gsm"], ["pcm"])
        tt("pool", tmpb[0:P, :], u[0:P, :], cw[0:P, 2, :], ALU.mult, ["pj_cg", "gsm"], ["tmpb"])
        tt("pool", tmpa[64:128, :], carry[64:128, :], cw[64:128, 0, :], ALU.mult, ["carry", "gsm"], ["tmpa"])
        tt("pool", pcm[64:128, :], carry[64:128, :], cw[64:128, 1, :], ALU.mult, ["carry", "gsm"], ["pcm"])
        mm(pA[0][0:P, :], shm[0:P, 0, 0:P], tmpa[0:P, :], True, False, ["shm", "tmpa"], ["pA0"])
        mm(pA[0][0:P, :], shm[0:P, 1, 0:P], pcm[0:P, :], False, False, ["shm", "pcm"], ["pA0"])
        mm(pA[0][0:P, :], shm[64:128, 2, 0:P], tmpa[64:128, :], False, False, ["shm", "tmpa"], ["pA0"])
        mm(pA[0][0:P, :], shm[64:128, 3, 0:P], pcm[64:128, :], False, True, ["shm", "pcm"], ["pA0"])
        tt("dve", tmpb[0:P, :], pA[0][0:P, :], tmpb[0:P, :], ALU.add, ["pA0", "tmpb"], ["tmpb"])
        tt("dve", tmpb[0:P, :], tmpb[0:P, :], proj[0:P, C_BG:C_BG + 512], ALU.mult, ["tmpb", "pj_bg"], ["tmpb"])
        act(tmpa[0:P, :], proj[0:P, C_ZB:C_ZB + 512], AF.Silu, ["pj_zb"], ["tmpa"])
        tt("dve", omix[0:P, 512:1024], tmpb[0:P, :], tmpa[0:P, :], ALU.mult, ["tmpb", "tmpa"], ["omix"])
        kb.dma(conv_s[l, bl], proj[6:8, C_CG:C_CG + 512], ["pj_cg"], [("conv_s", l, bl)])
        for c in range(8):
            tr(pT[:, c * 128:c * 128 + P], omix[0:P, c * 128:(c + 1) * 128], identb[0:P, 0:P], ["omix", "identb"], ["pT"])
        cp("act", xT, pT, ["pT"], ["xT"])
        for cg in range(2):
            bn = "pA%d" % cg
            for kc in range(8):
                mm(pA[cg][0:P, :], xT[:, kc * 128:kc * 128 + P], W_out[:, kc, cg * 512:(cg + 1) * 512], kc == 0, kc == 7,
                   ["xT", "W_out"], [bn])
            tt("dve", X[0:P, cg * 512:(cg + 1) * 512], pA[cg][0:P, :], X[0:P, cg * 512:(cg + 1) * 512], ALU.add, [bn, xn], [xn])
        kb.op("dve", lambda e: e.memset(st[:, 2:4], 0.0), ["st"], ["st"])
        act(omix[0:P, :], X[0:P, :], AF.Square, [xn, "st"], ["omix", "st"], accum_out=st[0:P, 2:3])
        rstd(st[0:P, 2:3], st[0:P, 3:4], 1024.0, ["st"])
        cp("dve", xb[0:P, :], X[0:P, :], [xn], ["omix"])
        for c in range(8):
            tr(pT[:, c * 128:c * 128 + P], xb[0:P, c * 128:(c + 1) * 128], identb[0:P, 0:P], ["omix", "identb"], ["pT"])
        cp("act", xT, pT, ["pT"], ["xT"])
        cp("pool", pb[0:P, :], pt[0][0:P, :], ["pt0"], ["pb"])
        for c in range(2):
            tr(pT[:, c * 128:c * 128 + P], pb[0:P, c * 128:(c + 1) * 128], identb[0:P, 0:P], ["pb", "identb"], ["pT"])
        cp("act", pTT, pT[:, 0:256], ["pT"], ["pTT"])
        for cg in range(2):
            for kc in range(8):
                mm(pA[0][0:P, :], xT[:, kc * 128:kc * 128 + P], W_pg[:, kc, cg * 512:(cg + 1) * 512], kc == 0, kc == 7,
                   ["xT", "W_pg"], ["pA0"])
            act(sg[0:P, :], pA[0][0:P, :], AF.Sigmoid, ["pA0", "st"], ["tmpa"], scale=st[0:P, 3:4])
            for kc in range(2):
                mm(pA[1][0:P, :], pTT[:, kc * 128:kc * 128 + P], W_ple[:, kc, cg * 512:(cg + 1) * 512], kc == 0, kc == 1,
                   ["pTT", "W_ple"], ["pA1"])
            tt("dve", sg[0:P, :], pA[1][0:P, :], sg[0:P, :], ALU.mult, ["pA1", "tmpa"], ["tmpa"])
            tt("pool", X[0:P, cg * 512:(cg + 1) * 512], X[0:P, cg * 512:(cg + 1) * 512], sg[0:P, :], ALU.add, [xn, "tmpa"], [xn])
        dst = y_s if last_layer else xs_scr
        kb.dma(dst[bl], X[0:P, :], [xn], [("xsd", bl)])

    for l in range(n_layers):
        load_layer(l)
        kb.op("dve", lambda e: e.memset(Vc, 0.0), ["Vc"], ["Vc"])
        kb.op("dve", lambda e: e.memset(KcT[0:64, :, :], 0.0), ["KcT"], ["KcT"])
        for i in range(NT):
            load_tile_inputs(l, i)
            prompt_tile(l, i)
        for bl in range(n_sb):
            sample_tile(l, bl)
    stats = kb.finalize()
    return nc, stats


def _prep_inputs(inp, n_layers, n_tiles, b, sbs=(0, 1, 2, 3)):
    S = n_tiles * 128
    sbs = list(sbs)
    c = _consts(n_tiles)
    f = lambda a: np.ascontiguousarray(np.asarray(a, dtype=np.float32))
    m = {
        "xp": f(inp["x_prompt"][b, :S]),
        "pp": f(inp["p_prompt"][:n_layers, b, :S]),
        "g_norm": f(inp["g_norm"][:n_layers]),
        "w_in": f(inp["w_in"][:n_layers]),
        "g_q": f(inp["g_q"][:n_layers]),
        "g_k": f(inp["g_k"][:n_layers]),
        "cmp_pos": f(inp["cmp_pos"][:n_layers]),
        "w_phi": f(inp["w_phi"][:n_layers]),
        "conv_w": f(inp["conv_w"][:n_layers]),
        "w_out": f(inp["w_out"][:n_layers]),
        "w_ple": f(inp["w_ple"][:n_layers]),
        "w_pg": f(inp["w_pg"][:n_layers]),
        "g_ple": f(inp["g_ple"][:n_layers]),
        "xs": f(inp["x_sample"][sbs]),
        "pps": f(inp["p_sample"][:n_layers][:, sbs]),
        "pool": np.asarray(inp["cache_kv"][:n_layers], dtype=np.float32).reshape(-1, 512),
        "cwin": f(inp["cache_win"][:n_layers][:, sbs]).reshape(n_layers, len(sbs), 512, 256),
        "scin": f(inp["state_conv"][:n_layers][:, sbs]),
        "ptab": np.ascontiguousarray(np.asarray(inp["page_table"], dtype=np.int32)[sbs]),
    }
    for k, v in c.items():
        m["c_" + k] = v
    return m


_PROG = {}


def kernel(**inputs):
    inp = {k: np.asarray(v) for k, v in inputs.items()}
    n_pool = inp["cache_kv"].shape[1]
    key = ("p", n_pool)
    if key not in _PROG:
        _PROG[key] = build_program(DEPTH, 32, n_pool=n_pool, n_sb=4)
    nc, _ = _PROG[key]
    in_maps = [_prep_inputs(inp, DEPTH, 32, c // 2, sbs=range(4 * c, 4 * c + 4)) for c in range(8)]
    res = run_bass_kernel_spmd(nc, in_maps, core_ids=list(range(8)))
    r = res.results
    y_p = np.stack([r[2 * b]["y_p"] for b in range(4)])
    kv_p = np.stack([r[2 * b]["kv_p"] for b in range(4)], 1).reshape(DEPTH, 4, SEQ, 4, 2, 64)
    win_p = np.stack([r[2 * b]["win_p"] for b in range(4)], 1).reshape(DEPTH, 4, 512, 2, 2, 64)
    conv_p = np.stack([r[2 * b]["conv_p"] for b in range(4)], 1)
    y_s = np.concatenate([r[c]["y_s"] for c in range(8)], 0)
    kv_s = np.concatenate([r[c]["kv_s"] for c in range(8)], 1).reshape(DEPTH, 32, 8, 4, 2, 64)
    win_s = np.concatenate([r[c]["win_s"] for c in range(8)], 1).reshape(DEPTH, 32, 512, 2, 2, 64)
    conv_s = np.concatenate([r[c]["conv_s"] for c in range(8)], 1)
    return (y_p, y_s, kv_p, win_p, conv_p, kv_s, win_s, conv_s)
```

```python
import numpy as np
import concourse.bass as bass
import concourse.mybir as mybir
from concourse.bass_utils import run_bass_kernel_spmd

F32 = mybir.dt.float32
BF16 = mybir.dt.bfloat16
I32 = mybir.dt.int32
AF = mybir.ActivationFunctionType
ALU = mybir.AluOpType
AX = mybir.AxisListType

N_DMA_SEMS = 40

D_MODEL = 1024
SEQ = 4096
DEPTH = 4
D_IN = 3864
D_PLE = 256
NEGB = -30000.0
EPS = 1e-6

C_Q, C_KC, C_VC, C_KS, C_VS, C_KW, C_VW, C_GL, C_ZA, C_BG, C_CG, C_HV, C_ZB = (
    0, 512, 640, 768, 896, 1024, 1152, 1280, 1304, 1816, 2328, 2840, 3352)
COL_GROUPS = [(0, 512, "pj_q"), (512, 1024, "pj_kv"), (1024, 1304, "pj_w"), (1304, 1816, "pj_za"),
              (1816, 2328, "pj_bg"), (2328, 2840, "pj_cg"), (2840, 3352, "pj_hv"), (3352, 3864, "pj_zb")]


class _Op:
    __slots__ = ("eng", "fn", "reads", "writes", "dma", "deps", "needs_inc", "cnt", "dsem", "dval")

    def __init__(self, eng, fn, reads, writes, dma):
        self.eng = eng
        self.fn = fn
        self.reads = reads
        self.writes = writes
        self.dma = dma
        self.deps = ()
        self.needs_inc = False
        self.cnt = 0
        self.dsem = -1
        self.dval = 0


class KB:
    def __init__(self, nc):
        self.nc = nc
        self.engs = {"pe": nc.tensor, "act": nc.scalar, "dve": nc.vector, "pool": nc.gpsimd, "sp": nc.sync}
        self.ops = []

    def op(self, eng, fn, reads=(), writes=(), dma=False):
        self.ops.append(_Op(eng, fn, tuple(reads), tuple(writes), dma))

    def dma(self, out, in_, reads, writes, q="sp", **kw):
        self.op(q, lambda e: e.dma_start(out=out, in_=in_, **kw), reads, writes, dma=True)

    def finalize(self):
        nc = self.nc
        ops = self.ops
        last_w = {}
        readers = {}
        for i, o in enumerate(ops):
            deps = set()
            for r in o.reads:
                j = last_w.get(r)
                if j is not None:
                    deps.add(j)
            for w in o.writes:
                j = last_w.get(w)
                if j is not None:
                    deps.add(j)
                for j in readers.get(w, ()):
                    deps.add(j)
            deps.discard(i)
            keep = []
            for j in deps:
                p = ops[j]
                if (not p.dma) and (not o.dma) and p.eng == o.eng:
                    if o.eng == "pe":
                        continue
                    raw = any(last_w.get(r) == j for r in o.reads) or any(last_w.get(w) == j for w in o.writes)
                    if not raw:
                        continue
                keep.append(j)
            o.deps = tuple(keep)
            for j in keep:
                if not ops[j].dma:
                    ops[j].needs_inc = True
            for r in o.reads:
                readers.setdefault(r, []).append(i)
            for w in o.writes:
                last_w[w] = i
                readers[w] = []
        cnt = {k: 0 for k in self.engs}
        ndma = 0
        dvals = [0] * N_DMA_SEMS
        for o in ops:
            if o.dma:
                k = ndma % N_DMA_SEMS
                ndma += 1
                dvals[k] += 16
                o.dsem = k
                o.dval = dvals[k]
            elif o.needs_inc:
                cnt[o.eng] += 1
                o.cnt = cnt[o.eng]
        esem = {k: nc.alloc_semaphore("es_" + k) for k in self.engs}
        dsem = [nc.alloc_semaphore("ds_%d" % k) for k in range(N_DMA_SEMS)]
        wm_e = {k: {k2: 0 for k2 in self.engs} for k in self.engs}
        wm_d = {k: [0] * N_DMA_SEMS for k in self.engs}
        n_wait = 0
        for o in ops:
            E = self.engs[o.eng]
            need_e = {}
            need_d = {}
            for j in o.deps:
                p = ops[j]
                if p.dma:
                    need_d[p.dsem] = max(need_d.get(p.dsem, 0), p.dval)
                else:
                    need_e[p.eng] = max(need_e.get(p.eng, 0), p.cnt)
            if o.dma and o.dval > 16:
                need_d[o.dsem] = max(need_d.get(o.dsem, 0), o.dval - 16)
            for f, c in need_e.items():
                if wm_e[o.eng][f] < c:
                    E.wait_ge(esem[f], c)
                    wm_e[o.eng][f] = c
                    n_wait += 1
            for k, v in need_d.items():
                if wm_d[o.eng][k] < v:
                    E.wait_ge(dsem[k], v)
                    wm_d[o.eng][k] = v
                    n_wait += 1
            ins = o.fn(E)
            if o.dma:
                ins.then_inc(dsem[o.dsem], 16)
            elif o.needs_inc:
                ins.then_inc(esem[o.eng], 1)
        sp = self.engs["sp"]
        for k in range(N_DMA_SEMS):
            if dvals[k] > 0:
                sp.wait_ge(dsem[k], dvals[k])
        self.stats = dict(n_ops=len(ops), n_wait=n_wait, n_dma=ndma, cnt=cnt)
        return self.stats


def _consts(n_tiles):
    c = {}
    c["ident"] = np.eye(128, dtype=np.float32)
    kpos = np.arange(SEQ)
    c["kaug"] = np.stack([128.0 * (kpos // 128), (kpos % 128).astype(np.float64), np.ones(SEQ)]).astype(np.float32)
    ce = np.arange(128) * 32 + 31
    c["kcaug"] = np.stack([128.0 * (ce // 128), (ce % 128).astype(np.float64), np.ones(128)]).astype(np.float32)
    m = 2.0 ** (-(np.arange(8) + 1.0))
    qa = np.zeros((3, 8, 128), np.float32)
    qa[0] = m[:, None]
    qa[1] = m[:, None]
    qa[2] = -64.0 * m[:, None]
    c["qaug"] = np.ascontiguousarray(qa[:, :, 0])
    qm = np.ones((3, 65), np.float32)
    qm[2] = 2 * np.arange(65) + 1
    c["qmul"] = qm
    k = np.arange(128)[:, None]
    t = np.arange(128)[None, :]
    cb = np.where(k <= t, 0.0, NEGB).astype(np.float32)
    wb = np.where(t < k, 0.0, NEGB).astype(np.float32)
    c["cmask"] = np.tile(cb, (1, 4))
    c["wmask"] = np.tile(wb, (1, 4))
    st = np.zeros((64, SEQ), np.float32)
    st[np.arange(SEQ) // 64, np.arange(SEQ)] = 1.0
    c["stair"] = st
    cp = np.arange(252)[None, :] - 124
    tl = np.arange(128)[:, None]
    c["mcmp"] = np.where(32 * cp + 31 <= tl, 0.0, NEGB).astype(np.float32)
    bon = np.zeros((32, 128, 64), np.float32)
    for qb in range(32):
        tt = (qb * 128 + np.arange(128))[:, None]
        bb = np.arange(64)[None, :]
        cur = tt // 64
        valid = bb * 64 <= tt
        forced = (bb == 0) | (bb == cur) | (bb == cur - 1)
        bon[qb] = np.where(valid, np.where(forced, 1e3, 0.0), -1e9)
    c["bonus"] = bon
    sh = np.zeros((4, 128, 128), np.float32)
    for tq in range(128):
        if tq - 2 >= 0:
            sh[0, tq - 2, tq] = 1.0
        if tq - 1 >= 0:
            sh[1, tq - 1, tq] = 1.0
    sh[2, 126, 0] = 1.0
    sh[2, 127, 1] = 1.0
    sh[3, 127, 0] = 1.0
    c["shm"] = sh
    c["iota"] = np.arange(128, dtype=np.float32).reshape(128, 1)
    c["knaug"] = np.stack([np.full(8, 8192.0), np.arange(8.0), np.ones(8)]).astype(np.float32)
    bs = np.full((8, 136), -1e9, np.float32)
    bs[:, 0:129] = 0.0
    bs[:, [0, 127, 128]] = 1e3
    c["bons"] = bs
    kk = np.arange(8)[:, None]
    qq = np.arange(8)[None, :]
    c["cmask8"] = np.tile(np.where(kk <= qq, 0.0, NEGB).astype(np.float32), (1, 4))
    return c


def build_program(n_layers=DEPTH, n_tiles=32, debug=False, n_pool=2560, n_sb=4):
    nc = bass.Bass("TRN2", target_bir_lowering=False)
    kb = KB(nc)
    NT = n_tiles
    S = NT * 128

    def din(name, shape, dt=F32):
        return nc.dram_tensor(name, list(shape), dt, kind="ExternalInput").ap()

    def dout(name, shape, dt=F32):
        return nc.dram_tensor(name, list(shape), dt, kind="ExternalOutput").ap()

    def sb(name, shape, dt=F32):
        return nc.alloc_sbuf_tensor(name, list(shape), dt).ap()

    def ps(name, shape, dt=F32):
        return nc.alloc_psum_tensor(name, list(shape), dt).ap()

    xp = din("xp", [S, D_MODEL])
    pp = din("pp", [n_layers, S, D_PLE])
    g_norm = din("g_norm", [n_layers, D_MODEL])
    w_in = din("w_in", [n_layers, D_MODEL, D_IN])
    g_q = din("g_q", [n_layers, 64])
    g_k = din("g_k", [n_layers, 3, 64])
    cmp_pos = din("cmp_pos", [n_layers, 2, 32, 64])
    w_phi = din("w_phi", [n_layers, 2, 32, 64, 64])
    conv_w = din("conv_w", [n_layers, 3, 512])
    w_out = din("w_out", [n_layers, 1024, 1024])
    w_ple = din("w_ple", [n_layers, 256, 1024])
    w_pg = din("w_pg", [n_layers, 1024, 1024])
    g_ple = din("g_ple", [n_layers, 1024])
    c_ident = din("c_ident", [128, 128])
    c_kaug = din("c_kaug", [3, SEQ])
    c_kcaug = din("c_kcaug", [3, 128])
    c_qaug = din("c_qaug", [3, 8])
    c_qmul = din("c_qmul", [3, 65])
    c_iota = din("c_iota", [128, 1])
    c_knaug = din("c_knaug", [3, 8])
    c_bons = din("c_bons", [8, 136])
    c_cmask8 = din("c_cmask8", [8, 32])
    c_cmask = din("c_cmask", [128, 512])
    c_wmask = din("c_wmask", [128, 512])
    c_stair = din("c_stair", [64, SEQ])
    c_mcmp = din("c_mcmp", [128, 252])
    c_bonus = din("c_bonus", [32, 128, 64])
    c_shm = din("c_shm", [4, 128, 128])

    xs_in = din("xs", [n_sb, 8, D_MODEL])
    pps = din("pps", [n_layers, n_sb, 8, D_PLE])
    pool = din("pool", [n_layers * n_pool * 128, 512])
    cw_in = din("cwin", [n_layers, n_sb, 512, 256])
    sc_in = din("scin", [n_layers, n_sb, 2, 512])
    ptab = din("ptab", [n_sb, 64], I32)
    y_s = dout("y_s", [n_sb, 8, D_MODEL])
    kv_s = dout("kv_s", [n_layers, n_sb, 8, 512])
    win_s = dout("win_s", [n_layers, n_sb, 512, 256])
    conv_s = dout("conv_s", [n_layers, n_sb, 2, 512])
    xs_scr = nc.dram_tensor("xs_scr", [n_sb, 8, D_MODEL], F32, kind="Internal").ap()
    y_p = dout("y_p", [S, D_MODEL])
    kv_p = dout("kv_p", [n_layers, S, 512])
    WROWS = min(512, S)
    win_p = dout("win_p", [n_layers, WROWS, 256])
    conv_p = dout("conv_p", [n_layers, 2, 512])
    xscr = nc.dram_tensor("xscr", [S, D_MODEL], F32, kind="Internal").ap()

    W_in = sb("W_in", [128, 8, D_IN], BF16)
    W_out = sb("W_out", [128, 8, 1024], BF16)
    W_pg = sb("W_pg", [128, 8, 1024], BF16)
    W_ple = sb("W_ple", [128, 2, 1024], BF16)
    W_phi = sb("W_phi", [64, 2, 32, 64], BF16)
    gn = sb("gn", [128, 8])
    gp = sb("gp", [128, 8])
    gq8 = sb("gq8", [128, 64])
    gk = sb("gk", [128, 3, 64])
    cw = sb("cw", [128, 3, 512])
    posrep = sb("posrep", [128, 2, 64])

    identb = sb("identb", [128, 128], BF16)
    shm = sb("shm", [128, 4, 128])
    cmaskb = sb("cmaskb", [128, 512], BF16)
    wmaskb = sb("wmaskb", [128, 512], BF16)
    stairb = sb("stairb", [64, SEQ], BF16)
    mcmp = sb("mcmp", [128, 252])
    qaugf = sb("qaugf", [128, 8])
    qmulf = sb("qmulf", [128, 65])

    KsT = sb("KsT", [128, 2, SEQ], BF16)
    KwT = sb("KwT", [128, 2, 6 * 128], BF16)
    Vs = sb("Vs", [128, 32, 2, 65], BF16)
    Vw = sb("Vw", [128, 6, 2, 65], BF16)
    KcT = sb("KcT", [128, 2, 256], BF16)
    Vc = sb("Vc", [128, 2, 2, 64], BF16)

    xt = [sb("xt0", [128, 1024])] * 2
    pt = [sb("pt0", [128, 256])] * 2
    bon = [sb("bon%d" % j, [128, 64]) for j in range(2)]
    st = sb("st", [128, 8])
    xT = sb("xT", [128, 1024], BF16)
    proj = sb("proj", [128, D_IN])
    tmpa = sb("tmpa", [128, 512])
    tmpb = sb("tmpb", [128, 512])
    stg = [tmpa, tmpb]
    sm = sb("sm", [128, 64])
    qnb = sb("qnb", [128, 512], BF16)
    QT = sb("QT", [128, 1024], BF16)
    knb = sb("knb", [128, 256], BF16)
    gates = sb("gates", [128, 24])
    kcp = sb("kcp", [128, 256], BF16)
    kcT = sb("kcT", [64, 4, 128], BF16)
    cst = sb("cst", [8, 8])
    kcn = sb("kcn", [8, 64])
    kcnb = sb("kcnb", [8, 64], BF16)
    vcnb = sb("vcnb", [8, 64], BF16)
    pcm = sb("pcm", [128, 512])
    score = sb("score", [128, 64])
    score2 = sb("score2", [128, 64])
    m8 = sb("m8", [128, 16])
    selb = sb("selb", [128, 64], BF16)
    selTg = [sb("selT%d" % j, [64, 512], BF16) for j in range(2)]
    PT = [sb("PT%d" % j, [128, 512], BF16) for j in range(2)]
    pnb = PT[0]
    pnT = PT[1]
    junkb = pcm.bitcast(BF16)
    PT2 = [sb("PTx%d" % j, [128, 512], BF16) for j in range(2)]
    iota = sb("iota", [128, 1])
    idxi = sb("idxi", [128, 64], I32)
    idxf = sb("idxf", [128, 64])
    KnT = sb("KnT", [128, 2, 2, 8], BF16)
    Vn = sb("Vn", [8, 2, 2, 65], BF16)
    scS = sb("scS", [8, 136])
    scS2 = sb("scS2", [8, 136])
    selbS = sb("selbS", [8, 128], BF16)
    selTS = sb("selTS", [64, 2, 2, 32], BF16)
    bonsS = sb("bonsS", [8, 136])
    cmask8 = sb("cmask8", [8, 32], BF16)
    oat = proj[:, C_Q:C_Q + 512]
    ocf = sb("ocf", [128, 16])
    omix = sb("omix", [128, 1024], BF16)
    xb = omix
    u = proj[:, C_CG:C_CG + 512]
    v0 = tmpa
    v1 = pcm
    carry = sb("carry", [128, 512])
    sg = tmpa
    pb = sb("pb", [128, 256], BF16)
    pTT = sb("pTT", [128, 256], BF16)

    pA = [ps("pA%d" % j, [128, 512]) for j in range(2)]
    pT = ps("pT", [128, 1024], BF16)
    pS = [ps("pS%d" % j, [128, 512]) for j in range(2)]
    pO = ps("pO", [128, 4, 65])
    pW = ps("pW", [128, 4, 65])
    pC = ps("pC", [128, 512])

    dumped = set()

    def dump(name, ap, res, shape, dt=F32):
        if not debug or name in dumped:
            return
        dumped.add(name)
        d = nc.dram_tensor("dbg_" + name, list(shape), dt, kind="ExternalOutput").ap()
        kb.dma(d, ap, list(res), [("dbg", name)])

    def mm(out, lhsT, rhs, start, stop, R, W, skip=False):
        if skip:
            kb.op("pe", lambda e: e.matmul(out, lhsT=lhsT, rhs=rhs, start=start, stop=stop, skip_group_check=True), R, W)
        else:
            kb.op("pe", lambda e: e.matmul(out, lhsT=lhsT, rhs=rhs, start=start, stop=stop), R, W)

    def tr(out, in_, ident, R, W):
        kb.op("pe", lambda e: e.transpose(out=out, in_=in_, identity=ident), R, W)

    def act(out, in_, func, R, W, **kw):
        kb.op("act", lambda e: e.activation(out=out, in_=in_, func=func, **kw), R, W)

    def cp(eng, out, in_, R, W):
        if eng == "act":
            kb.op("act", lambda e: e.copy(out=out, in_=in_), R, W)
        else:
            kb.op(eng, lambda e: e.tensor_copy(out=out, in_=in_), R, W)

    def tt(eng, out, in0, in1, op, R, W):
        kb.op(eng, lambda e: e.tensor_tensor(out=out, in0=in0, in1=in1, op=op), R, W)

    def ts(eng, out, in0, s1, s2, op0, op1, R, W):
        if op1 is None:
            kb.op(eng, lambda e: e.tensor_scalar(out=out, in0=in0, scalar1=s1, scalar2=None, op0=op0), R, W)
        else:
            kb.op(eng, lambda e: e.tensor_scalar(out=out, in0=in0, scalar1=s1, scalar2=s2, op0=op0, op1=op1), R, W)

    def red(eng, out, in_, axis, op, R, W):
        kb.op(eng, lambda e: e.tensor_reduce(out=out, in_=in_, axis=axis, op=op), R, W)

    def rstd(col_in, col_out, n, R):
        act(col_out, col_in, AF.Sqrt, R, R, scale=1.0 / n, bias=EPS)
        kb.op("dve", lambda e: e.reciprocal(out=col_out, in_=col_out), R, R)

    kb.dma(tmpa[:, 0:128], c_ident, ["tmpa"], ["tmpa"])
    cp("dve", identb, tmpa[:, 0:128], ["tmpa"], ["identb"])
    kb.dma(shm, c_shm.rearrange("k a b -> a k b"), [], ["shm"])
    kb.dma(mcmp, c_mcmp, [], ["mcmp"])
    kb.dma(qaugf[64:67, :], c_qaug, [], ["qaugf"])
    kb.dma(qmulf[64:67, :], c_qmul, [], ["qmulf"])
    kb.dma(stg[0][:, 0:512], c_cmask, ["tmpa"], ["tmpa"])
    cp("dve", cmaskb, stg[0][:, 0:512], ["tmpa"], ["cmaskb"])
    kb.dma(stg[1][:, 0:512], c_wmask, ["tmpb"], ["tmpb"])
    cp("dve", wmaskb, stg[1][:, 0:512], ["tmpb"], ["wmaskb"])
    for j in range(SEQ // 512):
        s_ = stg[j % 2]
        kb.dma(s_[0:64, :], c_stair[:, j * 512:(j + 1) * 512], [("tmpa", "tmpb")[j % 2]], [("tmpa", "tmpb")[j % 2]])
        cp("dve", stairb[:, j * 512:(j + 1) * 512], s_[0:64, :], [("tmpa", "tmpb")[j % 2]], ["stairb"])
    for j in range(SEQ // 512):
        s_ = stg[j % 2]
        kb.dma(s_[64:67, :], c_kaug[:, j * 512:(j + 1) * 512], [("tmpa", "tmpb")[j % 2]], [("tmpa", "tmpb")[j % 2]])
        for g in range(2):
            cp("dve", KsT[64:67, g, j * 512:(j + 1) * 512], s_[64:67, :], [("tmpa", "tmpb")[j % 2]], ["KsT"])
    kb.dma(stg[0][64:67, 0:128], c_kcaug, ["tmpa"], ["tmpa"])
    for g in range(2):
        for hf in range(2):
            cp("dve", KcT[64:67, g, hf * 128:(hf + 1) * 128], stg[0][64:67, 0:128], ["tmpa"], ["KcT"])
    kb.dma(iota, c_iota, [], ["iota"])
    kb.dma(bonsS, c_bons, [], ["bonsS"])
    kb.dma(stg[1][0:8, 0:32], c_cmask8, ["tmpb"], ["tmpb"])
    cp("dve", cmask8, stg[1][0:8, 0:32], ["tmpb"], ["cmask8"])
    kb.dma(stg[1][64:67, 0:8], c_knaug, ["tmpb"], ["tmpb"])
    for a_ in range(2):
        for g in range(2):
            cp("dve", KnT[64:67, a_, g, :], stg[1][64:67, 0:8], ["tmpb"], ["KnT"])
    kb.op("dve", lambda e: e.memset(Vn[:, :, :, 64:65], 1.0), [], ["Vn"])
    kb.op("dve", lambda e: e.memset(Vs[:, :, :, 64:65], 1.0), [], ["Vs"])
    kb.op("dve", lambda e: e.memset(Vw[:, :, :, 64:65], 1.0), [], ["Vw"])

    stg_ctr = [0]

    def load_cast(dst_ap_fn, src_ap_fn, nchunks, scale_col_fn, wname, parts=128):
        for j in range(nchunks):
            k = stg_ctr[0] % 2
            stg_ctr[0] += 1
            sname = ("tmpa", "tmpb")[k]
            src, width = src_ap_fn(j)
            kb.dma(stg[k][0:parts, 0:width], src, [sname], [sname])
            dst = dst_ap_fn(j)
            sc = scale_col_fn(j) if scale_col_fn is not None else None
            eng = "dve" if j % 2 == 0 else "pool"
            if sc is None:
                cp(eng, dst, stg[k][0:parts, 0:width], [sname], [wname])
            else:
                ts(eng, dst, stg[k][0:parts, 0:width], sc, None, ALU.mult, None, [sname, "gvec"], [wname])

    def load_layer(l):
        kb.dma(gn, g_norm[l].rearrange("(c p) -> p c", p=128), [], ["gvec"], allow_slow_non_contiguous=True)
        kb.dma(gp, g_ple[l].rearrange("(c p) -> p c", p=128), [], ["gvec"], allow_slow_non_contiguous=True)
        kb.dma(gq8, g_q[l].partition_broadcast(128), [], ["gsm"])
        kb.dma(gk, g_k[l].partition_broadcast(128), [], ["gsm"])
        kb.dma(cw, conv_w[l].partition_broadcast(128), [], ["gsm"])
        for c4 in range(4):
            kb.dma(posrep[c4 * 32:(c4 + 1) * 32, :, :], cmp_pos[l].rearrange("kv l d -> l kv d"), [], ["gsm"])
        ts("dve", gq8, gq8, 0.125, None, ALU.mult, None, ["gsm"], ["gsm"])
        wcols = [(j * 512, min(512, D_IN - j * 512)) for j in range(8)]
        for kc in range(8):
            load_cast(lambda j, kc=kc: W_in[:, kc, wcols[j][0]:wcols[j][0] + wcols[j][1]],
                      lambda j, kc=kc: (w_in[l, kc * 128:(kc + 1) * 128, wcols[j][0]:wcols[j][0] + wcols[j][1]], wcols[j][1]),
                      8, lambda j, kc=kc: gn[:, kc:kc + 1], "W_in")
        for kc in range(8):
            load_cast(lambda j, kc=kc: W_out[:, kc, j * 512:(j + 1) * 512],
                      lambda j, kc=kc: (w_out[l, kc * 128:(kc + 1) * 128, j * 512:(j + 1) * 512], 512), 2, None, "W_out")
        for kc in range(8):
            load_cast(lambda j, kc=kc: W_pg[:, kc, j * 512:(j + 1) * 512],
                      lambda j, kc=kc: (w_pg[l, kc * 128:(kc + 1) * 128, j * 512:(j + 1) * 512], 512), 2,
                      lambda j, kc=kc: gp[:, kc:kc + 1], "W_pg")
        for kc in range(2):
            load_cast(lambda j, kc=kc: W_ple[:, kc, j * 512:(j + 1) * 512],
                      lambda j, kc=kc: (w_ple[l, kc * 128:(kc + 1) * 128, j * 512:(j + 1) * 512], 512), 2, None, "W_ple")
        for kv in range(2):
            for hh in range(4):
                k = stg_ctr[0] % 2
                stg_ctr[0] += 1
                sname = ("tmpa", "tmpb")[k]
                kb.dma(stg[k][0:64, :].rearrange("d (l e) -> d l e", e=64),
                       w_phi[l, kv, hh * 8:(hh + 1) * 8].rearrange("l d e -> d l e"), [sname], [sname])
                cp("dve", W_phi[:, kv, hh * 8:(hh + 1) * 8, :],
                   stg[k][0:64, :].rearrange("d (l e) -> d l e", e=64), [sname], ["W_phi"])

    def load_tile_inputs(l, i):
        s = i % 2
        src = xp if l == 0 else xscr
        kb.dma(xt[s], src[i * 128:(i + 1) * 128, :], [("xd", i)], ["xt0"])
        kb.dma(pt[s], pp[l, i * 128:(i + 1) * 128, :], [], ["pt0"])
        kb.dma(bon[s], c_bonus[i], [], ["bon%d" % s])

    def prompt_tile(l, i):
        s = i % 2
        X = xt[s]
        xn = "xt0"
        last_layer = (l == n_layers - 1)
        cp("dve", xb, X, [xn], ["omix"])
        for c in range(8):
            tr(pT[:, c * 128:(c + 1) * 128], xb[:, c * 128:(c + 1) * 128], identb, ["omix", "identb"], ["pT"])
        cp("act", xT, pT, ["pT"], ["xT"])
        kb.op("dve", lambda e: e.memset(st, 0.0), ["st"], ["st"])
        act(junkb, X, AF.Square, [xn, "st"], ["pcm", "st"], accum_out=st[:, 0:1])
        rstd(st[:, 0:1], st[:, 1:2], 1024.0, ["st"])
        dump("xT", xT, ["xT"], [128, 1024], BF16)
        dump("xb", xb, ["omix"], [128, 1024], BF16)
        dump("W_in0", W_in[:, 0, 0:512], ["W_in"], [128, 512], BF16)
        dump("st", st, ["st"], [128, 8])
        for gi, (c0, c1, rn) in enumerate(COL_GROUPS):
            w = c1 - c0
            bank = pA[gi % 2]
            bn = "pA%d" % (gi % 2)
            for kc in range(8):
                mm(bank[:, 0:w], xT[:, kc * 128:(kc + 1) * 128], W_in[:, kc, c0:c1], kc == 0, kc == 7,
                   ["xT", "W_in"], [bn])
            act(proj[:, c0:c1], bank[:, 0:w], AF.Copy, [bn, "st"], [rn], scale=st[:, 1:2])
        dump("proj", proj, [r for _, _, r in COL_GROUPS], [128, D_IN])
        tt("dve", kcp.rearrange("p (a h d) -> p a h d", a=2, h=2), proj[:, C_KC:C_KC + 256].rearrange("p (a h d) -> p a h d", a=2, h=2),
           posrep.unsqueeze(2).to_broadcast([128, 2, 2, 64]), ALU.add, ["pj_kv", "gsm"], ["kcp"])
        for j in range(4):
            tr(pT[0:64, j * 128:(j + 1) * 128], kcp[:, j * 64:(j + 1) * 64], identb, ["kcp", "identb"], ["pT"])
        cp("act", kcT.rearrange("d a t -> d (a t)"), pT[0:64, 0:512], ["pT"], ["kcT"])
        for kv in range(2):
            for ll in range(32):
                lhs = kcT[:, 2 * kv:2 * kv + 2, :].rearrange("d h (c l) -> d h c l", l=32)[:, :, :, ll]
                mm(pC[0:8, kv * 64:(kv + 1) * 64], lhs, W_phi[:, kv, ll, :], ll == 0, ll == 31, ["kcT", "W_phi"], ["pC"])
        q3 = proj[:, C_Q:C_Q + 512].rearrange("p (h d) -> p h d", d=64)
        tt("dve", tmpa, proj[:, C_Q:C_Q + 512], proj[:, C_Q:C_Q + 512], ALU.mult, ["pj_q"], ["tmpa"])
        red("dve", sm[:, 0:8], tmpa.rearrange("p (h d) -> p h d", d=64), AX.X, ALU.add, ["tmpa"], ["sm"])
        rstd(sm[:, 0:8], sm[:, 0:8], 64.0, ["sm"])
        tt("dve", tmpa.rearrange("p (h d) -> p h d", d=64), q3, sm[:, 0:8].unsqueeze(2).to_broadcast([128, 8, 64]),
           ALU.mult, ["pj_q", "sm"], ["tmpa"])
        tt("pool", qnb.rearrange("p (h d) -> p h d", d=64), tmpa.rearrange("p (h d) -> p h d", d=64),
           gq8.unsqueeze(1).to_broadcast([128, 8, 64]), ALU.mult, ["tmpa", "gsm"], ["qnb"])
        for h in range(8):
            tr(pT[0:64, h * 128:(h + 1) * 128], qnb[:, h * 64:(h + 1) * 64], identb, ["qnb", "identb"], ["pT"])
        cp("act", QT[0:64, :], pT[0:64, :], ["pT"], ["QT"])
        ts("dve", QT[64:67, :].rearrange("p (h q) -> p h q", h=8), qaugf[64:67, :].unsqueeze(2).to_broadcast([3, 8, 128]),
           qmulf[64:67, i:i + 1], None, ALU.mult, None, ["qaugf", "qmulf"], ["QT"])
        for which, (c0, gi_) in enumerate(((C_KS, 1), (C_KW, 2))):
            rn = "pj_kv" if which == 0 else "pj_w"
            k3 = proj[:, c0:c0 + 128].rearrange("p (h d) -> p h d", d=64)
            tt("dve", tmpb[:, 0:128], proj[:, c0:c0 + 128], proj[:, c0:c0 + 128], ALU.mult, [rn], ["tmpb"])
            red("dve", sm[:, 8 + 2 * which:10 + 2 * which], tmpb[:, 0:128].rearrange("p (h d) -> p h d", d=64),
                AX.X, ALU.add, ["tmpb"], ["sm"])
            rstd(sm[:, 8 + 2 * which:10 + 2 * which], sm[:, 8 + 2 * which:10 + 2 * which], 64.0, ["sm"])
            tt("dve", k3, k3, sm[:, 8 + 2 * which:10 + 2 * which].unsqueeze(2).to_broadcast([128, 2, 64]),
               ALU.mult, [rn, "sm"], [rn])
            tt("dve", k3, k3, gk[:, gi_, :].unsqueeze(1).to_broadcast([128, 2, 64]), ALU.mult, [rn, "gsm"], [rn])
            cp("pool", knb[:, which * 128:(which + 1) * 128], proj[:, c0:c0 + 128], [rn], ["knb"])
        for j in range(4):
            tr(pT[0:64, j * 128:(j + 1) * 128], knb[:, j * 64:(j + 1) * 64], identb, ["knb", "identb"], ["pT"])
        ws = i % 6
        for g in range(2):
            cp("act", KsT[0:64, g, i * 128:(i + 1) * 128], pT[0:64, g * 128:(g + 1) * 128], ["pT"], [("KsT", i)])
            cp("act", KwT[0:64, g, ws * 128:(ws + 1) * 128], pT[0:64, (2 + g) * 128:(3 + g) * 128], ["pT"], [("KwT", ws)])
            cp("dve", KwT[64:67, g, ws * 128:(ws + 1) * 128], KsT[64:67, g, i * 128:(i + 1) * 128], ["KsT"], [("KwT", ws)])
        cp("pool", Vs[:, i, :, 0:64], proj[:, C_VS:C_VS + 128].rearrange("p (h d) -> p h d", d=64), ["pj_kv"], [("Vs", i)])
        cp("pool", Vw[:, ws, :, 0:64], proj[:, C_VW:C_VW + 128].rearrange("p (h d) -> p h d", d=64), ["pj_w"], [("Vw", ws)])
        kb.dma(kv_p[l, i * 128:(i + 1) * 128, :], proj[:, C_KC:C_KC + 512], ["pj_kv"], [("kv_p", l, i)])
        r0 = i * 128 - (S - WROWS)
        if r0 >= 0:
            kb.dma(win_p[l, r0:r0 + 128, :], proj[:, C_KW:C_KW + 256], ["pj_w"], [("win_p", l, i)])
        act(gates, proj[:, C_GL:C_GL + 24], AF.Sigmoid, ["pj_w"], ["gates"])
        kb.op("dve", lambda e: e.memset(cst, 0.0), ["cst"], ["cst"])
        act(kcn, pC[0:8, 0:64], AF.Square, ["pC", "cst"], ["kcn", "cst"], accum_out=cst[:, 0:1])
        rstd(cst[:, 0:1], cst[:, 1:2], 64.0, ["cst"])
        ts("dve", kcn, pC[0:8, 0:64], cst[:, 1:2], None, ALU.mult, None, ["pC", "cst"], ["kcn"])
        tt("dve", kcnb, kcn, gk[0:8, 0, :], ALU.mult, ["kcn", "gsm"], ["kcnb"])
        cp("act", vcnb, pC[0:8, 64:128], ["pC"], ["vcnb"])
        tr(pT[0:64, 0:8], kcnb, identb[0:8, 0:8], ["kcnb", "identb"], ["pT"])
        cp("act", KcT[0:64, :, 4 * i:4 * i + 4], pT[0:64, 0:8].rearrange("d (h c) -> d h c", h=2), ["pT"], ["KcT"])
        for g in range(2):
            kb.dma(Vc[4 * i:4 * i + 4, g, 0, :], vcnb[4 * g:4 * g + 4, :], ["vcnb", "Vc"], ["Vc"])

        def cmp_scores(g):
            for h in range(4):
                mm(pC[:, h * 128:(h + 1) * 128], QT[0:67, (4 * g + h) * 128:(4 * g + h + 1) * 128], KcT[0:67, g, 0:128],
                   True, True, ["QT", "KcT"], ["pC"])

        def cmp_chain(g):
            off = 124 - 4 * i
            tt("dve", tmpa.rearrange("p (h c) -> p h c", h=4), pC.rearrange("p (h c) -> p h c", h=4),
               mcmp[:, off:off + 128].unsqueeze(1).to_broadcast([128, 4, 128]), ALU.add, ["pC", "mcmp"], ["tmpa"])
            act(pcm, tmpa, AF.Exp, ["tmpa"], ["pcm"])
            red("dve", sm[:, 16:20], pcm.rearrange("p (h c) -> p h c", h=4), AX.X, ALU.add, ["pcm"], ["sm"])
            ts("dve", sm[:, 16:20], sm[:, 16:20], 1e-30, None, ALU.max, None, ["sm"], ["sm"])
            kb.op("dve", lambda e: e.reciprocal(out=sm[:, 16:20], in_=sm[:, 16:20]), ["sm"], ["sm"])
            tt("dve", pcm.rearrange("p (h c) -> p h c", h=4), pcm.rearrange("p (h c) -> p h c", h=4),
               sm[:, 16:20].unsqueeze(2).to_broadcast([128, 4, 128]), ALU.mult, ["pcm", "sm"], ["pcm"])
            cp("pool", pnb, pcm, ["pcm"], ["PT0"])
            for h in range(4):
                tr(pT[:, (4 + h) * 128:(5 + h) * 128], pnb[:, h * 128:(h + 1) * 128], identb, ["PT0", "identb"], ["pT"])
            cp("act", pnT, pT[:, 512:1024], ["pT"], ["PT1"])
            if i < 8:
                return
            red("dve", score, pcm.rearrange("p (h b t) -> p b h t", h=4, t=2), AX.XY, ALU.add, ["pcm"], ["score"])
            tt("dve", score, score, bon[s], ALU.add, ["score", "bon%d" % s], ["score"])
            kb.op("dve", lambda e: e.max(out=m8[:, 0:8], in_=score), ["score"], ["m8"])
            kb.op("dve", lambda e: e.match_replace(out=score2, in_to_replace=m8[:, 0:8], in_values=score, imm_value=-1e30),
                  ["score", "m8"], ["score2"])
            kb.op("dve", lambda e: e.max(out=m8[:, 8:16], in_=score2), ["score2"], ["m8"])
            ts("dve", score2, score, m8[:, 15:16], None, ALU.is_ge, None, ["score", "m8"], ["score2"])
            ts("dve", selb, score2, -NEGB, NEGB, ALU.mult, ALU.add, ["score2"], ["selb"])
            tr(pT[0:64, 0:128], selb, identb, ["selb", "identb"], ["pT"])
            cp("act", selTg[g].rearrange("b (h q) -> b h q", h=4), pT[0:64, 0:128].unsqueeze(1).to_broadcast([64, 4, 128]),
               ["pT"], ["selT%d" % g])

        def o_cmp(g):
            for h in range(4):
                mm(pC[:, h * 64:(h + 1) * 64], pnT[:, h * 128:(h + 1) * 128], Vc[:, g, 0, :], True, True, ["PT1", "Vc"], ["pC"])
            oat3 = oat[:, g * 256:(g + 1) * 256].rearrange("p (h d) -> p h d", d=64)
            g3 = gates[:, g * 12:(g + 1) * 12].rearrange("p (h b) -> p h b", b=3)
            tt("dve", oat3, pC[:, 0:256].rearrange("p (h d) -> p h d", d=64),
               g3[:, :, 0:1].to_broadcast([128, 4, 64]), ALU.mult, ["pC", "gates"], ["pj_q"])

        def branch(g, br, pacc, pname):
            Qg = QT[0:67, g * 512:(g + 1) * 512]
            kts = list(range(0, i + 1)) if br == 0 else list(range(max(0, i - 4), i + 1))
            pend = None
            for idx, kt in enumerate(kts):
                b2 = idx % 2
                bank = pS[b2]
                bname = "pS%d" % b2
                if br == 0:
                    klhs = KsT[0:67, g, kt * 128:(kt + 1) * 128]
                    kres = ("KsT", kt)
                    vv = Vs[:, kt, g, :]
                    vres = ("Vs", kt)
                else:
                    wsl = kt % 6
                    klhs = KwT[0:67, g, wsl * 128:(wsl + 1) * 128]
                    kres = ("KwT", wsl)
                    vv = Vw[:, wsl, g, :]
                    vres = ("Vw", wsl)
                diag = (kt == i)
                wfirst = (br == 1 and kt == i - 4)
                use_sel = (br == 0 and i >= 8)
                nextra = (1 if use_sel else 0) + (1 if diag else 0) + (1 if wfirst else 0)
                mm(bank, klhs, Qg, True, nextra == 0, [kres, "KsT", "KwT", "QT"], [bname])
                if use_sel:
                    nextra -= 1
                    mm(bank, stairb[:, kt * 128:(kt + 1) * 128], selTg[g], False, nextra == 0, ["stairb", "selT%d" % g], [bname])
                if diag:
                    nextra -= 1
                    mm(bank, identb, cmaskb, False, nextra == 0, ["identb", "cmaskb"], [bname])
                if wfirst:
                    nextra -= 1
                    mm(bank, identb, wmaskb, False, nextra == 0, ["identb", "wmaskb"], [bname])
                act(PT2[b2], bank, AF.Exp, [bname], ["PTx%d" % b2])
                if pend is not None:
                    pend()

                def pv(b2=b2, vv=vv, vres=vres, first=(idx == 0), lastk=(idx == len(kts) - 1)):
                    for h in range(4):
                        mm(pacc[:, h, :], PT2[b2][:, h * 128:(h + 1) * 128], vv, first and h == 0, lastk,
                           ["PTx%d" % b2, vres, "Vs", "Vw"], [pname], skip=True)
                pend = pv
            pend()

        def evac(g, br, pacc, pname):
            g3 = gates[:, g * 12:(g + 1) * 12].rearrange("p (h b) -> p h b", b=3)
            ts("dve", ocf[:, 0:4], pacc[:, :, 64], 1e-30, None, ALU.max, None, [pname], ["ocf"])
            kb.op("dve", lambda e: e.reciprocal(out=ocf[:, 0:4], in_=ocf[:, 0:4]), ["ocf"], ["ocf"])
            tt("dve", ocf[:, 4:8], ocf[:, 0:4], g3[:, :, 1 + br], ALU.mult, ["ocf", "gates"], ["ocf"])
            tt("dve", tmpb[:, 0:256].rearrange("p (h d) -> p h d", d=64), pacc[:, :, 0:64],
               ocf[:, 4:8].unsqueeze(2).to_broadcast([128, 4, 64]), ALU.mult, [pname, "ocf"], ["tmpb"])
            tt("pool", oat[:, g * 256:(g + 1) * 256], oat[:, g * 256:(g + 1) * 256], tmpb[:, 0:256], ALU.add,
               ["pj_q", "tmpb"], ["pj_q"])

        for g in range(2):
            cmp_scores(g)
            branch(g, 1, pW, "pW")
            cmp_chain(g)
            o_cmp(g)
            evac(g, 1, pW, "pW")
        branch(0, 0, pO, "pO")
        branch(1, 0, pW, "pW")
        evac(0, 0, pO, "pO")
        evac(1, 0, pW, "pW")
        act(tmpa, proj[:, C_ZA:C_ZA + 512], AF.Silu, ["pj_za"], ["tmpa"])
        tt("dve", omix[:, 0:512], oat, tmpa, ALU.mult, ["pj_q", "tmpa"], ["omix"])
        tt("pool", u, proj[:, C_CG:C_CG + 512], proj[:, C_HV:C_HV + 512], ALU.mult, ["pj_cg", "pj_hv"], ["pj_cg"])
        tt("pool", v0, u, cw[:, 0, :], ALU.mult, ["pj_cg", "gsm"], ["tmpa"])
        tt("pool", v1, u, cw[:, 1, :], ALU.mult, ["pj_cg", "gsm"], ["pcm"])
        tt("pool", tmpb, u, cw[:, 2, :], ALU.mult, ["pj_cg", "gsm"], ["tmpb"])
        mm(pA[0], shm[:, 0, :], v0, True, False, ["shm", "tmpa"], ["pA0"])
        mm(pA[0], shm[:, 1, :], v1, False, True, ["shm", "pcm"], ["pA0"])
        tt("dve", tmpb, pA[0], tmpb, ALU.add, ["pA0", "tmpb"], ["tmpb"])
        if i > 0:
            tt("dve", tmpb, tmpb, carry, ALU.add, ["tmpb", "carry"], ["tmpb"])
        if i < NT - 1:
            mm(pA[1], shm[:, 2, :], v0, True, False, ["shm", "tmpa"], ["pA1"])
            mm(pA[1], shm[:, 3, :], v1, False, True, ["shm", "pcm"], ["pA1"])
            cp("act", carry, pA[1], ["pA1"], ["carry"])
        tt("dve", tmpb, tmpb, proj[:, C_BG:C_BG + 512], ALU.mult, ["tmpb", "pj_bg"], ["tmpb"])
        act(tmpa, proj[:, C_ZB:C_ZB + 512], AF.Silu, ["pj_zb"], ["tmpa"])
        tt("dve", omix[:, 512:1024], tmpb, tmpa, ALU.mult, ["tmpb", "tmpa"], ["omix"])
        if i == NT - 1:
            kb.dma(conv_p[l], u[126:128, :], ["pj_cg"], [("conv_p", l)])
        for c in range(8):
            tr(pT[:, c * 128:(c + 1) * 128], omix[:, c * 128:(c + 1) * 128], identb, ["omix", "identb"], ["pT"])
        cp("act", xT, pT, ["pT"], ["xT"])
        for cg in range(2):
            bn = "pA%d" % cg
            for kc in range(8):
                mm(pA[cg], xT[:, kc * 128:(kc + 1) * 128], W_out[:, kc, cg * 512:(cg + 1) * 512], kc == 0, kc == 7,
                   ["xT", "W_out"], [bn])
            tt("dve", X[:, cg * 512:(cg + 1) * 512], pA[cg], X[:, cg * 512:(cg + 1) * 512], ALU.add, [bn, xn], [xn])
        cp("dve", xb, X, [xn], ["omix"])
        for c in range(8):
            tr(pT[:, c * 128:(c + 1) * 128], xb[:, c * 128:(c + 1) * 128], identb, ["omix", "identb"], ["pT"])
        cp("act", xT, pT, ["pT"], ["xT"])
        kb.op("dve", lambda e: e.memset(st[:, 2:4], 0.0), ["st"], ["st"])
        act(junkb, X, AF.Square, [xn, "st"], ["pcm", "st"], accum_out=st[:, 2:3])
        rstd(st[:, 2:3], st[:, 3:4], 1024.0, ["st"])
        cp("pool", pb, pt[s], ["pt0"], ["pb"])
        for c in range(2):
            tr(pT[:, c * 128:(c + 1) * 128], pb[:, c * 128:(c + 1) * 128], identb, ["pb", "identb"], ["pT"])
        cp("act", pTT, pT[:, 0:256], ["pT"], ["pTT"])
        for cg in range(2):
            for kc in range(8):
                mm(pA[0], xT[:, kc * 128:(kc + 1) * 128], W_pg[:, kc, cg * 512:(cg + 1) * 512], kc == 0, kc == 7,
                   ["xT", "W_pg"], ["pA0"])
            act(sg, pA[0], AF.Sigmoid, ["pA0", "st"], ["tmpa"], scale=st[:, 3:4])
            for kc in range(2):
                mm(pA[1], pTT[:, kc * 128:(kc + 1) * 128], W_ple[:, kc, cg * 512:(cg + 1) * 512], kc == 0, kc == 1,
                   ["pTT", "W_ple"], ["pA1"])
            tt("dve", sg, pA[1], sg, ALU.mult, ["pA1", "tmpa"], ["tmpa"])
            tt("pool", X[:, cg * 512:(cg + 1) * 512], X[:, cg * 512:(cg + 1) * 512], sg, ALU.add, [xn, "tmpa"], [xn])
        dst = y_p if last_layer else xscr
        kb.dma(dst[i * 128:(i + 1) * 128, :], X, [xn], [("xd", i)])

    pg = [tmpb[:, 0:256], tmpb[:, 256:512], pcm[:, 0:256], pcm[:, 256:512]]
    pgn = ["pgq0", "pgq1", "pgq2", "pgq3"]
    pool2 = pool.rearrange("r (two c) -> (r two) c", two=2)
    idxB = sb("idxB", [128, 64], I32)

    def fence():
        allr = ["tmpb", "pcm", "knb", "pT", "pS0", "pS1", "PT0", "PT1"] + pgn
        allr += [("pSq", g_, r_) for g_ in range(2) for r_ in range(4)] + [("PTq", g_, r_) for g_ in range(2) for r_ in range(4)]
        allr += [("knbq", 0), ("knbq", 1), ("pTq", 0), ("pTq", 1)]
        kb.op("dve", lambda e: e.memset(cst[0:1, 4:5], 0.0), allr, allr)
    QTv = QT.rearrange("p (h q) -> p h q", h=8)
    pTv = pT.rearrange("p (h q) -> p h q", h=8)

    def gather_page(l, j, k, second):
        buf = pg[k]
        src = pool2
        off = (idxB if second else idxi)[:, j:j + 1]
        kb.op("pool", lambda e: e.indirect_dma_start(out=buf, out_offset=None, in_=src,
                                                     in_offset=bass.IndirectOffsetOnAxis(ap=off, axis=0)),
              ["idxi"], [pgn[k]], dma=True)

    def sample_tile(l, bl):
        P = 8
        X = xt[0]
        xn = "xt0"
        last_layer = (l == n_layers - 1)
        src = xs_in if l == 0 else xs_scr
        kb.dma(X[0:P, :], src[bl], [("xsd", bl)], [xn])
        kb.dma(pt[0][0:P, :], pps[l, bl], [], ["pt0"])
        kb.dma(idxi, ptab[bl].partition_broadcast(128), [], ["idxi"])
        cp("dve", idxf, idxi, ["idxi"], ["idxf"])
        ts("dve", idxf, idxf, 128.0, iota[:, 0:1], ALU.mult, ALU.add, ["idxf", "iota"], ["idxf"])
        ts("dve", idxf, idxf, float(l * n_pool * 128), 2.0, ALU.add, ALU.mult, ["idxf"], ["idxf"])
        cp("dve", idxi, idxf, ["idxf"], ["idxi"])
        ts("dve", idxf, idxf, 1.0, None, ALU.add, None, ["idxf"], ["idxf"])
        cp("dve", idxB, idxf, ["idxf"], ["idxi"])
        kb.op("dve", lambda e: e.memset(st, 0.0), ["st"], ["st"])
        act(omix[0:P, :], X[0:P, :], AF.Square, [xn, "st"], ["omix", "st"], accum_out=st[0:P, 0:1])
        rstd(st[0:P, 0:1], st[0:P, 1:2], 1024.0, ["st"])
        cp("dve", xb[0:P, :], X[0:P, :], [xn], ["omix"])
        for c in range(8):
            tr(pT[:, c * 128:c * 128 + P], xb[0:P, c * 128:(c + 1) * 128], identb[0:P, 0:P], ["omix", "identb"], ["pT"])
        cp("act", xT, pT, ["pT"], ["xT"])
        for gi, (c0, c1, rn) in enumerate(COL_GROUPS):
            w = c1 - c0
            bank = pA[gi % 2]
            bn = "pA%d" % (gi % 2)
            for kc in range(8):
                mm(bank[0:P, 0:w], xT[:, kc * 128:kc * 128 + P], W_in[:, kc, c0:c1], kc == 0, kc == 7, ["xT", "W_in"], [bn])
            act(proj[0:P, c0:c1], bank[0:P, 0:w], AF.Copy, [bn, "st"], [rn], scale=st[0:P, 1:2])
        q3 = proj[0:P, C_Q:C_Q + 512].rearrange("p (h d) -> p h d", d=64)
        ta3 = tmpa[0:P, :].rearrange("p (h d) -> p h d", d=64)
        tt("dve", tmpa[0:P, :], proj[0:P, C_Q:C_Q + 512], proj[0:P, C_Q:C_Q + 512], ALU.mult, ["pj_q"], ["tmpa"])
        red("dve", sm[0:P, 0:8], ta3, AX.X, ALU.add, ["tmpa"], ["sm"])
        rstd(sm[0:P, 0:8], sm[0:P, 0:8], 64.0, ["sm"])
        tt("dve", ta3, q3, sm[0:P, 0:8].unsqueeze(2).to_broadcast([P, 8, 64]), ALU.mult, ["pj_q", "sm"], ["tmpa"])
        tt("pool", qnb[0:P, :].rearrange("p (h d) -> p h d", d=64), ta3, gq8[0:P, :].unsqueeze(1).to_broadcast([P, 8, 64]),
           ALU.mult, ["tmpa", "gsm"], ["qnb"])
        for h in range(8):
            tr(pT[0:64, h * 128:h * 128 + P], qnb[0:P, h * 64:(h + 1) * 64], identb[0:P, 0:P], ["qnb", "identb"], ["pT"])
        for vq, qb_ in ((0, 64), (1, 32)):
            cp("act", QTv[0:64, :, vq * 8:vq * 8 + 8], pTv[0:64, :, 0:8], ["pT"], ["QT"])
            ts("dve", QTv[64:67, :, vq * 8:vq * 8 + 8], qaugf[64:67, :].unsqueeze(2).to_broadcast([3, 8, 8]),
               qmulf[64:67, qb_:qb_ + 1], None, ALU.mult, None, ["qaugf", "qmulf"], ["QT"])
        for which, (c0, gi_) in enumerate(((C_KS, 1), (C_KW, 2))):
            rn = "pj_kv" if which == 0 else "pj_w"
            k3 = proj[0:P, c0:c0 + 128].rearrange("p (h d) -> p h d", d=64)
            tt("dve", tmpb[0:P, 0:128], proj[0:P, c0:c0 + 128], proj[0:P, c0:c0 + 128], ALU.mult, [rn], ["tmpb"])
            red("dve", sm[0:P, 8 + 2 * which:10 + 2 * which], tmpb[0:P, 0:128].rearrange("p (h d) -> p h d", d=64),
                AX.X, ALU.add, ["tmpb"], ["sm"])
            rstd(sm[0:P, 8 + 2 * which:10 + 2 * which], sm[0:P, 8 + 2 * which:10 + 2 * which], 64.0, ["sm"])
            tt("dve", k3, k3, sm[0:P, 8 + 2 * which:10 + 2 * which].unsqueeze(2).to_broadcast([P, 2, 64]), ALU.mult, [rn, "sm"], [rn])
            tt("dve", k3, k3, gk[0:P, gi_, :].unsqueeze(1).to_broadcast([P, 2, 64]), ALU.mult, [rn, "gsm"], [rn])
            cp("pool", knb[0:P, which * 128:(which + 1) * 128], proj[0:P, c0:c0 + 128], [rn], ["knb"])
        for j in range(4):
            tr(pT[0:64, j * 128:j * 128 + P], knb[0:P, j * 64:(j + 1) * 64], identb[0:P, 0:P], ["knb", "identb"], ["pT"])
        for which in range(2):
            for g in range(2):
                j = which * 2 + g
                cp("act", KnT[0:64, which, g, :], pT[0:64, j * 128:j * 128 + 8], ["pT"], ["KnT"])
        cp("pool", Vn[:, 0, :, 0:64], proj[0:P, C_VS:C_VS + 128].rearrange("p (h d) -> p h d", d=64), ["pj_kv"], ["Vn"])
        cp("pool", Vn[:, 1, :, 0:64], proj[0:P, C_VW:C_VW + 128].rearrange("p (h d) -> p h d", d=64), ["pj_w"], ["Vn"])
        kb.dma(kv_s[l, bl], proj[0:P, C_KC:C_KC + 512], ["pj_kv"], [("kv_s", l, bl)])
        kb.dma(win_s[l, bl, 504:512, :], proj[0:P, C_KW:C_KW + 256], ["pj_w"], [("win_s", l, bl)])
        kb.dma(win_s[l, bl, 0:504, :], cw_in[l, bl, 8:512, :], [], [("win_s2", l, bl)])
        act(gates[0:P, :], proj[0:P, C_GL:C_GL + 24], AF.Sigmoid, ["pj_w"], ["gates"])

        fence()
        for g in range(2):
            for half in range(2):
                for jj in range(32):
                    j = half * 32 + jj
                    k = jj % 4
                    gather_page(l, j, k, False)
                    pv = pg[k].rearrange("p (a h d) -> p a h d", a=2, h=2)[:, :, g, :]
                    tt("dve", kcp[:, 0:128].rearrange("p (a d) -> p a d", a=2), pv, posrep, ALU.add, [pgn[k], "gsm"], ["kcp"])
                    for a in range(2):
                        tr(pT[0:64, a * 128:(a + 1) * 128], kcp[:, a * 64:(a + 1) * 64], identb, ["kcp", "identb"], ["pT"])
                    cp("act", KsT[0:64, :, jj * 128:(jj + 1) * 128], pT[0:64, 0:256].rearrange("d (a t) -> d a t", a=2),
                       ["pT"], [("KsT", jj)])
                allk = [("KsT", jj) for jj in range(32)]
                for kv in range(2):
                    for ll in range(32):
                        lhs = KsT[0:64, kv, :].rearrange("d (c l) -> d c l", l=32)[:, :, ll]
                        mm(pC[:, kv * 64:(kv + 1) * 64], lhs, W_phi[:, kv, ll, :], ll == 0, ll == 31, allk + ["W_phi"], ["pC"])
                kb.op("dve", lambda e: e.memset(st[:, 4:6], 0.0), ["st"], ["st"])
                act(tmpa[:, 0:64], pC[:, 0:64], AF.Square, ["pC", "st"], ["tmpa", "st"], accum_out=st[:, 4:5])
                rstd(st[:, 4:5], st[:, 5:6], 64.0, ["st"])
                ts("dve", tmpa[:, 0:64], pC[:, 0:64], st[:, 5:6], None, ALU.mult, None, ["pC", "st"], ["tmpa"])
                tt("dve", kcp[:, 0:64], tmpa[:, 0:64], gk[:, 0, :], ALU.mult, ["tmpa", "gsm"], ["kcp"])
                tr(pT[0:64, 0:128], kcp[:, 0:64], identb, ["kcp", "identb"], ["pT"])
                cp("act", KcT[0:64, g, half * 128:(half + 1) * 128], pT[0:64, 0:128], ["pT"], ["KcT"])
                cp("act", Vc[:, g, half, :], pC[:, 64:128], ["pC"], ["Vc"])

        fence()
        for g in range(2):
            pbuf = [tmpa, pcm]
            pbn = ["tmpa", "pcm"]
            for hp in range(2):
                bank = pC if hp == 0 else pS[0]
                bn = "pC" if hp == 0 else "pS0"
                for hh in range(2):
                    h = hp * 2 + hh
                    for half in range(2):
                        mm(bank[0:P, (hh * 2 + half) * 128:(hh * 2 + half + 1) * 128],
                           QTv[0:67, 4 * g + h, half * 8:half * 8 + 8], KcT[0:67, g, half * 128:(half + 1) * 128],
                           True, True, ["QT", "KcT"], [bn])
                act(pbuf[hp][0:P, :], bank[0:P, :], AF.Exp, [bn], [pbn[hp]])
                red("dve", sm[0:P, 16 + 2 * hp:18 + 2 * hp], pbuf[hp][0:P, :].rearrange("p (h c) -> p h c", h=2), AX.X, ALU.add,
                    [pbn[hp]], ["sm"])
            ts("dve", sm[0:P, 16:20], sm[0:P, 16:20], 1e-30, None, ALU.max, None, ["sm"], ["sm"])
            kb.op("dve", lambda e: e.reciprocal(out=sm[0:P, 16:20], in_=sm[0:P, 16:20]), ["sm"], ["sm"])
            for hp in range(2):
                pv3 = pbuf[hp][0:P, :].rearrange("p (h c) -> p h c", h=2)
                tt("dve", pv3, pv3, sm[0:P, 16 + 2 * hp:18 + 2 * hp].unsqueeze(2).to_broadcast([P, 2, 256]), ALU.mult,
                   [pbn[hp], "sm"], [pbn[hp]])
                dst = scS if hp == 0 else scS2
                red("dve", dst[:, 0:128], pbuf[hp][0:P, :].rearrange("p (h b t) -> p b h t", h=2, t=2), AX.XY, ALU.add,
                    [pbn[hp]], ["scS" if hp == 0 else "scS2"])
                cp("pool", omix[0:P, hp * 512:(hp + 1) * 512], pbuf[hp][0:P, :], [pbn[hp]], ["omix"])
            tt("dve", scS[:, 0:128], scS[:, 0:128], scS2[:, 0:128], ALU.add, ["scS", "scS2"], ["scS"])
            tt("dve", scS[:, 0:128], scS[:, 0:128], bonsS[:, 0:128], ALU.add, ["scS", "bonsS"], ["scS"])
            cp("dve", scS[:, 128:136], bonsS[:, 128:136], ["bonsS", "scS"], ["scS"])
            kb.op("dve", lambda e: e.max(out=m8[0:P, 0:8], in_=scS), ["scS"], ["m8"])
            kb.op("dve", lambda e: e.match_replace(out=scS2, in_to_replace=m8[0:P, 0:8], in_values=scS, imm_value=-1e30),
                  ["scS", "m8"], ["scS2"])
            kb.op("dve", lambda e: e.max(out=m8[0:P, 8:16], in_=scS2), ["scS2"], ["m8"])
            ts("dve", scS2, scS, m8[0:P, 15:16], None, ALU.is_ge, None, ["scS", "m8"], ["scS2"])
            ts("dve", selbS, scS2[:, 0:128], -NEGB, NEGB, ALU.mult, ALU.add, ["scS2"], ["selbS"])
            for half in range(2):
                tr(pT[0:64, half * 128:half * 128 + P], selbS[:, half * 64:(half + 1) * 64], identb[0:P, 0:P],
                   ["selbS", "identb"], ["pT"])
            for half in range(2):
                cp("act", selTS[:, g, half, :].rearrange("b (h q) -> b h q", h=4),
                   pT[0:64, half * 128:half * 128 + 8].unsqueeze(1).to_broadcast([64, 4, 8]), ["pT"], ["selTS"])
            for hc in range(8):
                tr(pT[:, hc * 128:hc * 128 + P], omix[0:P, hc * 128:(hc + 1) * 128], identb[0:P, 0:P], ["omix", "identb"], ["pT"])
            cp("act", PT[1][:, 0:64].rearrange("c (a q) -> c a q", a=8), pTv[:, :, 0:8], ["pT"], ["PT1"])
            for h in range(4):
                for half in range(2):
                    hc = h * 2 + half
                    mm(pC[0:P, h * 64:(h + 1) * 64], PT[1][:, hc * 8:(hc + 1) * 8], Vc[:, g, half, :],
                       h == 0 and half == 0, half == 1, ["PT1", "Vc"], ["pC"], skip=True)
            oat3 = oat[0:P, g * 256:(g + 1) * 256].rearrange("p (h d) -> p h d", d=64)
            g3 = gates[0:P, g * 12:(g + 1) * 12].rearrange("p (h b) -> p h b", b=3)
            tt("dve", oat3, pC[0:P, 0:256].rearrange("p (h d) -> p h d", d=64), g3[:, :, 0:1].to_broadcast([P, 4, 64]),
               ALU.mult, ["pC", "gates"], ["pj_q"])

        paccs = [pO, pW]
        pnames = ["pO", "pW"]

        rot = [0]

        def key_tile_s(g, klhs, kres, qoff, extra, nk, r):
            bank = pS[g][0:nk, r * 32:(r + 1) * 32]
            bn = ("pSq", g, r)
            Qv = QTv[0:67, 4 * g:4 * g + 4, qoff:qoff + 8]
            mm(bank, klhs, Qv, True, len(extra) == 0, [kres, "KsT", "KwT", "KnT", "QT"], [bn])
            for xi, (el, er, eres) in enumerate(extra):
                mm(bank, el, er, False, xi == len(extra) - 1, list(eres), [bn])
            act(PT[g][0:nk, r * 32:(r + 1) * 32], bank, AF.Exp, [bn], [("PTq", g, r)])

        def key_tile_v(g, vv, vres, first, nk, r):
            for h in range(4):
                mm(paccs[g][0:P, h, :], PT[g][0:nk, r * 32 + h * 8:r * 32 + (h + 1) * 8], vv, first and h == 0, False,
                   [("PTq", g, r), vres, "Vs", "Vw", "Vn"], [pnames[g]], skip=True)

        def key_tile(g, klhs, kres, qoff, extra, vv, vres, first, nk):
            r = rot[0] % 4
            rot[0] += 1
            key_tile_s(g, klhs, kres, qoff, extra, nk, r)
            key_tile_v(g, vv, vres, first, nk, r)

        def combine(br):
            for g in range(2):
                pacc = paccs[g]
                g3 = gates[0:P, g * 12:(g + 1) * 12].rearrange("p (h b) -> p h b", b=3)
                ts("dve", ocf[0:P, 0:4], pacc[0:P, :, 64], 1e-30, None, ALU.max, None, [pnames[g]], ["ocf"])
                kb.op("dve", lambda e: e.reciprocal(out=ocf[0:P, 0:4], in_=ocf[0:P, 0:4]), ["ocf"], ["ocf"])
                tt("dve", ocf[0:P, 4:8], ocf[0:P, 0:4], g3[:, :, 1 + br], ALU.mult, ["ocf", "gates"], ["ocf"])
                tt("dve", tmpa[0:P, 0:256].rearrange("p (h d) -> p h d", d=64), pacc[0:P, :, 0:64],
                   ocf[0:P, 4:8].unsqueeze(2).to_broadcast([P, 4, 64]), ALU.mult, [pnames[g], "ocf"], ["tmpa"])
                tt("pool", oat[0:P, g * 256:(g + 1) * 256], oat[0:P, g * 256:(g + 1) * 256], tmpa[0:P, 0:256], ALU.add,
                   ["pj_q", "tmpa"], ["pj_q"])

        fence()
        for j in range(64):
            k = j % 4
            half = j // 32
            jj = j % 32
            slot = j % 6
            gather_page(l, j, k, True)
            kq = j % 2
            cp("dve", knb[:, kq * 128:(kq + 1) * 128], pg[k][:, 0:128], [pgn[k]], [("knbq", kq)])
            for g in range(2):
                tr(pT[0:64, kq * 256 + g * 128:kq * 256 + (g + 1) * 128], knb[:, kq * 128 + g * 64:kq * 128 + (g + 1) * 64], identb,
                   [("knbq", kq), "identb"], [("pTq", kq)])
            for g in range(2):
                cp("act", KwT[0:64, g, slot * 128:(slot + 1) * 128], pT[0:64, kq * 256 + g * 128:kq * 256 + (g + 1) * 128],
                   [("pTq", kq)], [("KwT", slot)])
                cp("dve", KwT[64:67, g, slot * 128:(slot + 1) * 128], KsT[64:67, g, jj * 128:(jj + 1) * 128], ["KsT"],
                   [("KwT", slot)])
            cp("dve", Vw[:, slot, :, 0:64], pg[k][:, 128:256].rearrange("p (h d) -> p h d", d=64), [pgn[k]], [("Vw", slot)])
            for g in range(2):
                key_tile_s(g, KwT[0:67, g, slot * 128:(slot + 1) * 128], ("KwT", slot), half * 8,
                           [(stairb[:, jj * 128:(jj + 1) * 128], selTS[:, g, half, :], ("stairb", "selTS"))], 128, j % 4)
            for g in range(2):
                key_tile_v(g, Vw[:, slot, g, :], ("Vw", slot), j == 0, 128, j % 4)
        for g in range(2):
            key_tile(g, KnT[0:67, 0, g, :], "KnT", 0, [(identb[0:8, 0:8], cmask8, ("identb", "cmask8"))],
                     Vn[:, 0, g, :], "Vn", False, 8)
        combine(0)
        wm8 = wmaskb.rearrange("k (h q) -> k h q", h=4)[:, :, 0:8]
        for r in range(4):
            k = r % 4
            slot = r
            kb.dma(pg[k], cw_in[l, bl, r * 128:(r + 1) * 128, :], [], [pgn[k]])
            kq = r % 2
            cp("dve", knb[:, kq * 128:(kq + 1) * 128], pg[k][:, 0:128], [pgn[k]], [("knbq", kq)])
            for g in range(2):
                tr(pT[0:64, kq * 256 + g * 128:kq * 256 + (g + 1) * 128], knb[:, kq * 128 + g * 64:kq * 128 + (g + 1) * 64], identb,
                   [("knbq", kq), "identb"], [("pTq", kq)])
            for g in range(2):
                cp("act", KwT[0:64, g, slot * 128:(slot + 1) * 128], pT[0:64, kq * 256 + g * 128:kq * 256 + (g + 1) * 128],
                   [("pTq", kq)], [("KwT", slot)])
                cp("dve", KwT[64:67, g, slot * 128:(slot + 1) * 128], KsT[64:67, g, (28 + r) * 128:(29 + r) * 128], ["KsT"],
                   [("KwT", slot)])
            cp("dve", Vw[:, slot, :, 0:64], pg[k][:, 128:256].rearrange("p (h d) -> p h d", d=64), [pgn[k]], [("Vw", slot)])
            for g in range(2):
                extra = [(identb, wm8, ("identb", "wmaskb"))] if r == 0 else []
                key_tile(g, KwT[0:67, g, slot * 128:(slot + 1) * 128], ("KwT", slot), 8, extra,
                         Vw[:, slot, g, :], ("Vw", slot), r == 0, 128)
        for g in range(2):
            key_tile(g, KnT[0:67, 1, g, :], "KnT", 0, [(identb[0:8, 0:8], cmask8, ("identb", "cmask8"))],
                     Vn[:, 1, g, :], "Vn", False, 8)
        combine(1)

        fence()
        act(tmpa[0:P, :], proj[0:P, C_ZA:C_ZA + 512], AF.Silu, ["pj_za"], ["tmpa"])
        tt("dve", omix[0:P, 0:512], oat[0:P, :], tmpa[0:P, :], ALU.mult, ["pj_q", "tmpa"], ["omix"])
        kb.op("pool", lambda e: e.memset(carry, 0.0), ["carry"], ["carry"])
        kb.dma(carry[126:128, :], sc_in[l, bl], ["carry"], ["carry"])
        tt("pool", u[0:P, :], proj[0:P, C_CG:C_CG + 512], proj[0:P, C_HV:C_HV + 512], ALU.mult, ["pj_cg", "pj_hv"], ["pj_cg"])
        tt("pool", tmpa[0:P, :], u[0:P, :], cw[0:P, 0, :], ALU.mult, ["pj_cg", "gsm"], ["tmpa"])
        tt("pool", pcm[0:P, :], u[0:P, :], cw[0:P, 1, :], ALU.mult, ["pj_cg", "gsm"], ["pcm"])
        tt("pool", tmpb[0:P, :], u[0:P, :], cw[0:P, 2, :], ALU.mult, ["pj_cg", "gsm"], ["tmpb"])
        tt("pool", tmpa[64:128, :], carry[64:128, :], cw[64:128, 0, :], ALU.mult, ["carry", "gsm"], ["tmpa"])
        tt("pool", pcm[64:128, :], carry[64:128, :], cw[64:128, 1, :], ALU.mult, ["carry", "gsm"], ["pcm"])
        mm(pA[0][0:P, :], shm[0:P, 0, 0:P], tmpa[0:P, :], True, False, ["shm", "tmpa"], ["pA0"])
        mm(pA[0][0:P, :], shm[0:P, 1, 0:P], pcm[0:P, :], False, False, ["shm", "pcm"], ["pA0"])
        mm(pA[0][0:P, :], shm[64:128, 2, 0:P], tmpa[64:128, :], False, False, ["shm", "tmpa"], ["pA0"])
        mm(pA[0][0:P, :], shm[64:128, 3, 0:P], pcm[64:128, :], False, True, ["shm", "pcm"], ["pA0"])
        tt("dve", tmpb[0:P, :], pA[0][0:P, :], tmpb[0:P, :], ALU.add, ["pA0", "tmpb"], ["tmpb"])
        tt("dve", tmpb[0:P, :], tmpb[0:P, :], proj[0:P, C_BG:C_BG + 512], ALU.mult, ["tmpb", "pj_bg"], ["tmpb"])
        act(tmpa[0:P, :], proj[0:P, C_ZB:C_ZB + 512], AF.Silu, ["pj_zb"], ["tmpa"])
        tt("dve", omix[0:P, 512:1024], tmpb[0:P, :], tmpa[0:P, :], ALU.mult, ["tmpb", "tmpa"], ["omix"])
        kb.dma(conv_s[l, bl], proj[6:8, C_CG:C_CG + 512], ["pj_cg"], [("conv_s", l, bl)])
        for c in range(8):
            tr(pT[:, c * 128:c * 128 + P], omix[0:P, c * 128:(c + 1) * 128], identb[0:P, 0:P], ["omix", "identb"], ["pT"])
        cp("act", xT, pT, ["pT"], ["xT"])
        for cg in range(2):
            bn = "pA%d" % cg
            for kc in range(8):
                mm(pA[cg][0:P, :], xT[:, kc * 128:kc * 128 + P], W_out[:, kc, cg * 512:(cg + 1) * 512], kc == 0, kc == 7,
                   ["xT", "W_out"], [bn])
            tt("dve", X[0:P, cg * 512:(cg + 1) * 512], pA[cg][0:P, :], X[0:P, cg * 512:(cg + 1) * 512], ALU.add, [bn, xn], [xn])
        kb.op("dve", lambda e: e.memset(st[:, 2:4], 0.0), ["st"], ["st"])
        act(omix[0:P, :], X[0:P, :], AF.Square, [xn, "st"], ["omix", "st"], accum_out=st[0:P, 2:3])
        rstd(st[0:P, 2:3], st[0:P, 3:4], 1024.0, ["st"])
        cp("dve", xb[0:P, :], X[0:P, :], [xn], ["omix"])
        for c in range(8):
            tr(pT[:, c * 128:c * 128 + P], xb[0:P, c * 128:(c + 1) * 128], identb[0:P, 0:P], ["omix", "identb"], ["pT"])
        cp("act", xT, pT, ["pT"], ["xT"])
        cp("pool", pb[0:P, :], pt[0][0:P, :], ["pt0"], ["pb"])
        for c in range(2):
            tr(pT[:, c * 128:c * 128 + P], pb[0:P, c * 128:(c + 1) * 128], identb[0:P, 0:P], ["pb", "identb"], ["pT"])
        cp("act", pTT, pT[:, 0:256], ["pT"], ["pTT"])
        for cg in range(2):
            for kc in range(8):
                mm(pA[0][0:P, :], xT[:, kc * 128:kc * 128 + P], W_pg[:, kc, cg * 512:(cg + 1) * 512], kc == 0, kc == 7,
                   ["xT", "W_pg"], ["pA0"])
            act(sg[0:P, :], pA[0][0:P, :], AF.Sigmoid, ["pA0", "st"], ["tmpa"], scale=st[0:P, 3:4])
            for kc in range(2):
                mm(pA[1][0:P, :], pTT[:, kc * 128:kc * 128 + P], W_ple[:, kc, cg * 512:(cg + 1) * 512], kc == 0, kc == 1,
                   ["pTT", "W_ple"], ["pA1"])
            tt("dve", sg[0:P, :], pA[1][0:P, :], sg[0:P, :], ALU.mult, ["pA1", "tmpa"], ["tmpa"])
            tt("pool", X[0:P, cg * 512:(cg + 1) * 512], X[0:P, cg * 512:(cg + 1) * 512], sg[0:P, :], ALU.add, [xn, "tmpa"], [xn])
        dst = y_s if last_layer else xs_scr
        kb.dma(dst[bl], X[0:P, :], [xn], [("xsd", bl)])

    for l in range(n_layers):
        load_layer(l)
        kb.op("dve", lambda e: e.memset(Vc, 0.0), ["Vc"], ["Vc"])
        kb.op("dve", lambda e: e.memset(KcT[0:64, :, :], 0.0), ["KcT"], ["KcT"])
        for i in range(NT):
            load_tile_inputs(l, i)
            prompt_tile(l, i)
        for bl in range(n_sb):
            sample_tile(l, bl)
    stats = kb.finalize()
    return nc, stats


def _prep_inputs(inp, n_layers, n_tiles, b, sbs=(0, 1, 2, 3)):
    S = n_tiles * 128
    sbs = list(sbs)
    c = _consts(n_tiles)
    f = lambda a: np.ascontiguousarray(np.asarray(a, dtype=np.float32))
    m = {
        "xp": f(inp["x_prompt"][b, :S]),
        "pp": f(inp["p_prompt"][:n_layers, b, :S]),
        "g_norm": f(inp["g_norm"][:n_layers]),
        "w_in": f(inp["w_in"][:n_layers]),
        "g_q": f(inp["g_q"][:n_layers]),
        "g_k": f(inp["g_k"][:n_layers]),
        "cmp_pos": f(inp["cmp_pos"][:n_layers]),
        "w_phi": f(inp["w_phi"][:n_layers]),
        "conv_w": f(inp["conv_w"][:n_layers]),
        "w_out": f(inp["w_out"][:n_layers]),
        "w_ple": f(inp["w_ple"][:n_layers]),
        "w_pg": f(inp["w_pg"][:n_layers]),
        "g_ple": f(inp["g_ple"][:n_layers]),
        "xs": f(inp["x_sample"][sbs]),
        "pps": f(inp["p_sample"][:n_layers][:, sbs]),
        "pool": np.asarray(inp["cache_kv"][:n_layers], dtype=np.float32).reshape(-1, 512),
        "cwin": f(inp["cache_win"][:n_layers][:, sbs]).reshape(n_layers, len(sbs), 512, 256),
        "scin": f(inp["state_conv"][:n_layers][:, sbs]),
        "ptab": np.ascontiguousarray(np.asarray(inp["page_table"], dtype=np.int32)[sbs]),
    }
    for k, v in c.items():
        m["c_" + k] = v
    return m


_PROG = {}


def kernel(**inputs):
    inp = {k: np.asarray(v) for k, v in inputs.items()}
    n_pool = inp["cache_kv"].shape[1]
    key = ("p", n_pool)
    if key not in _PROG:
        _PROG[key] = build_program(DEPTH, 32, n_pool=n_pool, n_sb=4)
    nc, _ = _PROG[key]
    in_maps = [_prep_inputs(inp, DEPTH, 32, c // 2, sbs=range(4 * c, 4 * c + 4)) for c in range(8)]
    res = run_bass_kernel_spmd(nc, in_maps, core_ids=list(range(8)))
    r = res.results
    y_p = np.stack([r[2 * b]["y_p"] for b in range(4)])
    kv_p = np.stack([r[2 * b]["kv_p"] for b in range(4)], 1).reshape(DEPTH, 4, SEQ, 4, 2, 64)
    win_p = np.stack([r[2 * b]["win_p"] for b in range(4)], 1).reshape(DEPTH, 4, 512, 2, 2, 64)
    conv_p = np.stack([r[2 * b]["conv_p"] for b in range(4)], 1)
    y_s = np.concatenate([r[c]["y_s"] for c in range(8)], 0)
    kv_s = np.concatenate([r[c]["kv_s"] for c in range(8)], 1).reshape(DEPTH, 32, 8, 4, 2, 64)
    win_s = np.concatenate([r[c]["win_s"] for c in range(8)], 1).reshape(DEPTH, 32, 512, 2, 2, 64)
    conv_s = np.concatenate([r[c]["conv_s"] for c in range(8)], 1)
    return (y_p, y_s, kv_p, win_p, conv_p, kv_s, win_s, conv_s)
```

```python
import numpy as np
import concourse.bass as bass
import concourse.mybir as mybir
from concourse.bass_utils import run_bass_kernel_spmd

F32 = mybir.dt.float32
BF16 = mybir.dt.bfloat16
I32 = mybir.dt.int32
AF = mybir.ActivationFunctionType
ALU = mybir.AluOpType
AX = mybir.AxisListType

N_DMA_SEMS = 40

D_MODEL = 1024
SEQ = 4096
DEPTH = 4
D_IN = 3864
D_PLE = 256
NEGB = -30000.0
EPS = 1e-6

C_Q, C_KC, C_VC, C_KS, C_VS, C_KW, C_VW, C_GL, C_ZA, C_BG, C_CG, C_HV, C_ZB = (
    0, 512, 640, 768, 896, 1024, 1152, 1280, 1304, 1816, 2328, 2840, 3352)
COL_GROUPS = [(0, 512, "pj_q"), (512, 1024, "pj_kv"), (1024, 1304, "pj_w"), (1304, 1816, "pj_za"),
              (1816, 2328, "pj_bg"), (2328, 2840, "pj_cg"), (2840, 3352, "pj_hv"), (3352, 3864, "pj_zb")]


class _Op:
    __slots__ = ("eng", "fn", "reads", "writes", "dma", "deps", "needs_inc", "cnt", "dsem", "dval")

    def __init__(self, eng, fn, reads, writes, dma):
        self.eng = eng
        self.fn = fn
        self.reads = reads
        self.writes = writes
        self.dma = dma
        self.deps = ()
        self.needs_inc = False
        self.cnt = 0
        self.dsem = -1
        self.dval = 0


class KB:
    def __init__(self, nc):
        self.nc = nc
        self.engs = {"pe": nc.tensor, "act": nc.scalar, "dve": nc.vector, "pool": nc.gpsimd, "sp": nc.sync}
        self.ops = []

    def op(self, eng, fn, reads=(), writes=(), dma=False):
        self.ops.append(_Op(eng, fn, tuple(reads), tuple(writes), dma))

    def dma(self, out, in_, reads, writes, q="sp", **kw):
        self.op(q, lambda e: e.dma_start(out=out, in_=in_, **kw), reads, writes, dma=True)

    def finalize(self):
        nc = self.nc
        ops = self.ops
        last_w = {}
        readers = {}
        for i, o in enumerate(ops):
            deps = set()
            for r in o.reads:
                j = last_w.get(r)
                if j is not None:
                    deps.add(j)
            for w in o.writes:
                j = last_w.get(w)
                if j is not None:
                    deps.add(j)
                for j in readers.get(w, ()):
                    deps.add(j)
            deps.discard(i)
            keep = []
            for j in deps:
                p = ops[j]
                if (not p.dma) and (not o.dma) and p.eng == o.eng:
                    if o.eng == "pe":
                        continue
                    raw = any(last_w.get(r) == j for r in o.reads) or any(last_w.get(w) == j for w in o.writes)
                    if not raw:
                        continue
                keep.append(j)
            o.deps = tuple(keep)
            for j in keep:
                if not ops[j].dma:
                    ops[j].needs_inc = True
            for r in o.reads:
                readers.setdefault(r, []).append(i)
            for w in o.writes:
                last_w[w] = i
                readers[w] = []
        cnt = {k: 0 for k in self.engs}
        ndma = 0
        dvals = [0] * N_DMA_SEMS
        for o in ops:
            if o.dma:
                k = ndma % N_DMA_SEMS
                ndma += 1
                dvals[k] += 16
                o.dsem = k
                o.dval = dvals[k]
            elif o.needs_inc:
                cnt[o.eng] += 1
                o.cnt = cnt[o.eng]
        esem = {k: nc.alloc_semaphore("es_" + k) for k in self.engs}
        dsem = [nc.alloc_semaphore("ds_%d" % k) for k in range(N_DMA_SEMS)]
        wm_e = {k: {k2: 0 for k2 in self.engs} for k in self.engs}
        wm_d = {k: [0] * N_DMA_SEMS for k in self.engs}
        n_wait = 0
        for o in ops:
            E = self.engs[o.eng]
            need_e = {}
            need_d = {}
            for j in o.deps:
                p = ops[j]
                if p.dma:
                    need_d[p.dsem] = max(need_d.get(p.dsem, 0), p.dval)
                else:
                    need_e[p.eng] = max(need_e.get(p.eng, 0), p.cnt)
            if o.dma and o.dval > 16:
                need_d[o.dsem] = max(need_d.get(o.dsem, 0), o.dval - 16)
            for f, c in need_e.items():
                if wm_e[o.eng][f] < c:
                    E.wait_ge(esem[f], c)
                    wm_e[o.eng][f] = c
                    n_wait += 1
            for k, v in need_d.items():
                if wm_d[o.eng][k] < v:
                    E.wait_ge(dsem[k], v)
                    wm_d[o.eng][k] = v
                    n_wait += 1
            ins = o.fn(E)
            if o.dma:
                ins.then_inc(dsem[o.dsem], 16)
            elif o.needs_inc:
                ins.then_inc(esem[o.eng], 1)
        sp = self.engs["sp"]
        for k in range(N_DMA_SEMS):
            if dvals[k] > 0:
                sp.wait_ge(dsem[k], dvals[k])
        self.stats = dict(n_ops=len(ops), n_wait=n_wait, n_dma=ndma, cnt=cnt)
        return self.stats


def _consts(n_tiles):
    c = {}
    c["ident"] = np.eye(128, dtype=np.float32)
    kpos = np.arange(SEQ)
    c["kaug"] = np.stack([128.0 * (kpos // 128), (kpos % 128).astype(np.float64), np.ones(SEQ)]).astype(np.float32)
    ce = np.arange(128) * 32 + 31
    c["kcaug"] = np.stack([128.0 * (ce // 128), (ce % 128).astype(np.float64), np.ones(128)]).astype(np.float32)
    m = 2.0 ** (-(np.arange(8) + 1.0))
    qa = np.zeros((3, 8, 128), np.float32)
    qa[0] = m[:, None]
    qa[1] = m[:, None]
    qa[2] = -64.0 * m[:, None]
    c["qaug"] = np.ascontiguousarray(qa[:, :, 0])
    qm = np.ones((3, 65), np.float32)
    qm[2] = 2 * np.arange(65) + 1
    c["qmul"] = qm
    k = np.arange(128)[:, None]
    t = np.arange(128)[None, :]
    cb = np.where(k <= t, 0.0, NEGB).astype(np.float32)
    wb = np.where(t < k, 0.0, NEGB).astype(np.float32)
    c["cmask"] = np.tile(cb, (1, 4))
    c["wmask"] = np.tile(wb, (1, 4))
    st = np.zeros((64, SEQ), np.float32)
    st[np.arange(SEQ) // 64, np.arange(SEQ)] = 1.0
    c["stair"] = st
    cp = np.arange(252)[None, :] - 124
    tl = np.arange(128)[:, None]
    c["mcmp"] = np.where(32 * cp + 31 <= tl, 0.0, NEGB).astype(np.float32)
    bon = np.zeros((32, 128, 64), np.float32)
    for qb in range(32):
        tt = (qb * 128 + np.arange(128))[:, None]
        bb = np.arange(64)[None, :]
        cur = tt // 64
        valid = bb * 64 <= tt
        forced = (bb == 0) | (bb == cur) | (bb == cur - 1)
        bon[qb] = np.where(valid, np.where(forced, 1e3, 0.0), -1e9)
    c["bonus"] = bon
    sh = np.zeros((4, 128, 128), np.float32)
    for tq in range(128):
        if tq - 2 >= 0:
            sh[0, tq - 2, tq] = 1.0
        if tq - 1 >= 0:
            sh[1, tq - 1, tq] = 1.0
    sh[2, 126, 0] = 1.0
    sh[2, 127, 1] = 1.0
    sh[3, 127, 0] = 1.0
    c["shm"] = sh
    c["iota"] = np.arange(128, dtype=np.float32).reshape(128, 1)
    c["knaug"] = np.stack([np.full(8, 8192.0), np.arange(8.0), np.ones(8)]).astype(np.float32)
    bs = np.full((8, 136), -1e9, np.float32)
    bs[:, 0:129] = 0.0
    bs[:, [0, 127, 128]] = 1e3
    c["bons"] = bs
    kk = np.arange(8)[:, None]
    qq = np.arange(8)[None, :]
    c["cmask8"] = np.tile(np.where(kk <= qq, 0.0, NEGB).astype(np.float32), (1, 4))
    return c


def build_program(n_layers=DEPTH, n_tiles=32, debug=False, n_pool=2560, n_sb=4):
    nc = bass.Bass("TRN2", target_bir_lowering=False)
    kb = KB(nc)
    NT = n_tiles
    S = NT * 128

    def din(name, shape, dt=F32):
        return nc.dram_tensor(name, list(shape), dt, kind="ExternalInput").ap()

    def dout(name, shape, dt=F32):
        return nc.dram_tensor(name, list(shape), dt, kind="ExternalOutput").ap()

    def sb(name, shape, dt=F32):
        return nc.alloc_sbuf_tensor(name, list(shape), dt).ap()

    def ps(name, shape, dt=F32):
        return nc.alloc_psum_tensor(name, list(shape), dt).ap()

    xp = din("xp", [S, D_MODEL])
    pp = din("pp", [n_layers, S, D_PLE])
    g_norm = din("g_norm", [n_layers, D_MODEL])
    w_in = din("w_in", [n_layers, D_MODEL, D_IN])
    g_q = din("g_q", [n_layers, 64])
    g_k = din("g_k", [n_layers, 3, 64])
    cmp_pos = din("cmp_pos", [n_layers, 2, 32, 64])
    w_phi = din("w_phi", [n_layers, 2, 32, 64, 64])
    conv_w = din("conv_w", [n_layers, 3, 512])
    w_out = din("w_out", [n_layers, 1024, 1024])
    w_ple = din("w_ple", [n_layers, 256, 1024])
    w_pg = din("w_pg", [n_layers, 1024, 1024])
    g_ple = din("g_ple", [n_layers, 1024])
    c_ident = din("c_ident", [128, 128])
    c_kaug = din("c_kaug", [3, SEQ])
    c_kcaug = din("c_kcaug", [3, 128])
    c_qaug = din("c_qaug", [3, 8])
    c_qmul = din("c_qmul", [3, 65])
    c_iota = din("c_iota", [128, 1])
    c_knaug = din("c_knaug", [3, 8])
    c_bons = din("c_bons", [8, 136])
    c_cmask8 = din("c_cmask8", [8, 32])
    c_cmask = din("c_cmask", [128, 512])
    c_wmask = din("c_wmask", [128, 512])
    c_stair = din("c_stair", [64, SEQ])
    c_mcmp = din("c_mcmp", [128, 252])
    c_bonus = din("c_bonus", [32, 128, 64])
    c_shm = din("c_shm", [4, 128, 128])

    xs_in = din("xs", [n_sb, 8, D_MODEL])
    pps = din("pps", [n_layers, n_sb, 8, D_PLE])
    pool = din("pool", [n_layers * n_pool * 128, 512])
    cw_in = din("cwin", [n_layers, n_sb, 512, 256])
    sc_in = din("scin", [n_layers, n_sb, 2, 512])
    ptab = din("ptab", [n_sb, 64], I32)
    y_s = dout("y_s", [n_sb, 8, D_MODEL])
    kv_s = dout("kv_s", [n_layers, n_sb, 8, 512])
    win_s = dout("win_s", [n_layers, n_sb, 512, 256])
    conv_s = dout("conv_s", [n_layers, n_sb, 2, 512])
    xs_scr = nc.dram_tensor("xs_scr", [n_sb, 8, D_MODEL], F32, kind="Internal").ap()
    y_p = dout("y_p", [S, D_MODEL])
    kv_p = dout("kv_p", [n_layers, S, 512])
    WROWS = min(512, S)
    win_p = dout("win_p", [n_layers, WROWS, 256])
    conv_p = dout("conv_p", [n_layers, 2, 512])
    xscr = nc.dram_tensor("xscr", [S, D_MODEL], F32, kind="Internal").ap()

    W_in = sb("W_in", [128, 8, D_IN], BF16)
    W_out = sb("W_out", [128, 8, 1024], BF16)
    W_pg = sb("W_pg", [128, 8, 1024], BF16)
    W_ple = sb("W_ple", [128, 2, 1024], BF16)
    W_phi = sb("W_phi", [64, 2, 32, 64], BF16)
    gn = sb("gn", [128, 8])
    gp = sb("gp", [128, 8])
    gq8 = sb("gq8", [128, 64])
    gk = sb("gk", [128, 3, 64])
    cw = sb("cw", [128, 3, 512])
    posrep = sb("posrep", [128, 2, 64])

    identb = sb("identb", [128, 128], BF16)
    shm = sb("shm", [128, 4, 128])
    cmaskb = sb("cmaskb", [128, 512], BF16)
    wmaskb = sb("wmaskb", [128, 512], BF16)
    stairb = sb("stairb", [128, SEQ], BF16)
    mcmp = sb("mcmp", [128, 252])
    qaugf = sb("qaugf", [128, 8])
    qmulf = sb("qmulf", [128, 65])

    KsT = sb("KsT", [128, 2, SEQ], BF16)
    KwT = sb("KwT", [128, 2, 6 * 128], BF16)
    Vs = sb("Vs", [128, 32, 2, 65], BF16)
    Vw = sb("Vw", [128, 6, 2, 65], BF16)
    KcT = sb("KcT", [128, 2, 256], BF16)
    Vc = sb("Vc", [128, 2, 2, 64], BF16)

    xt = [sb("xt0", [128, 1024])] * 2
    pt = [sb("pt0", [128, 256])] * 2
    bon = [sb("bon%d" % j, [128, 64]) for j in range(2)]
    st = sb("st", [128, 8])
    xT = sb("xT", [128, 1024], BF16)
    proj = sb("proj", [128, D_IN])
    tmpa = sb("tmpa", [128, 512])
    tmpb = sb("tmpb", [128, 512])
    stg = [tmpa, tmpb]
    sm = sb("sm", [128, 64])
    qnb = sb("qnb", [128, 512], BF16)
    QT = sb("QT", [128, 1024], BF16)
    knb = sb("knb", [128, 256], BF16)
    gates = sb("gates", [128, 24])
    kcp = sb("kcp", [128, 256], BF16)
    kcT = sb("kcT", [64, 4, 128], BF16)
    cst = sb("cst", [8, 8])
    kcn = sb("kcn", [8, 64])
    kcnb = sb("kcnb", [8, 64], BF16)
    vcnb = sb("vcnb", [8, 64], BF16)
    pcm = sb("pcm", [128, 512])
    score = sb("score", [128, 64])
    score2 = sb("score2", [128, 64])
    m8 = sb("m8", [128, 16])
    selb = sb("selb", [128, 64], BF16)
    selTg = [sb("selT%d" % j, [128, 512], BF16) for j in range(2)]
    PT = [sb("PT%d" % j, [128, 512], BF16) for j in range(2)]
    pnb = PT[0]
    pnT = PT[1]
    junkb = pcm.bitcast(BF16)
    PT2 = [sb("PTx%d" % j, [128, 512], BF16) for j in range(2)]
    iota = sb("iota", [128, 1])
    idxi = sb("idxi", [128, 64], I32)
    idxf = sb("idxf", [128, 64])
    KnT = sb("KnT", [128, 2, 2, 8], BF16)
    Vn = sb("Vn", [8, 2, 2, 65], BF16)
    scS = sb("scS", [8, 136])
    scS2 = sb("scS2", [8, 136])
    selbS = sb("selbS", [8, 128], BF16)
    selTS = sb("selTS", [64, 2, 2, 32], BF16)
    bonsS = sb("bonsS", [8, 136])
    cmask8 = sb("cmask8", [8, 32], BF16)
    oat = proj[:, C_Q:C_Q + 512]
    ocf = sb("ocf", [128, 16])
    omix = sb("omix", [128, 1024], BF16)
    xb = omix
    u = proj[:, C_CG:C_CG + 512]
    v0 = tmpa
    v1 = pcm
    carry = sb("carry", [128, 512])
    sg = tmpa
    pb = sb("pb", [128, 256], BF16)
    pTT = sb("pTT", [128, 256], BF16)

    pA = [ps("pA%d" % j, [128, 512]) for j in range(2)]
    pT = ps("pT", [128, 1024], BF16)
    pS = [ps("pS%d" % j, [128, 512]) for j in range(2)]
    pO = ps("pO", [128, 4, 65])
    pW = ps("pW", [128, 4, 65])
    pC = ps("pC", [128, 512])

    dumped = set()

    def dump(name, ap, res, shape, dt=F32):
        if not debug or name in dumped:
            return
        dumped.add(name)
        d = nc.dram_tensor("dbg_" + name, list(shape), dt, kind="ExternalOutput").ap()
        kb.dma(d, ap, list(res), [("dbg", name)])

    def mm(out, lhsT, rhs, start, stop, R, W, skip=False):
        if skip:
            kb.op("pe", lambda e: e.matmul(out, lhsT=lhsT, rhs=rhs, start=start, stop=stop, skip_group_check=True), R, W)
        else:
            kb.op("pe", lambda e: e.matmul(out, lhsT=lhsT, rhs=rhs, start=start, stop=stop), R, W)

    def tr(out, in_, ident, R, W):
        kb.op("pe", lambda e: e.transpose(out=out, in_=in_, identity=ident), R, W)

    def act(out, in_, func, R, W, **kw):
        kb.op("act", lambda e: e.activation(out=out, in_=in_, func=func, **kw), R, W)

    def cp(eng, out, in_, R, W):
        if eng == "act":
            kb.op("act", lambda e: e.copy(out=out, in_=in_), R, W)
        else:
            kb.op(eng, lambda e: e.tensor_copy(out=out, in_=in_), R, W)

    def tt(eng, out, in0, in1, op, R, W):
        kb.op(eng, lambda e: e.tensor_tensor(out=out, in0=in0, in1=in1, op=op), R, W)

    def ts(eng, out, in0, s1, s2, op0, op1, R, W):
        if op1 is None:
            kb.op(eng, lambda e: e.tensor_scalar(out=out, in0=in0, scalar1=s1, scalar2=None, op0=op0), R, W)
        else:
            kb.op(eng, lambda e: e.tensor_scalar(out=out, in0=in0, scalar1=s1, scalar2=s2, op0=op0, op1=op1), R, W)

    def red(eng, out, in_, axis, op, R, W):
        kb.op(eng, lambda e: e.tensor_reduce(out=out, in_=in_, axis=axis, op=op), R, W)

    def rstd(col_in, col_out, n, R):
        act(col_out, col_in, AF.Sqrt, R, R, scale=1.0 / n, bias=EPS)
        kb.op("dve", lambda e: e.reciprocal(out=col_out, in_=col_out), R, R)

    kb.dma(tmpa[:, 0:128], c_ident, ["tmpa"], ["tmpa"])
    cp("dve", identb, tmpa[:, 0:128], ["tmpa"], ["identb"])
    kb.dma(shm, c_shm.rearrange("k a b -> a k b"), [], ["shm"])
    kb.dma(mcmp, c_mcmp, [], ["mcmp"])
    kb.dma(qaugf[64:67, :], c_qaug, [], ["qaugf"])
    kb.dma(qmulf[64:67, :], c_qmul, [], ["qmulf"])
    kb.dma(stg[0][:, 0:512], c_cmask, ["tmpa"], ["tmpa"])
    cp("dve", cmaskb, stg[0][:, 0:512], ["tmpa"], ["cmaskb"])
    kb.dma(stg[1][:, 0:512], c_wmask, ["tmpb"], ["tmpb"])
    cp("dve", wmaskb, stg[1][:, 0:512], ["tmpb"], ["wmaskb"])
    kb.op("dve", lambda e: e.memset(KsT[64:128, :, :], 0.0), [], ["KsT"])
    kb.op("dve", lambda e: e.memset(KwT[64:128, :, :], 0.0), [], [("KwT", w_) for w_ in range(6)])
    kb.op("dve", lambda e: e.memset(QT[64:128, :], 0.0), [], ["QT"])
    kb.op("dve", lambda e: e.memset(stairb[64:128, :], 0.0), [], ["stairb"])
    for g_ in range(2):
        kb.op("dve", lambda e, g_=g_: e.memset(selTg[g_][64:128, :], 0.0), [], ["selT%d" % g_])
    for j in range(SEQ // 512):
        s_ = stg[j % 2]
        kb.dma(s_[0:64, :], c_stair[:, j * 512:(j + 1) * 512], [("tmpa", "tmpb")[j % 2]], [("tmpa", "tmpb")[j % 2]])
        cp("dve", stairb[0:64, j * 512:(j + 1) * 512], s_[0:64, :], [("tmpa", "tmpb")[j % 2]], ["stairb"])
    for j in range(SEQ // 512):
        s_ = stg[j % 2]
        kb.dma(s_[64:67, :], c_kaug[:, j * 512:(j + 1) * 512], [("tmpa", "tmpb")[j % 2]], [("tmpa", "tmpb")[j % 2]])
        for g in range(2):
            cp("dve", KsT[64:67, g, j * 512:(j + 1) * 512], s_[64:67, :], [("tmpa", "tmpb")[j % 2]], ["KsT"])
    kb.dma(stg[0][64:67, 0:128], c_kcaug, ["tmpa"], ["tmpa"])
    for g in range(2):
        for hf in range(2):
            cp("dve", KcT[64:67, g, hf * 128:(hf + 1) * 128], stg[0][64:67, 0:128], ["tmpa"], ["KcT"])
    kb.dma(iota, c_iota, [], ["iota"])
    kb.dma(bonsS, c_bons, [], ["bonsS"])
    kb.dma(stg[1][0:8, 0:32], c_cmask8, ["tmpb"], ["tmpb"])
    cp("dve", cmask8, stg[1][0:8, 0:32], ["tmpb"], ["cmask8"])
    kb.dma(stg[1][64:67, 0:8], c_knaug, ["tmpb"], ["tmpb"])
    for a_ in range(2):
        for g in range(2):
            cp("dve", KnT[64:67, a_, g, :], stg[1][64:67, 0:8], ["tmpb"], ["KnT"])
    kb.op("dve", lambda e: e.memset(Vn[:, :, :, 64:65], 1.0), [], ["Vn"])
    kb.op("dve", lambda e: e.memset(Vs[:, :, :, 64:65], 1.0), [], ["Vs"])
    kb.op("dve", lambda e: e.memset(Vw[:, :, :, 64:65], 1.0), [], ["Vw"])

    stg_ctr = [0]

    def load_cast(dst_ap_fn, src_ap_fn, nchunks, scale_col_fn, wname, parts=128):
        for j in range(nchunks):
            k = stg_ctr[0] % 2
            stg_ctr[0] += 1
            sname = ("tmpa", "tmpb")[k]
            src, width = src_ap_fn(j)
            kb.dma(stg[k][0:parts, 0:width], src, [sname], [sname])
            dst = dst_ap_fn(j)
            sc = scale_col_fn(j) if scale_col_fn is not None else None
            eng = "dve" if j % 2 == 0 else "pool"
            if sc is None:
                cp(eng, dst, stg[k][0:parts, 0:width], [sname], [wname])
            else:
                ts(eng, dst, stg[k][0:parts, 0:width], sc, None, ALU.mult, None, [sname, "gvec"], [wname])

    def load_layer(l):
        kb.dma(gn, g_norm[l].rearrange("(c p) -> p c", p=128), [], ["gvec"], allow_slow_non_contiguous=True)
        kb.dma(gp, g_ple[l].rearrange("(c p) -> p c", p=128), [], ["gvec"], allow_slow_non_contiguous=True)
        kb.dma(gq8, g_q[l].partition_broadcast(128), [], ["gsm"])
        kb.dma(gk, g_k[l].partition_broadcast(128), [], ["gsm"])
        kb.dma(cw, conv_w[l].partition_broadcast(128), [], ["gsm"])
        for c4 in range(4):
            kb.dma(posrep[c4 * 32:(c4 + 1) * 32, :, :], cmp_pos[l].rearrange("kv l d -> l kv d"), [], ["gsm"])
        ts("dve", gq8, gq8, 0.125, None, ALU.mult, None, ["gsm"], ["gsm"])
        wcols = [(j * 512, min(512, D_IN - j * 512)) for j in range(8)]
        for kc in range(8):
            load_cast(lambda j, kc=kc: W_in[:, kc, wcols[j][0]:wcols[j][0] + wcols[j][1]],
                      lambda j, kc=kc: (w_in[l, kc * 128:(kc + 1) * 128, wcols[j][0]:wcols[j][0] + wcols[j][1]], wcols[j][1]),
                      8, lambda j, kc=kc: gn[:, kc:kc + 1], "W_in")
        for kc in range(8):
            load_cast(lambda j, kc=kc: W_out[:, kc, j * 512:(j + 1) * 512],
                      lambda j, kc=kc: (w_out[l, kc * 128:(kc + 1) * 128, j * 512:(j + 1) * 512], 512), 2, None, "W_out")
        for kc in range(8):
            load_cast(lambda j, kc=kc: W_pg[:, kc, j * 512:(j + 1) * 512],
                      lambda j, kc=kc: (w_pg[l, kc * 128:(kc + 1) * 128, j * 512:(j + 1) * 512], 512), 2,
                      lambda j, kc=kc: gp[:, kc:kc + 1], "W_pg")
        for kc in range(2):
            load_cast(lambda j, kc=kc: W_ple[:, kc, j * 512:(j + 1) * 512],
                      lambda j, kc=kc: (w_ple[l, kc * 128:(kc + 1) * 128, j * 512:(j + 1) * 512], 512), 2, None, "W_ple")
        for kv in range(2):
            for hh in range(4):
                k = stg_ctr[0] % 2
                stg_ctr[0] += 1
                sname = ("tmpa", "tmpb")[k]
                kb.dma(stg[k][0:64, :].rearrange("d (l e) -> d l e", e=64),
                       w_phi[l, kv, hh * 8:(hh + 1) * 8].rearrange("l d e -> d l e"), [sname], [sname])
                cp("dve", W_phi[:, kv, hh * 8:(hh + 1) * 8, :],
                   stg[k][0:64, :].rearrange("d (l e) -> d l e", e=64), [sname], ["W_phi"])

    def load_tile_inputs(l, i):
        s = i % 2
        src = xp if l == 0 else xscr
        kb.dma(xt[s], src[i * 128:(i + 1) * 128, :], [("xd", i)], ["xt0"])
        kb.dma(pt[s], pp[l, i * 128:(i + 1) * 128, :], [], ["pt0"])
        kb.dma(bon[s], c_bonus[i], [], ["bon%d" % s])

    def prompt_tile(l, i):
        s = i % 2
        X = xt[s]
        xn = "xt0"
        last_layer = (l == n_layers - 1)
        cp("dve", xb, X, [xn], ["omix"])
        for c in range(8):
            tr(pT[:, c * 128:(c + 1) * 128], xb[:, c * 128:(c + 1) * 128], identb, ["omix", "identb"], ["pT"])
        cp("act", xT, pT, ["pT"], ["xT"])
        kb.op("dve", lambda e: e.memset(st, 0.0), ["st"], ["st"])
        act(junkb, X, AF.Square, [xn, "st"], ["pcm", "st"], accum_out=st[:, 0:1])
        rstd(st[:, 0:1], st[:, 1:2], 1024.0, ["st"])
        dump("xT", xT, ["xT"], [128, 1024], BF16)
        dump("xb", xb, ["omix"], [128, 1024], BF16)
        dump("W_in0", W_in[:, 0, 0:512], ["W_in"], [128, 512], BF16)
        dump("st", st, ["st"], [128, 8])
        for gi, (c0, c1, rn) in enumerate(COL_GROUPS):
            w = c1 - c0
            bank = pA[gi % 2]
            bn = "pA%d" % (gi % 2)
            for kc in range(8):
                mm(bank[:, 0:w], xT[:, kc * 128:(kc + 1) * 128], W_in[:, kc, c0:c1], kc == 0, kc == 7,
                   ["xT", "W_in"], [bn])
            act(proj[:, c0:c1], bank[:, 0:w], AF.Copy, [bn, "st"], [rn], scale=st[:, 1:2])
        dump("proj", proj, [r for _, _, r in COL_GROUPS], [128, D_IN])
        tt("dve", kcp.rearrange("p (a h d) -> p a h d", a=2, h=2), proj[:, C_KC:C_KC + 256].rearrange("p (a h d) -> p a h d", a=2, h=2),
           posrep.unsqueeze(2).to_broadcast([128, 2, 2, 64]), ALU.add, ["pj_kv", "gsm"], ["kcp"])
        for j in range(4):
            tr(pT[0:64, j * 128:(j + 1) * 128], kcp[:, j * 64:(j + 1) * 64], identb, ["kcp", "identb"], ["pT"])
        cp("act", kcT.rearrange("d a t -> d (a t)"), pT[0:64, 0:512], ["pT"], ["kcT"])
        for kv in range(2):
            for ll in range(32):
                lhs = kcT[:, 2 * kv:2 * kv + 2, :].rearrange("d h (c l) -> d h c l", l=32)[:, :, :, ll]
                mm(pC[0:8, kv * 64:(kv + 1) * 64], lhs, W_phi[:, kv, ll, :], ll == 0, ll == 31, ["kcT", "W_phi"], ["pC"])
        q3 = proj[:, C_Q:C_Q + 512].rearrange("p (h d) -> p h d", d=64)
        tt("dve", tmpa, proj[:, C_Q:C_Q + 512], proj[:, C_Q:C_Q + 512], ALU.mult, ["pj_q"], ["tmpa"])
        red("dve", sm[:, 0:8], tmpa.rearrange("p (h d) -> p h d", d=64), AX.X, ALU.add, ["tmpa"], ["sm"])
        rstd(sm[:, 0:8], sm[:, 0:8], 64.0, ["sm"])
        tt("dve", tmpa.rearrange("p (h d) -> p h d", d=64), q3, sm[:, 0:8].unsqueeze(2).to_broadcast([128, 8, 64]),
           ALU.mult, ["pj_q", "sm"], ["tmpa"])
        tt("pool", qnb.rearrange("p (h d) -> p h d", d=64), tmpa.rearrange("p (h d) -> p h d", d=64),
           gq8.unsqueeze(1).to_broadcast([128, 8, 64]), ALU.mult, ["tmpa", "gsm"], ["qnb"])
        for h in range(8):
            tr(pT[0:64, h * 128:(h + 1) * 128], qnb[:, h * 64:(h + 1) * 64], identb, ["qnb", "identb"], ["pT"])
        cp("act", QT[0:64, :], pT[0:64, :], ["pT"], ["QT"])
        ts("dve", QT[64:67, :].rearrange("p (h q) -> p h q", h=8), qaugf[64:67, :].unsqueeze(2).to_broadcast([3, 8, 128]),
           qmulf[64:67, i:i + 1], None, ALU.mult, None, ["qaugf", "qmulf"], ["QT"])
        for which, (c0, gi_) in enumerate(((C_KS, 1), (C_KW, 2))):
            rn = "pj_kv" if which == 0 else "pj_w"
            k3 = proj[:, c0:c0 + 128].rearrange("p (h d) -> p h d", d=64)
            tt("dve", tmpb[:, 0:128], proj[:, c0:c0 + 128], proj[:, c0:c0 + 128], ALU.mult, [rn], ["tmpb"])
            red("dve", sm[:, 8 + 2 * which:10 + 2 * which], tmpb[:, 0:128].rearrange("p (h d) -> p h d", d=64),
                AX.X, ALU.add, ["tmpb"], ["sm"])
            rstd(sm[:, 8 + 2 * which:10 + 2 * which], sm[:, 8 + 2 * which:10 + 2 * which], 64.0, ["sm"])
            tt("dve", k3, k3, sm[:, 8 + 2 * which:10 + 2 * which].unsqueeze(2).to_broadcast([128, 2, 64]),
               ALU.mult, [rn, "sm"], [rn])
            tt("dve", k3, k3, gk[:, gi_, :].unsqueeze(1).to_broadcast([128, 2, 64]), ALU.mult, [rn, "gsm"], [rn])
            cp("pool", knb[:, which * 128:(which + 1) * 128], proj[:, c0:c0 + 128], [rn], ["knb"])
        for j in range(4):
            tr(pT[0:64, j * 128:(j + 1) * 128], knb[:, j * 64:(j + 1) * 64], identb, ["knb", "identb"], ["pT"])
        ws = i % 6
        for g in range(2):
            cp("act", KsT[0:64, g, i * 128:(i + 1) * 128], pT[0:64, g * 128:(g + 1) * 128], ["pT"], [("KsT", i)])
            cp("act", KwT[0:64, g, ws * 128:(ws + 1) * 128], pT[0:64, (2 + g) * 128:(3 + g) * 128], ["pT"], [("KwT", ws)])
            cp("dve", KwT[64:67, g, ws * 128:(ws + 1) * 128], KsT[64:67, g, i * 128:(i + 1) * 128], ["KsT"], [("KwT", ws)])
        cp("pool", Vs[:, i, :, 0:64], proj[:, C_VS:C_VS + 128].rearrange("p (h d) -> p h d", d=64), ["pj_kv"], [("Vs", i)])
        cp("pool", Vw[:, ws, :, 0:64], proj[:, C_VW:C_VW + 128].rearrange("p (h d) -> p h d", d=64), ["pj_w"], [("Vw", ws)])
        kb.dma(kv_p[l, i * 128:(i + 1) * 128, :], proj[:, C_KC:C_KC + 512], ["pj_kv"], [("kv_p", l, i)])
        r0 = i * 128 - (S - WROWS)
        if r0 >= 0:
            kb.dma(win_p[l, r0:r0 + 128, :], proj[:, C_KW:C_KW + 256], ["pj_w"], [("win_p", l, i)])
        act(gates, proj[:, C_GL:C_GL + 24], AF.Sigmoid, ["pj_w"], ["gates"])
        kb.op("dve", lambda e: e.memset(cst, 0.0), ["cst"], ["cst"])
        act(kcn, pC[0:8, 0:64], AF.Square, ["pC", "cst"], ["kcn", "cst"], accum_out=cst[:, 0:1])
        rstd(cst[:, 0:1], cst[:, 1:2], 64.0, ["cst"])
        ts("dve", kcn, pC[0:8, 0:64], cst[:, 1:2], None, ALU.mult, None, ["pC", "cst"], ["kcn"])
        tt("dve", kcnb, kcn, gk[0:8, 0, :], ALU.mult, ["kcn", "gsm"], ["kcnb"])
        cp("act", vcnb, pC[0:8, 64:128], ["pC"], ["vcnb"])
        tr(pT[0:64, 0:8], kcnb, identb[0:8, 0:8], ["kcnb", "identb"], ["pT"])
        cp("act", KcT[0:64, :, 4 * i:4 * i + 4], pT[0:64, 0:8].rearrange("d (h c) -> d h c", h=2), ["pT"], ["KcT"])
        for g in range(2):
            kb.dma(Vc[4 * i:4 * i + 4, g, 0, :], vcnb[4 * g:4 * g + 4, :], ["vcnb", "Vc"], ["Vc"])

        def cmp_scores(g):
            for h in range(4):
                mm(pC[:, h * 128:(h + 1) * 128], QT[0:67, (4 * g + h) * 128:(4 * g + h + 1) * 128], KcT[0:67, g, 0:128],
                   True, True, ["QT", "KcT"], ["pC"])

        def cmp_chain(g):
            off = 124 - 4 * i
            tt("dve", tmpa.rearrange("p (h c) -> p h c", h=4), pC.rearrange("p (h c) -> p h c", h=4),
               mcmp[:, off:off + 128].unsqueeze(1).to_broadcast([128, 4, 128]), ALU.add, ["pC", "mcmp"], ["tmpa"])
            act(pcm, tmpa, AF.Exp, ["tmpa"], ["pcm"])
            red("dve", sm[:, 16:20], pcm.rearrange("p (h c) -> p h c", h=4), AX.X, ALU.add, ["pcm"], ["sm"])
            ts("dve", sm[:, 16:20], sm[:, 16:20], 1e-30, None, ALU.max, None, ["sm"], ["sm"])
            kb.op("dve", lambda e: e.reciprocal(out=sm[:, 16:20], in_=sm[:, 16:20]), ["sm"], ["sm"])
            tt("dve", pcm.rearrange("p (h c) -> p h c", h=4), pcm.rearrange("p (h c) -> p h c", h=4),
               sm[:, 16:20].unsqueeze(2).to_broadcast([128, 4, 128]), ALU.mult, ["pcm", "sm"], ["pcm"])
            cp("pool", pnb, pcm, ["pcm"], ["PT0"])
            for h in range(4):
                tr(pT[:, (4 + h) * 128:(5 + h) * 128], pnb[:, h * 128:(h + 1) * 128], identb, ["PT0", "identb"], ["pT"])
            cp("act", pnT, pT[:, 512:1024], ["pT"], ["PT1"])
            if i < 8:
                return
            red("dve", score, pcm.rearrange("p (h b t) -> p b h t", h=4, t=2), AX.XY, ALU.add, ["pcm"], ["score"])
            tt("dve", score, score, bon[s], ALU.add, ["score", "bon%d" % s], ["score"])
            kb.op("dve", lambda e: e.max(out=m8[:, 0:8], in_=score), ["score"], ["m8"])
            kb.op("dve", lambda e: e.match_replace(out=score2, in_to_replace=m8[:, 0:8], in_values=score, imm_value=-1e30),
                  ["score", "m8"], ["score2"])
            kb.op("dve", lambda e: e.max(out=m8[:, 8:16], in_=score2), ["score2"], ["m8"])
            ts("dve", score2, score, m8[:, 15:16], None, ALU.is_ge, None, ["score", "m8"], ["score2"])
            ts("dve", selb, score2, -NEGB, NEGB, ALU.mult, ALU.add, ["score2"], ["selb"])
            tr(pT[0:64, 0:128], selb, identb, ["selb", "identb"], ["pT"])
            cp("act", selTg[g][0:64, :].rearrange("b (h q) -> b h q", h=4), pT[0:64, 0:128].unsqueeze(1).to_broadcast([64, 4, 128]),
               ["pT"], ["selT%d" % g])

        def o_cmp(g):
            for h in range(4):
                mm(pC[:, h * 64:(h + 1) * 64], pnT[:, h * 128:(h + 1) * 128], Vc[:, g, 0, :], True, True, ["PT1", "Vc"], ["pC"])
            oat3 = oat[:, g * 256:(g + 1) * 256].rearrange("p (h d) -> p h d", d=64)
            g3 = gates[:, g * 12:(g + 1) * 12].rearrange("p (h b) -> p h b", b=3)
            tt("dve", oat3, pC[:, 0:256].rearrange("p (h d) -> p h d", d=64),
               g3[:, :, 0:1].to_broadcast([128, 4, 64]), ALU.mult, ["pC", "gates"], ["pj_q"])

        def branch(g, br, pacc, pname):
            Qg = QT[0:128, g * 512:(g + 1) * 512]
            kts = list(range(0, i + 1)) if br == 0 else list(range(max(0, i - 4), i + 1))
            pend = None
            for idx, kt in enumerate(kts):
                b2 = idx % 2
                bank = pS[b2]
                bname = "pS%d" % b2
                if br == 0:
                    klhs = KsT[0:128, g, kt * 128:(kt + 1) * 128]
                    kres = ("KsT", kt)
                    vv = Vs[:, kt, g, :]
                    vres = ("Vs", kt)
                else:
                    wsl = kt % 6
                    klhs = KwT[0:128, g, wsl * 128:(wsl + 1) * 128]
                    kres = ("KwT", wsl)
                    vv = Vw[:, wsl, g, :]
                    vres = ("Vw", wsl)
                diag = (kt == i)
                wfirst = (br == 1 and kt == i - 4)
                use_sel = (br == 0 and i >= 8)
                nextra = (1 if use_sel else 0) + (1 if diag else 0) + (1 if wfirst else 0)
                mm(bank, klhs, Qg, True, nextra == 0, [kres, "KsT", "KwT", "QT"], [bname])
                if use_sel:
                    nextra -= 1
                    mm(bank, stairb[:, kt * 128:(kt + 1) * 128], selTg[g], False, nextra == 0, ["stairb", "selT%d" % g], [bname])
                if diag:
                    nextra -= 1
                    mm(bank, identb, cmaskb, False, nextra == 0, ["identb", "cmaskb"], [bname])
                if wfirst:
                    nextra -= 1
                    mm(bank, identb, wmaskb, False, nextra == 0, ["identb", "wmaskb"], [bname])
                act(PT2[b2], bank, AF.Exp, [bname], ["PTx%d" % b2])
                if pend is not None:
                    pend()

                def pv(b2=b2, vv=vv, vres=vres, first=(idx == 0), lastk=(idx == len(kts) - 1)):
                    for h in range(4):
                        mm(pacc[:, h, :], PT2[b2][:, h * 128:(h + 1) * 128], vv, first and h == 0, lastk,
                           ["PTx%d" % b2, vres, "Vs", "Vw"], [pname], skip=True)
                pend = pv
            pend()

        def evac(g, br, pacc, pname):
            g3 = gates[:, g * 12:(g + 1) * 12].rearrange("p (h b) -> p h b", b=3)
            ts("dve", ocf[:, 0:4], pacc[:, :, 64], 1e-30, None, ALU.max, None, [pname], ["ocf"])
            kb.op("dve", lambda e: e.reciprocal(out=ocf[:, 0:4], in_=ocf[:, 0:4]), ["ocf"], ["ocf"])
            tt("dve", ocf[:, 4:8], ocf[:, 0:4], g3[:, :, 1 + br], ALU.mult, ["ocf", "gates"], ["ocf"])
            tt("dve", tmpb[:, 0:256].rearrange("p (h d) -> p h d", d=64), pacc[:, :, 0:64],
               ocf[:, 4:8].unsqueeze(2).to_broadcast([128, 4, 64]), ALU.mult, [pname, "ocf"], ["tmpb"])
            tt("pool", oat[:, g * 256:(g + 1) * 256], oat[:, g * 256:(g + 1) * 256], tmpb[:, 0:256], ALU.add,
               ["pj_q", "tmpb"], ["pj_q"])

        for g in range(2):
            cmp_scores(g)
            branch(g, 1, pW, "pW")
            cmp_chain(g)
            o_cmp(g)
            evac(g, 1, pW, "pW")
        branch(0, 0, pO, "pO")
        branch(1, 0, pW, "pW")
        evac(0, 0, pO, "pO")
        evac(1, 0, pW, "pW")
        act(tmpa, proj[:, C_ZA:C_ZA + 512], AF.Silu, ["pj_za"], ["tmpa"])
        tt("dve", omix[:, 0:512], oat, tmpa, ALU.mult, ["pj_q", "tmpa"], ["omix"])
        tt("pool", u, proj[:, C_CG:C_CG + 512], proj[:, C_HV:C_HV + 512], ALU.mult, ["pj_cg", "pj_hv"], ["pj_cg"])
        tt("pool", v0, u, cw[:, 0, :], ALU.mult, ["pj_cg", "gsm"], ["tmpa"])
        tt("pool", v1, u, cw[:, 1, :], ALU.mult, ["pj_cg", "gsm"], ["pcm"])
        tt("pool", tmpb, u, cw[:, 2, :], ALU.mult, ["pj_cg", "gsm"], ["tmpb"])
        mm(pA[0], shm[:, 0, :], v0, True, False, ["shm", "tmpa"], ["pA0"])
        mm(pA[0], shm[:, 1, :], v1, False, True, ["shm", "pcm"], ["pA0"])
        tt("dve", tmpb, pA[0], tmpb, ALU.add, ["pA0", "tmpb"], ["tmpb"])
        if i > 0:
            tt("dve", tmpb, tmpb, carry, ALU.add, ["tmpb", "carry"], ["tmpb"])
        if i < NT - 1:
            mm(pA[1], shm[:, 2, :], v0, True, False, ["shm", "tmpa"], ["pA1"])
            mm(pA[1], shm[:, 3, :], v1, False, True, ["shm", "pcm"], ["pA1"])
            cp("act", carry, pA[1], ["pA1"], ["carry"])
        tt("dve", tmpb, tmpb, proj[:, C_BG:C_BG + 512], ALU.mult, ["tmpb", "pj_bg"], ["tmpb"])
        act(tmpa, proj[:, C_ZB:C_ZB + 512], AF.Silu, ["pj_zb"], ["tmpa"])
        tt("dve", omix[:, 512:1024], tmpb, tmpa, ALU.mult, ["tmpb", "tmpa"], ["omix"])
        if i == NT - 1:
            kb.dma(conv_p[l], u[126:128, :], ["pj_cg"], [("conv_p", l)])
        for c in range(8):
            tr(pT[:, c * 128:(c + 1) * 128], omix[:, c * 128:(c + 1) * 128], identb, ["omix", "identb"], ["pT"])
        cp("act", xT, pT, ["pT"], ["xT"])
        for cg in range(2):
            bn = "pA%d" % cg
            for kc in range(8):
                mm(pA[cg], xT[:, kc * 128:(kc + 1) * 128], W_out[:, kc, cg * 512:(cg + 1) * 512], kc == 0, kc == 7,
                   ["xT", "W_out"], [bn])
            tt("dve", X[:, cg * 512:(cg + 1) * 512], pA[cg], X[:, cg * 512:(cg + 1) * 512], ALU.add, [bn, xn], [xn])
        cp("dve", xb, X, [xn], ["omix"])
        for c in range(8):
            tr(pT[:, c * 128:(c + 1) * 128], xb[:, c * 128:(c + 1) * 128], identb, ["omix", "identb"], ["pT"])
        cp("act", xT, pT, ["pT"], ["xT"])
        kb.op("dve", lambda e: e.memset(st[:, 2:4], 0.0), ["st"], ["st"])
        act(junkb, X, AF.Square, [xn, "st"], ["pcm", "st"], accum_out=st[:, 2:3])
        rstd(st[:, 2:3], st[:, 3:4], 1024.0, ["st"])
        cp("pool", pb, pt[s], ["pt0"], ["pb"])
        for c in range(2):
            tr(pT[:, c * 128:(c + 1) * 128], pb[:, c * 128:(c + 1) * 128], identb, ["pb", "identb"], ["pT"])
        cp("act", pTT, pT[:, 0:256], ["pT"], ["pTT"])
        for cg in range(2):
            for kc in range(8):
                mm(pA[0], xT[:, kc * 128:(kc + 1) * 128], W_pg[:, kc, cg * 512:(cg + 1) * 512], kc == 0, kc == 7,
                   ["xT", "W_pg"], ["pA0"])
            act(sg, pA[0], AF.Sigmoid, ["pA0", "st"], ["tmpa"], scale=st[:, 3:4])
            for kc in range(2):
                mm(pA[1], pTT[:, kc * 128:(kc + 1) * 128], W_ple[:, kc, cg * 512:(cg + 1) * 512], kc == 0, kc == 1,
                   ["pTT", "W_ple"], ["pA1"])
            tt("dve", sg, pA[1], sg, ALU.mult, ["pA1", "tmpa"], ["tmpa"])
            tt("pool", X[:, cg * 512:(cg + 1) * 512], X[:, cg * 512:(cg + 1) * 512], sg, ALU.add, [xn, "tmpa"], [xn])
        dst = y_p if last_layer else xscr
        kb.dma(dst[i * 128:(i + 1) * 128, :], X, [xn], [("xd", i)])

    pg = [tmpb[:, 0:256], tmpb[:, 256:512], pcm[:, 0:256], pcm[:, 256:512]]
    pgn = ["pgq0", "pgq1", "pgq2", "pgq3"]
    pool2 = pool.rearrange("r (two c) -> (r two) c", two=2)
    idxB = sb("idxB", [128, 64], I32)

    def fence():
        allr = ["tmpb", "pcm", "knb", "pT", "pS0", "pS1", "PT0", "PT1"] + pgn
        allr += [("pSq", g_, r_) for g_ in range(2) for r_ in range(4)] + [("PTq", g_, r_) for g_ in range(2) for r_ in range(4)]
        allr += [("knbq", 0), ("knbq", 1), ("pTq", 0), ("pTq", 1)]
        kb.op("dve", lambda e: e.memset(cst[0:1, 4:5], 0.0), allr, allr)
    QTv = QT.rearrange("p (h q) -> p h q", h=8)
    pTv = pT.rearrange("p (h q) -> p h q", h=8)

    def gather_page(l, j, k, second):
        buf = pg[k]
        src = pool2
        off = (idxB if second else idxi)[:, j:j + 1]
        kb.op("pool", lambda e: e.indirect_dma_start(out=buf, out_offset=None, in_=src,
                                                     in_offset=bass.IndirectOffsetOnAxis(ap=off, axis=0)),
              ["idxi"], [pgn[k]], dma=True)

    def sample_tile(l, bl):
        P = 8
        X = xt[0]
        xn = "xt0"
        last_layer = (l == n_layers - 1)
        src = xs_in if l == 0 else xs_scr
        kb.dma(X[0:P, :], src[bl], [("xsd", bl)], [xn])
        kb.dma(pt[0][0:P, :], pps[l, bl], [], ["pt0"])
        kb.dma(idxi, ptab[bl].partition_broadcast(128), [], ["idxi"])
        cp("dve", idxf, idxi, ["idxi"], ["idxf"])
        ts("dve", idxf, idxf, 128.0, iota[:, 0:1], ALU.mult, ALU.add, ["idxf", "iota"], ["idxf"])
        ts("dve", idxf, idxf, float(l * n_pool * 128), 2.0, ALU.add, ALU.mult, ["idxf"], ["idxf"])
        cp("dve", idxi, idxf, ["idxf"], ["idxi"])
        ts("dve", idxf, idxf, 1.0, None, ALU.add, None, ["idxf"], ["idxf"])
        cp("dve", idxB, idxf, ["idxf"], ["idxi"])
        kb.op("dve", lambda e: e.memset(st, 0.0), ["st"], ["st"])
        act(omix[0:P, :], X[0:P, :], AF.Square, [xn, "st"], ["omix", "st"], accum_out=st[0:P, 0:1])
        rstd(st[0:P, 0:1], st[0:P, 1:2], 1024.0, ["st"])
        cp("dve", xb[0:P, :], X[0:P, :], [xn], ["omix"])
        for c in range(8):
            tr(pT[:, c * 128:c * 128 + P], xb[0:P, c * 128:(c + 1) * 128], identb[0:P, 0:P], ["omix", "identb"], ["pT"])
        cp("act", xT, pT, ["pT"], ["xT"])
        for gi, (c0, c1, rn) in enumerate(COL_GROUPS):
            w = c1 - c0
            bank = pA[gi % 2]
            bn = "pA%d" % (gi % 2)
            for kc in range(8):
                mm(bank[0:P, 0:w], xT[:, kc * 128:kc * 128 + P], W_in[:, kc, c0:c1], kc == 0, kc == 7, ["xT", "W_in"], [bn])
            act(proj[0:P, c0:c1], bank[0:P, 0:w], AF.Copy, [bn, "st"], [rn], scale=st[0:P, 1:2])
        q3 = proj[0:P, C_Q:C_Q + 512].rearrange("p (h d) -> p h d", d=64)
        ta3 = tmpa[0:P, :].rearrange("p (h d) -> p h d", d=64)
        tt("dve", tmpa[0:P, :], proj[0:P, C_Q:C_Q + 512], proj[0:P, C_Q:C_Q + 512], ALU.mult, ["pj_q"], ["tmpa"])
        red("dve", sm[0:P, 0:8], ta3, AX.X, ALU.add, ["tmpa"], ["sm"])
        rstd(sm[0:P, 0:8], sm[0:P, 0:8], 64.0, ["sm"])
        tt("dve", ta3, q3, sm[0:P, 0:8].unsqueeze(2).to_broadcast([P, 8, 64]), ALU.mult, ["pj_q", "sm"], ["tmpa"])
        tt("pool", qnb[0:P, :].rearrange("p (h d) -> p h d", d=64), ta3, gq8[0:P, :].unsqueeze(1).to_broadcast([P, 8, 64]),
           ALU.mult, ["tmpa", "gsm"], ["qnb"])
        for h in range(8):
            tr(pT[0:64, h * 128:h * 128 + P], qnb[0:P, h * 64:(h + 1) * 64], identb[0:P, 0:P], ["qnb", "identb"], ["pT"])
        for vq, qb_ in ((0, 64), (1, 32)):
            cp("act", QTv[0:64, :, vq * 8:vq * 8 + 8], pTv[0:64, :, 0:8], ["pT"], ["QT"])
            ts("dve", QTv[64:67, :, vq * 8:vq * 8 + 8], qaugf[64:67, :].unsqueeze(2).to_broadcast([3, 8, 8]),
               qmulf[64:67, qb_:qb_ + 1], None, ALU.mult, None, ["qaugf", "qmulf"], ["QT"])
        for which, (c0, gi_) in enumerate(((C_KS, 1), (C_KW, 2))):
            rn = "pj_kv" if which == 0 else "pj_w"
            k3 = proj[0:P, c0:c0 + 128].rearrange("p (h d) -> p h d", d=64)
            tt("dve", tmpb[0:P, 0:128], proj[0:P, c0:c0 + 128], proj[0:P, c0:c0 + 128], ALU.mult, [rn], ["tmpb"])
            red("dve", sm[0:P, 8 + 2 * which:10 + 2 * which], tmpb[0:P, 0:128].rearrange("p (h d) -> p h d", d=64),
                AX.X, ALU.add, ["tmpb"], ["sm"])
            rstd(sm[0:P, 8 + 2 * which:10 + 2 * which], sm[0:P, 8 + 2 * which:10 + 2 * which], 64.0, ["sm"])
            tt("dve", k3, k3, sm[0:P, 8 + 2 * which:10 + 2 * which].unsqueeze(2).to_broadcast([P, 2, 64]), ALU.mult, [rn, "sm"], [rn])
            tt("dve", k3, k3, gk[0:P, gi_, :].unsqueeze(1).to_broadcast([P, 2, 64]), ALU.mult, [rn, "gsm"], [rn])
            cp("pool", knb[0:P, which * 128:(which + 1) * 128], proj[0:P, c0:c0 + 128], [rn], ["knb"])
        for j in range(4):
            tr(pT[0:64, j * 128:j * 128 + P], knb[0:P, j * 64:(j + 1) * 64], identb[0:P, 0:P], ["knb", "identb"], ["pT"])
        for which in range(2):
            for g in range(2):
                j = which * 2 + g
                cp("act", KnT[0:64, which, g, :], pT[0:64, j * 128:j * 128 + 8], ["pT"], ["KnT"])
        cp("pool", Vn[:, 0, :, 0:64], proj[0:P, C_VS:C_VS + 128].rearrange("p (h d) -> p h d", d=64), ["pj_kv"], ["Vn"])
        cp("pool", Vn[:, 1, :, 0:64], proj[0:P, C_VW:C_VW + 128].rearrange("p (h d) -> p h d", d=64), ["pj_w"], ["Vn"])
        kb.dma(kv_s[l, bl], proj[0:P, C_KC:C_KC + 512], ["pj_kv"], [("kv_s", l, bl)])
        kb.dma(win_s[l, bl, 504:512, :], proj[0:P, C_KW:C_KW + 256], ["pj_w"], [("win_s", l, bl)])
        kb.dma(win_s[l, bl, 0:504, :], cw_in[l, bl, 8:512, :], [], [("win_s2", l, bl)])
        act(gates[0:P, :], proj[0:P, C_GL:C_GL + 24], AF.Sigmoid, ["pj_w"], ["gates"])

        fence()
        for g in range(2):
            for half in range(2):
                for jj in range(32):
                    j = half * 32 + jj
                    k = jj % 4
                    gather_page(l, j, k, False)
                    pv = pg[k].rearrange("p (a h d) -> p a h d", a=2, h=2)[:, :, g, :]
                    tt("dve", kcp[:, 0:128].rearrange("p (a d) -> p a d", a=2), pv, posrep, ALU.add, [pgn[k], "gsm"], ["kcp"])
                    for a in range(2):
                        tr(pT[0:64, a * 128:(a + 1) * 128], kcp[:, a * 64:(a + 1) * 64], identb, ["kcp", "identb"], ["pT"])
                    cp("act", KsT[0:64, :, jj * 128:(jj + 1) * 128], pT[0:64, 0:256].rearrange("d (a t) -> d a t", a=2),
                       ["pT"], [("KsT", jj)])
                allk = [("KsT", jj) for jj in range(32)]
                for kv in range(2):
                    for ll in range(32):
                        lhs = KsT[0:64, kv, :].rearrange("d (c l) -> d c l", l=32)[:, :, ll]
                        mm(pC[:, kv * 64:(kv + 1) * 64], lhs, W_phi[:, kv, ll, :], ll == 0, ll == 31, allk + ["W_phi"], ["pC"])
                kb.op("dve", lambda e: e.memset(st[:, 4:6], 0.0), ["st"], ["st"])
                act(tmpa[:, 0:64], pC[:, 0:64], AF.Square, ["pC", "st"], ["tmpa", "st"], accum_out=st[:, 4:5])
                rstd(st[:, 4:5], st[:, 5:6], 64.0, ["st"])
                ts("dve", tmpa[:, 0:64], pC[:, 0:64], st[:, 5:6], None, ALU.mult, None, ["pC", "st"], ["tmpa"])
                tt("dve", kcp[:, 0:64], tmpa[:, 0:64], gk[:, 0, :], ALU.mult, ["tmpa", "gsm"], ["kcp"])
                tr(pT[0:64, 0:128], kcp[:, 0:64], identb, ["kcp", "identb"], ["pT"])
                cp("act", KcT[0:64, g, half * 128:(half + 1) * 128], pT[0:64, 0:128], ["pT"], ["KcT"])
                cp("act", Vc[:, g, half, :], pC[:, 64:128], ["pC"], ["Vc"])

        fence()
        for g in range(2):
            pbuf = [tmpa, pcm]
            pbn = ["tmpa", "pcm"]
            for hp in range(2):
                bank = pC if hp == 0 else pS[0]
                bn = "pC" if hp == 0 else "pS0"
                for hh in range(2):
                    h = hp * 2 + hh
                    for half in range(2):
                        mm(bank[0:P, (hh * 2 + half) * 128:(hh * 2 + half + 1) * 128],
                           QTv[0:67, 4 * g + h, half * 8:half * 8 + 8], KcT[0:67, g, half * 128:(half + 1) * 128],
                           True, True, ["QT", "KcT"], [bn])
                act(pbuf[hp][0:P, :], bank[0:P, :], AF.Exp, [bn], [pbn[hp]])
                red("dve", sm[0:P, 16 + 2 * hp:18 + 2 * hp], pbuf[hp][0:P, :].rearrange("p (h c) -> p h c", h=2), AX.X, ALU.add,
                    [pbn[hp]], ["sm"])
            ts("dve", sm[0:P, 16:20], sm[0:P, 16:20], 1e-30, None, ALU.max, None, ["sm"], ["sm"])
            kb.op("dve", lambda e: e.reciprocal(out=sm[0:P, 16:20], in_=sm[0:P, 16:20]), ["sm"], ["sm"])
            for hp in range(2):
                pv3 = pbuf[hp][0:P, :].rearrange("p (h c) -> p h c", h=2)
                tt("dve", pv3, pv3, sm[0:P, 16 + 2 * hp:18 + 2 * hp].unsqueeze(2).to_broadcast([P, 2, 256]), ALU.mult,
                   [pbn[hp], "sm"], [pbn[hp]])
                dst = scS if hp == 0 else scS2
                red("dve", dst[:, 0:128], pbuf[hp][0:P, :].rearrange("p (h b t) -> p b h t", h=2, t=2), AX.XY, ALU.add,
                    [pbn[hp]], ["scS" if hp == 0 else "scS2"])
                cp("pool", omix[0:P, hp * 512:(hp + 1) * 512], pbuf[hp][0:P, :], [pbn[hp]], ["omix"])
            tt("dve", scS[:, 0:128], scS[:, 0:128], scS2[:, 0:128], ALU.add, ["scS", "scS2"], ["scS"])
            tt("dve", scS[:, 0:128], scS[:, 0:128], bonsS[:, 0:128], ALU.add, ["scS", "bonsS"], ["scS"])
            cp("dve", scS[:, 128:136], bonsS[:, 128:136], ["bonsS", "scS"], ["scS"])
            kb.op("dve", lambda e: e.max(out=m8[0:P, 0:8], in_=scS), ["scS"], ["m8"])
            kb.op("dve", lambda e: e.match_replace(out=scS2, in_to_replace=m8[0:P, 0:8], in_values=scS, imm_value=-1e30),
                  ["scS", "m8"], ["scS2"])
            kb.op("dve", lambda e: e.max(out=m8[0:P, 8:16], in_=scS2), ["scS2"], ["m8"])
            ts("dve", scS2, scS, m8[0:P, 15:16], None, ALU.is_ge, None, ["scS", "m8"], ["scS2"])
            ts("dve", selbS, scS2[:, 0:128], -NEGB, NEGB, ALU.mult, ALU.add, ["scS2"], ["selbS"])
            for half in range(2):
                tr(pT[0:64, half * 128:half * 128 + P], selbS[:, half * 64:(half + 1) * 64], identb[0:P, 0:P],
                   ["selbS", "identb"], ["pT"])
            for half in range(2):
                cp("act", selTS[:, g, half, :].rearrange("b (h q) -> b h q", h=4),
                   pT[0:64, half * 128:half * 128 + 8].unsqueeze(1).to_broadcast([64, 4, 8]), ["pT"], ["selTS"])
            for hc in range(8):
                tr(pT[:, hc * 128:hc * 128 + P], omix[0:P, hc * 128:(hc + 1) * 128], identb[0:P, 0:P], ["omix", "identb"], ["pT"])
            cp("act", PT[1][:, 0:64].rearrange("c (a q) -> c a q", a=8), pTv[:, :, 0:8], ["pT"], ["PT1"])
            for h in range(4):
                for half in range(2):
                    hc = h * 2 + half
                    mm(pC[0:P, h * 64:(h + 1) * 64], PT[1][:, hc * 8:(hc + 1) * 8], Vc[:, g, half, :],
                       h == 0 and half == 0, half == 1, ["PT1", "Vc"], ["pC"], skip=True)
            oat3 = oat[0:P, g * 256:(g + 1) * 256].rearrange("p (h d) -> p h d", d=64)
            g3 = gates[0:P, g * 12:(g + 1) * 12].rearrange("p (h b) -> p h b", b=3)
            tt("dve", oat3, pC[0:P, 0:256].rearrange("p (h d) -> p h d", d=64), g3[:, :, 0:1].to_broadcast([P, 4, 64]),
               ALU.mult, ["pC", "gates"], ["pj_q"])

        paccs = [pO, pW]
        pnames = ["pO", "pW"]

        rot = [0]

        def key_tile_s(g, klhs, kres, qoff, extra, nk, r):
            bank = pS[g][0:nk, r * 32:(r + 1) * 32]
            bn = ("pSq", g, r)
            Qv = QTv[0:67, 4 * g:4 * g + 4, qoff:qoff + 8]
            mm(bank, klhs, Qv, True, len(extra) == 0, [kres, "KsT", "KwT", "KnT", "QT"], [bn])
            for xi, (el, er, eres) in enumerate(extra):
                mm(bank, el, er, False, xi == len(extra) - 1, list(eres), [bn])
            act(PT[g][0:nk, r * 32:(r + 1) * 32], bank, AF.Exp, [bn], [("PTq", g, r)])

        def key_tile_v(g, vv, vres, first, nk, r):
            for h in range(4):
                mm(paccs[g][0:P, h, :], PT[g][0:nk, r * 32 + h * 8:r * 32 + (h + 1) * 8], vv, first and h == 0, False,
                   [("PTq", g, r), vres, "Vs", "Vw", "Vn"], [pnames[g]], skip=True)

        def key_tile(g, klhs, kres, qoff, extra, vv, vres, first, nk):
            r = rot[0] % 4
            rot[0] += 1
            key_tile_s(g, klhs, kres, qoff, extra, nk, r)
            key_tile_v(g, vv, vres, first, nk, r)

        def combine(br):
            for g in range(2):
                pacc = paccs[g]
                g3 = gates[0:P, g * 12:(g + 1) * 12].rearrange("p (h b) -> p h b", b=3)
                ts("dve", ocf[0:P, 0:4], pacc[0:P, :, 64], 1e-30, None, ALU.max, None, [pnames[g]], ["ocf"])
                kb.op("dve", lambda e: e.reciprocal(out=ocf[0:P, 0:4], in_=ocf[0:P, 0:4]), ["ocf"], ["ocf"])
                tt("dve", ocf[0:P, 4:8], ocf[0:P, 0:4], g3[:, :, 1 + br], ALU.mult, ["ocf", "gates"], ["ocf"])
                tt("dve", tmpa[0:P, 0:256].rearrange("p (h d) -> p h d", d=64), pacc[0:P, :, 0:64],
                   ocf[0:P, 4:8].unsqueeze(2).to_broadcast([P, 4, 64]), ALU.mult, [pnames[g], "ocf"], ["tmpa"])
                tt("pool", oat[0:P, g * 256:(g + 1) * 256], oat[0:P, g * 256:(g + 1) * 256], tmpa[0:P, 0:256], ALU.add,
                   ["pj_q", "tmpa"], ["pj_q"])

        fence()
        for j in range(64):
            k = j % 4
            half = j // 32
            jj = j % 32
            slot = j % 6
            gather_page(l, j, k, True)
            kq = j % 2
            cp("dve", knb[:, kq * 128:(kq + 1) * 128], pg[k][:, 0:128], [pgn[k]], [("knbq", kq)])
            for g in range(2):
                tr(pT[0:64, kq * 256 + g * 128:kq * 256 + (g + 1) * 128], knb[:, kq * 128 + g * 64:kq * 128 + (g + 1) * 64], identb,
                   [("knbq", kq), "identb"], [("pTq", kq)])
            for g in range(2):
                cp("act", KwT[0:64, g, slot * 128:(slot + 1) * 128], pT[0:64, kq * 256 + g * 128:kq * 256 + (g + 1) * 128],
                   [("pTq", kq)], [("KwT", slot)])
                cp("dve", KwT[64:67, g, slot * 128:(slot + 1) * 128], KsT[64:67, g, jj * 128:(jj + 1) * 128], ["KsT"],
                   [("KwT", slot)])
            cp("dve", Vw[:, slot, :, 0:64], pg[k][:, 128:256].rearrange("p (h d) -> p h d", d=64), [pgn[k]], [("Vw", slot)])
            for g in range(2):
                key_tile_s(g, KwT[0:67, g, slot * 128:(slot + 1) * 128], ("KwT", slot), half * 8,
                           [(stairb[0:64, jj * 128:(jj + 1) * 128], selTS[:, g, half, :], ("stairb", "selTS"))], 128, j % 4)
            for g in range(2):
                key_tile_v(g, Vw[:, slot, g, :], ("Vw", slot), j == 0, 128, j % 4)
        for g in range(2):
            key_tile(g, KnT[0:67, 0, g, :], "KnT", 0, [(identb[0:8, 0:8], cmask8, ("identb", "cmask8"))],
                     Vn[:, 0, g, :], "Vn", False, 8)
        combine(0)
        wm8 = wmaskb.rearrange("k (h q) -> k h q", h=4)[:, :, 0:8]
        for r in range(4):
            k = r % 4
            slot = r
            kb.dma(pg[k], cw_in[l, bl, r * 128:(r + 1) * 128, :], [], [pgn[k]])
            kq = r % 2
            cp("dve", knb[:, kq * 128:(kq + 1) * 128], pg[k][:, 0:128], [pgn[k]], [("knbq", kq)])
            for g in range(2):
                tr(pT[0:64, kq * 256 + g * 128:kq * 256 + (g + 1) * 128], knb[:, kq * 128 + g * 64:kq * 128 + (g + 1) * 64], identb,
                   [("knbq", kq), "identb"], [("pTq", kq)])
            for g in range(2):
                cp("act", KwT[0:64, g, slot * 128:(slot + 1) * 128], pT[0:64, kq * 256 + g * 128:kq * 256 + (g + 1) * 128],
                   [("pTq", kq)], [("KwT", slot)])
                cp("dve", KwT[64:67, g, slot * 128:(slot + 1) * 128], KsT[64:67, g, (28 + r) * 128:(29 + r) * 128], ["KsT"],
                   [("KwT", slot)])
            cp("dve", Vw[:, slot, :, 0:64], pg[k][:, 128:256].rearrange("p (h d) -> p h d", d=64), [pgn[k]], [("Vw", slot)])
            for g in range(2):
                extra = [(identb, wm8, ("identb", "wmaskb"))] if r == 0 else []
                key_tile(g, KwT[0:67, g, slot * 128:(slot + 1) * 128], ("KwT", slot), 8, extra,
                         Vw[:, slot, g, :], ("Vw", slot), r == 0, 128)
        for g in range(2):
            key_tile(g, KnT[0:67, 1, g, :], "KnT", 0, [(identb[0:8, 0:8], cmask8, ("identb", "cmask8"))],
                     Vn[:, 1, g, :], "Vn", False, 8)
        combine(1)

        fence()
        act(tmpa[0:P, :], proj[0:P, C_ZA:C_ZA + 512], AF.Silu, ["pj_za"], ["tmpa"])
        tt("dve", omix[0:P, 0:512], oat[0:P, :], tmpa[0:P, :], ALU.mult, ["pj_q", "tmpa"], ["omix"])
        kb.op("pool", lambda e: e.memset(carry, 0.0), ["carry"], ["carry"])
        kb.dma(carry[126:128, :], sc_in[l, bl], ["carry"], ["carry"])
        tt("pool", u[0:P, :], proj[0:P, C_CG:C_CG + 512], proj[0:P, C_HV:C_HV + 512], ALU.mult, ["pj_cg", "pj_hv"], ["pj_cg"])
        tt("pool", tmpa[0:P, :], u[0:P, :], cw[0:P, 0, :], ALU.mult, ["pj_cg", "gsm"], ["tmpa"])
        tt("pool", pcm[0:P, :], u[0:P, :], cw[0:P, 1, :], ALU.mult, ["pj_cg", "gsm"], ["pcm"])
        tt("pool", tmpb[0:P, :], u[0:P, :], cw[0:P, 2, :], ALU.mult, ["pj_cg", "gsm"], ["tmpb"])
        tt("pool", tmpa[64:128, :], carry[64:128, :], cw[64:128, 0, :], ALU.mult, ["carry", "gsm"], ["tmpa"])
        tt("pool", pcm[64:128, :], carry[64:128, :], cw[64:128, 1, :], ALU.mult, ["carry", "gsm"], ["pcm"])
        mm(pA[0][0:P, :], shm[0:P, 0, 0:P], tmpa[0:P, :], True, False, ["shm", "tmpa"], ["pA0"])
        mm(pA[0][0:P, :], shm[0:P, 1, 0:P], pcm[0:P, :], False, False, ["shm", "pcm"], ["pA0"])
        mm(pA[0][0:P, :], shm[64:128, 2, 0:P], tmpa[64:128, :], False, False, ["shm", "tmpa"], ["pA0"])
        mm(pA[0][0:P, :], shm[64:128, 3, 0:P], pcm[64:128, :], False, True, ["shm", "pcm"], ["pA0"])
        tt("dve", tmpb[0:P, :], pA[0][0:P, :], tmpb[0:P, :], ALU.add, ["pA0", "tmpb"], ["tmpb"])
        tt("dve", tmpb[0:P, :], tmpb[0:P, :], proj[0:P, C_BG:C_BG + 512], ALU.mult, ["tmpb", "pj_bg"], ["tmpb"])
        act(tmpa[0:P, :], proj[0:P, C_ZB:C_ZB + 512], AF.Silu, ["pj_zb"], ["tmpa"])
        tt("dve", omix[0:P, 512:1024], tmpb[0:P, :], tmpa[0:P, :], ALU.mult, ["tmpb", "tmpa"], ["omix"])
        kb.dma(conv_s[l, bl], proj[6:8, C_CG:C_CG + 512], ["pj_cg"], [("conv_s", l, bl)])
        for c in range(8):
            tr(pT[:, c * 128:c * 128 + P], omix[0:P, c * 128:(c + 1) * 128], identb[0:P, 0:P], ["omix", "identb"], ["pT"])
        cp("act", xT, pT, ["pT"], ["xT"])
        for cg in range(2):
            bn = "pA%d" % cg
            for kc in range(8):
                mm(pA[cg][0:P, :], xT[:, kc * 128:kc * 128 + P], W_out[:, kc, cg * 512:(cg + 1) * 512], kc == 0, kc == 7,
                   ["xT", "W_out"], [bn])
            tt("dve", X[0:P, cg * 512:(cg + 1) * 512], pA[cg][0:P, :], X[0:P, cg * 512:(cg + 1) * 512], ALU.add, [bn, xn], [xn])
        kb.op("dve", lambda e: e.memset(st[:, 2:4], 0.0), ["st"], ["st"])
        act(omix[0:P, :], X[0:P, :], AF.Square, [xn, "st"], ["omix", "st"], accum_out=st[0:P, 2:3])
        rstd(st[0:P, 2:3], st[0:P, 3:4], 1024.0, ["st"])
        cp("dve", xb[0:P, :], X[0:P, :], [xn], ["omix"])
        for c in range(8):
            tr(pT[:, c * 128:c * 128 + P], xb[0:P, c * 128:(c + 1) * 128], identb[0:P, 0:P], ["omix", "identb"], ["pT"])
        cp("act", xT, pT, ["pT"], ["xT"])
        cp("pool", pb[0:P, :], pt[0][0:P, :], ["pt0"], ["pb"])
        for c in range(2):
            tr(pT[:, c * 128:c * 128 + P], pb[0:P, c * 128:(c + 1) * 128], identb[0:P, 0:P], ["pb", "identb"], ["pT"])
        cp("act", pTT, pT[:, 0:256], ["pT"], ["pTT"])
        for cg in range(2):
            for kc in range(8):
                mm(pA[0][0:P, :], xT[:, kc * 128:kc * 128 + P], W_pg[:, kc, cg * 512:(cg + 1) * 512], kc == 0, kc == 7,
                   ["xT", "W_pg"], ["pA0"])
            act(sg[0:P, :], pA[0][0:P, :], AF.Sigmoid, ["pA0", "st"], ["tmpa"], scale=st[0:P, 3:4])
            for kc in range(2):
                mm(pA[1][0:P, :], pTT[:, kc * 128:kc * 128 + P], W_ple[:, kc, cg * 512:(cg + 1) * 512], kc == 0, kc == 1,
                   ["pTT", "W_ple"], ["pA1"])
            tt("dve", sg[0:P, :], pA[1][0:P, :], sg[0:P, :], ALU.mult, ["pA1", "tmpa"], ["tmpa"])
            tt("pool", X[0:P, cg * 512:(cg + 1) * 512], X[0:P, cg * 512:(cg + 1) * 512], sg[0:P, :], ALU.add, [xn, "tmpa"], [xn])
        dst = y_s if last_layer else xs_scr
        kb.dma(dst[bl], X[0:P, :], [xn], [("xsd", bl)])

    for l in range(n_layers):
        load_layer(l)
        kb.op("dve", lambda e: e.memset(Vc, 0.0), ["Vc"], ["Vc"])
        kb.op("dve", lambda e: e.memset(KcT[0:64, :, :], 0.0), ["KcT"], ["KcT"])
        for i in range(NT):
            load_tile_inputs(l, i)
            prompt_tile(l, i)
        for bl in range(n_sb):
            sample_tile(l, bl)
    stats = kb.finalize()
    return nc, stats


def _prep_inputs(inp, n_layers, n_tiles, b, sbs=(0, 1, 2, 3)):
    S = n_tiles * 128
    sbs = list(sbs)
    c = _consts(n_tiles)
    f = lambda a: np.ascontiguousarray(np.asarray(a, dtype=np.float32))
    m = {
        "xp": f(inp["x_prompt"][b, :S]),
        "pp": f(inp["p_prompt"][:n_layers, b, :S]),
        "g_norm": f(inp["g_norm"][:n_layers]),
        "w_in": f(inp["w_in"][:n_layers]),
        "g_q": f(inp["g_q"][:n_layers]),
        "g_k": f(inp["g_k"][:n_layers]),
        "cmp_pos": f(inp["cmp_pos"][:n_layers]),
        "w_phi": f(inp["w_phi"][:n_layers]),
        "conv_w": f(inp["conv_w"][:n_layers]),
        "w_out": f(inp["w_out"][:n_layers]),
        "w_ple": f(inp["w_ple"][:n_layers]),
        "w_pg": f(inp["w_pg"][:n_layers]),
        "g_ple": f(inp["g_ple"][:n_layers]),
        "xs": f(inp["x_sample"][sbs]),
        "pps": f(inp["p_sample"][:n_layers][:, sbs]),
        "pool": np.asarray(inp["cache_kv"][:n_layers], dtype=np.float32).reshape(-1, 512),
        "cwin": f(inp["cache_win"][:n_layers][:, sbs]).reshape(n_layers, len(sbs), 512, 256),
        "scin": f(inp["state_conv"][:n_layers][:, sbs]),
        "ptab": np.ascontiguousarray(np.asarray(inp["page_table"], dtype=np.int32)[sbs]),
    }
    for k, v in c.items():
        m["c_" + k] = v
    return m


_PROG = {}


def kernel(**inputs):
    inp = {k: np.asarray(v) for k, v in inputs.items()}
    n_pool = inp["cache_kv"].shape[1]
    key = ("p", n_pool)
    if key not in _PROG:
        _PROG[key] = build_program(DEPTH, 32, n_pool=n_pool, n_sb=4)
    nc, _ = _PROG[key]
    in_maps = [_prep_inputs(inp, DEPTH, 32, c // 2, sbs=range(4 * c, 4 * c + 4)) for c in range(8)]
    res = run_bass_kernel_spmd(nc, in_maps, core_ids=list(range(8)))
    r = res.results
    y_p = np.stack([r[2 * b]["y_p"] for b in range(4)])
    kv_p = np.stack([r[2 * b]["kv_p"] for b in range(4)], 1).reshape(DEPTH, 4, SEQ, 4, 2, 64)
    win_p = np.stack([r[2 * b]["win_p"] for b in range(4)], 1).reshape(DEPTH, 4, 512, 2, 2, 64)
    conv_p = np.stack([r[2 * b]["conv_p"] for b in range(4)], 1)
    y_s = np.concatenate([r[c]["y_s"] for c in range(8)], 0)
    kv_s = np.concatenate([r[c]["kv_s"] for c in range(8)], 1).reshape(DEPTH, 32, 8, 4, 2, 64)
    win_s = np.concatenate([r[c]["win_s"] for c in range(8)], 1).reshape(DEPTH, 32, 512, 2, 2, 64)
    conv_s = np.concatenate([r[c]["conv_s"] for c in range(8)], 1)
    return (y_p, y_s, kv_p, win_p, conv_p, kv_s, win_s, conv_s)
```

```python
import numpy as np
import concourse.bass as bass
import concourse.mybir as mybir
from concourse.bass_utils import run_bass_kernel_spmd

F32 = mybir.dt.float32
BF16 = mybir.dt.bfloat16
I32 = mybir.dt.int32
AF = mybir.ActivationFunctionType
ALU = mybir.AluOpType
AX = mybir.AxisListType

N_DMA_SEMS = 40

D_MODEL = 1024
SEQ = 4096
DEPTH = 4
D_IN = 3864
D_PLE = 256
NEGB = -30000.0
EPS = 1e-6

C_Q, C_KC, C_VC, C_KS, C_VS, C_KW, C_VW, C_GL, C_ZA, C_BG, C_CG, C_HV, C_ZB = (
    0, 512, 640, 768, 896, 1024, 1152, 1280, 1304, 1816, 2328, 2840, 3352)
COL_GROUPS = [(0, 512, "pj_q"), (512, 1024, "pj_kv"), (1024, 1304, "pj_w"), (1304, 1816, "pj_za"),
              (1816, 2328, "pj_bg"), (2328, 2840, "pj_cg"), (2840, 3352, "pj_hv"), (3352, 3864, "pj_zb")]


class _Op:
    __slots__ = ("eng", "fn", "reads", "writes", "dma", "deps", "needs_inc", "cnt", "dsem", "dval")

    def __init__(self, eng, fn, reads, writes, dma):
        self.eng = eng
        self.fn = fn
        self.reads = reads
        self.writes = writes
        self.dma = dma
        self.deps = ()
        self.needs_inc = False
        self.cnt = 0
        self.dsem = -1
        self.dval = 0


class KB:
    def __init__(self, nc):
        self.nc = nc
        self.engs = {"pe": nc.tensor, "act": nc.scalar, "dve": nc.vector, "pool": nc.gpsimd, "sp": nc.sync}
        self.ops = []

    def op(self, eng, fn, reads=(), writes=(), dma=False):
        self.ops.append(_Op(eng, fn, tuple(reads), tuple(writes), dma))

    def dma(self, out, in_, reads, writes, q="sp", **kw):
        self.op(q, lambda e: e.dma_start(out=out, in_=in_, **kw), reads, writes, dma=True)

    def finalize(self):
        nc = self.nc
        ops = self.ops
        last_w = {}
        readers = {}
        for i, o in enumerate(ops):
            deps = set()
            for r in o.reads:
                j = last_w.get(r)
                if j is not None:
                    deps.add(j)
            for w in o.writes:
                j = last_w.get(w)
                if j is not None:
                    deps.add(j)
                for j in readers.get(w, ()):
                    deps.add(j)
            deps.discard(i)
            keep = []
            for j in deps:
                p = ops[j]
                if (not p.dma) and (not o.dma) and p.eng == o.eng:
                    if o.eng == "pe":
                        continue
                    raw = any(last_w.get(r) == j for r in o.reads) or any(last_w.get(w) == j for w in o.writes)
                    if not raw:
                        continue
                keep.append(j)
            o.deps = tuple(keep)
            for j in keep:
                if not ops[j].dma:
                    ops[j].needs_inc = True
            for r in o.reads:
                readers.setdefault(r, []).append(i)
            for w in o.writes:
                last_w[w] = i
                readers[w] = []
        cnt = {k: 0 for k in self.engs}
        ndma = 0
        dvals = [0] * N_DMA_SEMS
        for o in ops:
            if o.dma:
                k = ndma % N_DMA_SEMS
                ndma += 1
                dvals[k] += 16
                o.dsem = k
                o.dval = dvals[k]
            elif o.needs_inc:
                cnt[o.eng] += 1
                o.cnt = cnt[o.eng]
        esem = {k: nc.alloc_semaphore("es_" + k) for k in self.engs}
        dsem = [nc.alloc_semaphore("ds_%d" % k) for k in range(N_DMA_SEMS)]
        wm_e = {k: {k2: 0 for k2 in self.engs} for k in self.engs}
        wm_d = {k: [0] * N_DMA_SEMS for k in self.engs}
        n_wait = 0
        for o in ops:
            E = self.engs[o.eng]
            need_e = {}
            need_d = {}
            for j in o.deps:
                p = ops[j]
                if p.dma:
                    need_d[p.dsem] = max(need_d.get(p.dsem, 0), p.dval)
                else:
                    need_e[p.eng] = max(need_e.get(p.eng, 0), p.cnt)
            if o.dma and o.dval > 16:
                need_d[o.dsem] = max(need_d.get(o.dsem, 0), o.dval - 16)
            for f, c in need_e.items():
                if wm_e[o.eng][f] < c:
                    E.wait_ge(esem[f], c)
                    wm_e[o.eng][f] = c
                    n_wait += 1
            for k, v in need_d.items():
                if wm_d[o.eng][k] < v:
                    E.wait_ge(dsem[k], v)
                    wm_d[o.eng][k] = v
                    n_wait += 1
            ins = o.fn(E)
            if o.dma:
                ins.then_inc(dsem[o.dsem], 16)
            elif o.needs_inc:
                ins.then_inc(esem[o.eng], 1)
        sp = self.engs["sp"]
        for k in range(N_DMA_SEMS):
            if dvals[k] > 0:
                sp.wait_ge(dsem[k], dvals[k])
        self.stats = dict(n_ops=len(ops), n_wait=n_wait, n_dma=ndma, cnt=cnt)
        return self.stats


def _consts(n_tiles):
    c = {}
    c["ident"] = np.eye(128, dtype=np.float32)
    kpos = np.arange(SEQ)
    c["kaug"] = np.stack([128.0 * (kpos // 128), (kpos % 128).astype(np.float64), np.ones(SEQ)]).astype(np.float32)
    ce = np.arange(128) * 32 + 31
    c["kcaug"] = np.stack([128.0 * (ce // 128), (ce % 128).astype(np.float64), np.ones(128)]).astype(np.float32)
    m = 2.0 ** (-(np.arange(8) + 1.0))
    qa = np.zeros((3, 8, 128), np.float32)
    qa[0] = m[:, None]
    qa[1] = m[:, None]
    qa[2] = -64.0 * m[:, None]
    c["qaug"] = np.ascontiguousarray(qa[:, :, 0])
    qm = np.ones((3, 65), np.float32)
    qm[2] = 2 * np.arange(65) + 1
    c["qmul"] = qm
    k = np.arange(128)[:, None]
    t = np.arange(128)[None, :]
    cb = np.where(k <= t, 0.0, NEGB).astype(np.float32)
    wb = np.where(t < k, 0.0, NEGB).astype(np.float32)
    c["cmask"] = np.tile(cb, (1, 4))
    c["wmask"] = np.tile(wb, (1, 4))
    st = np.zeros((64, SEQ), np.float32)
    st[np.arange(SEQ) // 64, np.arange(SEQ)] = 1.0
    c["stair"] = st
    cp = np.arange(252)[None, :] - 124
    tl = np.arange(128)[:, None]
    c["mcmp"] = np.where(32 * cp + 31 <= tl, 0.0, NEGB).astype(np.float32)
    bon = np.zeros((32, 128, 64), np.float32)
    for qb in range(32):
        tt = (qb * 128 + np.arange(128))[:, None]
        bb = np.arange(64)[None, :]
        cur = tt // 64
        valid = bb * 64 <= tt
        forced = (bb == 0) | (bb == cur) | (bb == cur - 1)
        bon[qb] = np.where(valid, np.where(forced, 1e3, 0.0), -1e9)
    c["bonus"] = bon
    sh = np.zeros((4, 128, 128), np.float32)
    for tq in range(128):
        if tq - 2 >= 0:
            sh[0, tq - 2, tq] = 1.0
        if tq - 1 >= 0:
            sh[1, tq - 1, tq] = 1.0
    sh[2, 126, 0] = 1.0
    sh[2, 127, 1] = 1.0
    sh[3, 127, 0] = 1.0
    c["shm"] = sh
    c["iota"] = np.arange(128, dtype=np.float32).reshape(128, 1)
    c["knaug"] = np.stack([np.full(8, 8192.0), np.arange(8.0), np.ones(8)]).astype(np.float32)
    bs = np.full((8, 136), -1e9, np.float32)
    bs[:, 0:129] = 0.0
    bs[:, [0, 127, 128]] = 1e3
    c["bons"] = bs
    kk = np.arange(8)[:, None]
    qq = np.arange(8)[None, :]
    c["cmask8"] = np.tile(np.where(kk <= qq, 0.0, NEGB).astype(np.float32), (1, 4))
    return c


def build_program(n_layers=DEPTH, n_tiles=32, debug=False, n_pool=2560, n_sb=4):
    nc = bass.Bass("TRN2", target_bir_lowering=False)
    kb = KB(nc)
    NT = n_tiles
    S = NT * 128

    def din(name, shape, dt=F32):
        return nc.dram_tensor(name, list(shape), dt, kind="ExternalInput").ap()

    def dout(name, shape, dt=F32):
        return nc.dram_tensor(name, list(shape), dt, kind="ExternalOutput").ap()

    def sb(name, shape, dt=F32):
        return nc.alloc_sbuf_tensor(name, list(shape), dt).ap()

    def ps(name, shape, dt=F32):
        return nc.alloc_psum_tensor(name, list(shape), dt).ap()

    xp = din("xp", [S, D_MODEL])
    pp = din("pp", [n_layers, S, D_PLE])
    g_norm = din("g_norm", [n_layers, D_MODEL])
    w_in = din("w_in", [n_layers, D_MODEL, D_IN])
    g_q = din("g_q", [n_layers, 64])
    g_k = din("g_k", [n_layers, 3, 64])
    cmp_pos = din("cmp_pos", [n_layers, 2, 32, 64])
    w_phi = din("w_phi", [n_layers, 2, 32, 64, 64])
    conv_w = din("conv_w", [n_layers, 3, 512])
    w_out = din("w_out", [n_layers, 1024, 1024])
    w_ple = din("w_ple", [n_layers, 256, 1024])
    w_pg = din("w_pg", [n_layers, 1024, 1024])
    g_ple = din("g_ple", [n_layers, 1024])
    c_ident = din("c_ident", [128, 128])
    c_kaug = din("c_kaug", [3, SEQ])
    c_kcaug = din("c_kcaug", [3, 128])
    c_qaug = din("c_qaug", [3, 8])
    c_qmul = din("c_qmul", [3, 65])
    c_iota = din("c_iota", [128, 1])
    c_knaug = din("c_knaug", [3, 8])
    c_bons = din("c_bons", [8, 136])
    c_cmask8 = din("c_cmask8", [8, 32])
    c_cmask = din("c_cmask", [128, 512])
    c_wmask = din("c_wmask", [128, 512])
    c_stair = din("c_stair", [64, SEQ])
    c_mcmp = din("c_mcmp", [128, 252])
    c_bonus = din("c_bonus", [32, 128, 64])
    c_shm = din("c_shm", [4, 128, 128])

    xs_in = din("xs", [n_sb, 8, D_MODEL])
    pps = din("pps", [n_layers, n_sb, 8, D_PLE])
    pool = din("pool", [n_layers * n_pool * 128, 512])
    cw_in = din("cwin", [n_layers, n_sb, 512, 256])
    sc_in = din("scin", [n_layers, n_sb, 2, 512])
    ptab = din("ptab", [n_sb, 64], I32)
    y_s = dout("y_s", [n_sb, 8, D_MODEL])
    kv_s = dout("kv_s", [n_layers, n_sb, 8, 512])
    win_s = dout("win_s", [n_layers, n_sb, 512, 256])
    conv_s = dout("conv_s", [n_layers, n_sb, 2, 512])
    xs_scr = nc.dram_tensor("xs_scr", [n_sb, 8, D_MODEL], F32, kind="Internal").ap()
    y_p = dout("y_p", [S, D_MODEL])
    kv_p = dout("kv_p", [n_layers, S, 512])
    WROWS = min(512, S)
    win_p = dout("win_p", [n_layers, WROWS, 256])
    conv_p = dout("conv_p", [n_layers, 2, 512])
    xscr = nc.dram_tensor("xscr", [S, D_MODEL], F32, kind="Internal").ap()

    W_in = sb("W_in", [128, 8, D_IN], BF16)
    W_out = sb("W_out", [128, 8, 1024], BF16)
    W_pg = sb("W_pg", [128, 8, 1024], BF16)
    W_ple = sb("W_ple", [128, 2, 1024], BF16)
    W_phi = sb("W_phi", [64, 2, 32, 64], BF16)
    gn = sb("gn", [128, 8])
    gp = sb("gp", [128, 8])
    gq8 = sb("gq8", [128, 64])
    gk = sb("gk", [128, 3, 64])
    cw = sb("cw", [128, 3, 512])
    posrep = sb("posrep", [128, 2, 64])

    identb = sb("identb", [128, 128], BF16)
    shm = sb("shm", [128, 4, 128])
    cmaskb = sb("cmaskb", [128, 512], BF16)
    wmaskb = sb("wmaskb", [128, 512], BF16)
    stairb = sb("stairb", [128, SEQ], BF16)
    mcmp = sb("mcmp", [128, 252])
    qaugf = sb("qaugf", [128, 8])
    qmulf = sb("qmulf", [128, 65])

    KsT = sb("KsT", [128, 2, SEQ], BF16)
    KwT = sb("KwT", [128, 2, 6 * 128], BF16)
    Vs = sb("Vs", [128, 32, 2, 65], BF16)
    Vw = sb("Vw", [128, 6, 2, 65], BF16)
    KcT = sb("KcT", [128, 2, 256], BF16)
    Vc = sb("Vc", [128, 2, 2, 64], BF16)

    xt = [sb("xt0", [128, 1024])] * 2
    pt = [sb("pt0", [128, 256])] * 2
    bon = [sb("bon%d" % j, [128, 64]) for j in range(2)]
    st = sb("st", [128, 8])
    xT = sb("xT", [128, 1024], BF16)
    proj = sb("proj", [128, D_IN])
    tmpa = sb("tmpa", [128, 512])
    tmpb = sb("tmpb", [128, 512])
    stg = [tmpa, tmpb]
    sm = sb("sm", [128, 64])
    qnb = sb("qnb", [128, 512], BF16)
    QT = sb("QT", [128, 1024], BF16)
    knb = sb("knb", [128, 256], BF16)
    gates = sb("gates", [128, 24])
    kcp = sb("kcp", [128, 256], BF16)
    kcT = sb("kcT", [64, 4, 128], BF16)
    cst = sb("cst", [8, 8])
    kcn = sb("kcn", [8, 64])
    kcnb = sb("kcnb", [8, 64], BF16)
    vcnb = sb("vcnb", [8, 64], BF16)
    pcm = sb("pcm", [128, 512])
    score = sb("score", [128, 64])
    score2 = sb("score2", [128, 64])
    m8 = sb("m8", [128, 16])
    selb = sb("selb", [128, 64], BF16)
    selTg = [sb("selT%d" % j, [128, 512], BF16) for j in range(2)]
    PT = [sb("PT%d" % j, [128, 512], BF16) for j in range(2)]
    pnb = PT[0]
    pnT = PT[1]
    junkb = pcm.bitcast(BF16)
    PT2 = [sb("PTx%d" % j, [128, 512], BF16) for j in range(2)]
    iota = sb("iota", [128, 1])
    idxi = sb("idxi", [128, 64], I32)
    idxf = sb("idxf", [128, 64])
    KnT = sb("KnT", [128, 2, 2, 8], BF16)
    Vn = sb("Vn", [8, 2, 2, 65], BF16)
    scS = sb("scS", [8, 136])
    scS2 = sb("scS2", [8, 136])
    selbS = sb("selbS", [8, 128], BF16)
    selTS = sb("selTS", [64, 2, 2, 32], BF16)
    bonsS = sb("bonsS", [8, 136])
    cmask8 = sb("cmask8", [8, 32], BF16)
    oat = proj[:, C_Q:C_Q + 512]
    ocf = sb("ocf", [128, 16])
    omix = sb("omix", [128, 1024], BF16)
    xb = omix
    u = proj[:, C_CG:C_CG + 512]
    v0 = tmpa
    v1 = pcm
    carry = sb("carry", [128, 512])
    sg = tmpa
    pb = sb("pb", [128, 256], BF16)
    pTT = sb("pTT", [128, 256], BF16)

    pA = [ps("pA%d" % j, [128, 512]) for j in range(2)]
    pT = ps("pT", [128, 1024], BF16)
    pS = [ps("pS%d" % j, [128, 512]) for j in range(2)]
    pO = ps("pO", [128, 4, 65])
    pW = ps("pW", [128, 4, 65])
    pC = ps("pC", [128, 512])

    dumped = set()

    def dump(name, ap, res, shape, dt=F32):
        if not debug or name in dumped:
            return
        dumped.add(name)
        d = nc.dram_tensor("dbg_" + name, list(shape), dt, kind="ExternalOutput").ap()
        kb.dma(d, ap, list(res), [("dbg", name)])

    def mm(out, lhsT, rhs, start, stop, R, W, skip=False):
        if skip:
            kb.op("pe", lambda e: e.matmul(out, lhsT=lhsT, rhs=rhs, start=start, stop=stop, skip_group_check=True), R, W)
        else:
            kb.op("pe", lambda e: e.matmul(out, lhsT=lhsT, rhs=rhs, start=start, stop=stop), R, W)

    def tr(out, in_, ident, R, W):
        kb.op("pe", lambda e: e.transpose(out=out, in_=in_, identity=ident), R, W)

    def act(out, in_, func, R, W, **kw):
        kb.op("act", lambda e: e.activation(out=out, in_=in_, func=func, **kw), R, W)

    def cp(eng, out, in_, R, W):
        if eng == "act":
            kb.op("act", lambda e: e.copy(out=out, in_=in_), R, W)
        else:
            kb.op(eng, lambda e: e.tensor_copy(out=out, in_=in_), R, W)

    def tt(eng, out, in0, in1, op, R, W):
        kb.op(eng, lambda e: e.tensor_tensor(out=out, in0=in0, in1=in1, op=op), R, W)

    def ts(eng, out, in0, s1, s2, op0, op1, R, W):
        if op1 is None:
            kb.op(eng, lambda e: e.tensor_scalar(out=out, in0=in0, scalar1=s1, scalar2=None, op0=op0), R, W)
        else:
            kb.op(eng, lambda e: e.tensor_scalar(out=out, in0=in0, scalar1=s1, scalar2=s2, op0=op0, op1=op1), R, W)

    def red(eng, out, in_, axis, op, R, W):
        kb.op(eng, lambda e: e.tensor_reduce(out=out, in_=in_, axis=axis, op=op), R, W)

    def rstd(col_in, col_out, n, R):
        act(col_out, col_in, AF.Sqrt, R, R, scale=1.0 / n, bias=EPS)
        kb.op("dve", lambda e: e.reciprocal(out=col_out, in_=col_out), R, R)

    kb.dma(tmpa[:, 0:128], c_ident, ["tmpa"], ["tmpa"])
    cp("dve", identb, tmpa[:, 0:128], ["tmpa"], ["identb"])
    kb.dma(shm, c_shm.rearrange("k a b -> a k b"), [], ["shm"])
    kb.dma(mcmp, c_mcmp, [], ["mcmp"])
    kb.dma(qaugf[64:67, :], c_qaug, [], ["qaugf"])
    kb.dma(qmulf[64:67, :], c_qmul, [], ["qmulf"])
    kb.dma(stg[0][:, 0:512], c_cmask, ["tmpa"], ["tmpa"])
    cp("dve", cmaskb, stg[0][:, 0:512], ["tmpa"], ["cmaskb"])
    kb.dma(stg[1][:, 0:512], c_wmask, ["tmpb"], ["tmpb"])
    cp("dve", wmaskb, stg[1][:, 0:512], ["tmpb"], ["wmaskb"])
    kb.op("dve", lambda e: e.memset(KsT[64:128, :, :], 0.0), [], ["KsT"])
    kb.op("dve", lambda e: e.memset(KwT[64:128, :, :], 0.0), [], [("KwT", w_) for w_ in range(6)])
    kb.op("dve", lambda e: e.memset(QT[64:128, :], 0.0), [], ["QT"])
    kb.op("dve", lambda e: e.memset(stairb[64:128, :], 0.0), [], ["stairb"])
    for g_ in range(2):
        kb.op("dve", lambda e, g_=g_: e.memset(selTg[g_][64:128, :], 0.0), [], ["selT%d" % g_])
    for j in range(SEQ // 512):
        s_ = stg[j % 2]
        kb.dma(s_[0:64, :], c_stair[:, j * 512:(j + 1) * 512], [("tmpa", "tmpb")[j % 2]], [("tmpa", "tmpb")[j % 2]])
        cp("dve", stairb[0:64, j * 512:(j + 1) * 512], s_[0:64, :], [("tmpa", "tmpb")[j % 2]], ["stairb"])
    for j in range(SEQ // 512):
        s_ = stg[j % 2]
        kb.dma(s_[64:67, :], c_kaug[:, j * 512:(j + 1) * 512], [("tmpa", "tmpb")[j % 2]], [("tmpa", "tmpb")[j % 2]])
        for g in range(2):
            cp("dve", KsT[64:67, g, j * 512:(j + 1) * 512], s_[64:67, :], [("tmpa", "tmpb")[j % 2]], ["KsT"])
    kb.dma(stg[0][64:67, 0:128], c_kcaug, ["tmpa"], ["tmpa"])
    for g in range(2):
        for hf in range(2):
            cp("dve", KcT[64:67, g, hf * 128:(hf + 1) * 128], stg[0][64:67, 0:128], ["tmpa"], ["KcT"])
    kb.dma(iota, c_iota, [], ["iota"])
    kb.dma(bonsS, c_bons, [], ["bonsS"])
    kb.dma(stg[1][0:8, 0:32], c_cmask8, ["tmpb"], ["tmpb"])
    cp("dve", cmask8, stg[1][0:8, 0:32], ["tmpb"], ["cmask8"])
    kb.dma(stg[1][64:67, 0:8], c_knaug, ["tmpb"], ["tmpb"])
    for a_ in range(2):
        for g in range(2):
            cp("dve", KnT[64:67, a_, g, :], stg[1][64:67, 0:8], ["tmpb"], ["KnT"])
    kb.op("dve", lambda e: e.memset(Vn[:, :, :, 64:65], 1.0), [], ["Vn"])
    kb.op("dve", lambda e: e.memset(Vs[:, :, :, 64:65], 1.0), [], ["Vs"])
    kb.op("dve", lambda e: e.memset(Vw[:, :, :, 64:65], 1.0), [], ["Vw"])

    stg_ctr = [0]

    def load_cast(dst_ap_fn, src_ap_fn, nchunks, scale_col_fn, wname, parts=128):
        for j in range(nchunks):
            k = stg_ctr[0] % 2
            stg_ctr[0] += 1
            sname = ("tmpa", "tmpb")[k]
            src, width = src_ap_fn(j)
            kb.dma(stg[k][0:parts, 0:width], src, [sname], [sname])
            dst = dst_ap_fn(j)
            sc = scale_col_fn(j) if scale_col_fn is not None else None
            eng = "dve" if j % 2 == 0 else "pool"
            if sc is None:
                cp(eng, dst, stg[k][0:parts, 0:width], [sname], [wname])
            else:
                ts(eng, dst, stg[k][0:parts, 0:width], sc, None, ALU.mult, None, [sname, "gvec"], [wname])

    def load_layer(l):
        kb.dma(gn, g_norm[l].rearrange("(c p) -> p c", p=128), [], ["gvec"], allow_slow_non_contiguous=True)
        kb.dma(gp, g_ple[l].rearrange("(c p) -> p c", p=128), [], ["gvec"], allow_slow_non_contiguous=True)
        kb.dma(gq8, g_q[l].partition_broadcast(128), [], ["gsm"])
        kb.dma(gk, g_k[l].partition_broadcast(128), [], ["gsm"])
        kb.dma(cw, conv_w[l].partition_broadcast(128), [], ["gsm"])
        for c4 in range(4):
            kb.dma(posrep[c4 * 32:(c4 + 1) * 32, :, :], cmp_pos[l].rearrange("kv l d -> l kv d"), [], ["gsm"])
        ts("dve", gq8, gq8, 0.125, None, ALU.mult, None, ["gsm"], ["gsm"])
        wcols = [(j * 512, min(512, D_IN - j * 512)) for j in range(8)]
        for kc in range(8):
            load_cast(lambda j, kc=kc: W_in[:, kc, wcols[j][0]:wcols[j][0] + wcols[j][1]],
                      lambda j, kc=kc: (w_in[l, kc * 128:(kc + 1) * 128, wcols[j][0]:wcols[j][0] + wcols[j][1]], wcols[j][1]),
                      8, lambda j, kc=kc: gn[:, kc:kc + 1], "W_in")
        for kc in range(8):
            load_cast(lambda j, kc=kc: W_out[:, kc, j * 512:(j + 1) * 512],
                      lambda j, kc=kc: (w_out[l, kc * 128:(kc + 1) * 128, j * 512:(j + 1) * 512], 512), 2, None, "W_out")
        for kc in range(8):
            load_cast(lambda j, kc=kc: W_pg[:, kc, j * 512:(j + 1) * 512],
                      lambda j, kc=kc: (w_pg[l, kc * 128:(kc + 1) * 128, j * 512:(j + 1) * 512], 512), 2,
                      lambda j, kc=kc: gp[:, kc:kc + 1], "W_pg")
        for kc in range(2):
            load_cast(lambda j, kc=kc: W_ple[:, kc, j * 512:(j + 1) * 512],
                      lambda j, kc=kc: (w_ple[l, kc * 128:(kc + 1) * 128, j * 512:(j + 1) * 512], 512), 2, None, "W_ple")
        for kv in range(2):
            for hh in range(4):
                k = stg_ctr[0] % 2
                stg_ctr[0] += 1
                sname = ("tmpa", "tmpb")[k]
                kb.dma(stg[k][0:64, :].rearrange("d (l e) -> d l e", e=64),
                       w_phi[l, kv, hh * 8:(hh + 1) * 8].rearrange("l d e -> d l e"), [sname], [sname])
                cp("dve", W_phi[:, kv, hh * 8:(hh + 1) * 8, :],
                   stg[k][0:64, :].rearrange("d (l e) -> d l e", e=64), [sname], ["W_phi"])

    def load_tile_inputs(l, i):
        s = i % 2
        src = xp if l == 0 else xscr
        kb.dma(xt[s], src[i * 128:(i + 1) * 128, :], [("xd", i)], ["xt0"])
        kb.dma(pt[s], pp[l, i * 128:(i + 1) * 128, :], [], ["pt0"])
        kb.dma(bon[s], c_bonus[i], [], ["bon%d" % s])

    def prompt_tile(l, i):
        s = i % 2
        X = xt[s]
        xn = "xt0"
        last_layer = (l == n_layers - 1)
        cp("dve", xb, X, [xn], ["omix"])
        for c in range(8):
            tr(pT[:, c * 128:(c + 1) * 128], xb[:, c * 128:(c + 1) * 128], identb, ["omix", "identb"], ["pT"])
        cp("act", xT, pT, ["pT"], ["xT"])
        kb.op("dve", lambda e: e.memset(st, 0.0), ["st"], ["st"])
        act(junkb, X, AF.Square, [xn, "st"], ["pcm", "st"], accum_out=st[:, 0:1])
        rstd(st[:, 0:1], st[:, 1:2], 1024.0, ["st"])
        dump("xT", xT, ["xT"], [128, 1024], BF16)
        dump("xb", xb, ["omix"], [128, 1024], BF16)
        dump("W_in0", W_in[:, 0, 0:512], ["W_in"], [128, 512], BF16)
        dump("st", st, ["st"], [128, 8])
        for gi, (c0, c1, rn) in enumerate(COL_GROUPS):
            w = c1 - c0
            bank = pA[gi % 2]
            bn = "pA%d" % (gi % 2)
            for kc in range(8):
                mm(bank[:, 0:w], xT[:, kc * 128:(kc + 1) * 128], W_in[:, kc, c0:c1], kc == 0, kc == 7,
                   ["xT", "W_in"], [bn])
            act(proj[:, c0:c1], bank[:, 0:w], AF.Copy, [bn, "st"], [rn], scale=st[:, 1:2])
        dump("proj", proj, [r for _, _, r in COL_GROUPS], [128, D_IN])
        tt("dve", kcp.rearrange("p (a h d) -> p a h d", a=2, h=2), proj[:, C_KC:C_KC + 256].rearrange("p (a h d) -> p a h d", a=2, h=2),
           posrep.unsqueeze(2).to_broadcast([128, 2, 2, 64]), ALU.add, ["pj_kv", "gsm"], ["kcp"])
        for j in range(4):
            tr(pT[0:64, j * 128:(j + 1) * 128], kcp[:, j * 64:(j + 1) * 64], identb, ["kcp", "identb"], ["pT"])
        cp("act", kcT.rearrange("d a t -> d (a t)"), pT[0:64, 0:512], ["pT"], ["kcT"])
        for kv in range(2):
            for ll in range(32):
                lhs = kcT[:, 2 * kv:2 * kv + 2, :].rearrange("d h (c l) -> d h c l", l=32)[:, :, :, ll]
                mm(pC[0:8, kv * 64:(kv + 1) * 64], lhs, W_phi[:, kv, ll, :], ll == 0, ll == 31, ["kcT", "W_phi"], ["pC"])
        q3 = proj[:, C_Q:C_Q + 512].rearrange("p (h d) -> p h d", d=64)
        tt("dve", tmpa, proj[:, C_Q:C_Q + 512], proj[:, C_Q:C_Q + 512], ALU.mult, ["pj_q"], ["tmpa"])
        red("dve", sm[:, 0:8], tmpa.rearrange("p (h d) -> p h d", d=64), AX.X, ALU.add, ["tmpa"], ["sm"])
        rstd(sm[:, 0:8], sm[:, 0:8], 64.0, ["sm"])
        tt("dve", tmpa.rearrange("p (h d) -> p h d", d=64), q3, sm[:, 0:8].unsqueeze(2).to_broadcast([128, 8, 64]),
           ALU.mult, ["pj_q", "sm"], ["tmpa"])
        tt("pool", qnb.rearrange("p (h d) -> p h d", d=64), tmpa.rearrange("p (h d) -> p h d", d=64),
           gq8.unsqueeze(1).to_broadcast([128, 8, 64]), ALU.mult, ["tmpa", "gsm"], ["qnb"])
        for h in range(8):
            tr(pT[0:64, h * 128:(h + 1) * 128], qnb[:, h * 64:(h + 1) * 64], identb, ["qnb", "identb"], ["pT"])
        cp("act", QT[0:64, :], pT[0:64, :], ["pT"], ["QT"])
        ts("dve", QT[64:67, :].rearrange("p (h q) -> p h q", h=8), qaugf[64:67, :].unsqueeze(2).to_broadcast([3, 8, 128]),
           qmulf[64:67, i:i + 1], None, ALU.mult, None, ["qaugf", "qmulf"], ["QT"])
        for which, (c0, gi_) in enumerate(((C_KS, 1), (C_KW, 2))):
            rn = "pj_kv" if which == 0 else "pj_w"
            k3 = proj[:, c0:c0 + 128].rearrange("p (h d) -> p h d", d=64)
            tt("dve", tmpb[:, 0:128], proj[:, c0:c0 + 128], proj[:, c0:c0 + 128], ALU.mult, [rn], ["tmpb"])
            red("dve", sm[:, 8 + 2 * which:10 + 2 * which], tmpb[:, 0:128].rearrange("p (h d) -> p h d", d=64),
                AX.X, ALU.add, ["tmpb"], ["sm"])
            rstd(sm[:, 8 + 2 * which:10 + 2 * which], sm[:, 8 + 2 * which:10 + 2 * which], 64.0, ["sm"])
            tt("dve", k3, k3, sm[:, 8 + 2 * which:10 + 2 * which].unsqueeze(2).to_broadcast([128, 2, 64]),
               ALU.mult, [rn, "sm"], [rn])
            tt("dve", k3, k3, gk[:, gi_, :].unsqueeze(1).to_broadcast([128, 2, 64]), ALU.mult, [rn, "gsm"], [rn])
            cp("pool", knb[:, which * 128:(which + 1) * 128], proj[:, c0:c0 + 128], [rn], ["knb"])
        for j in range(4):
            tr(pT[0:64, j * 128:(j + 1) * 128], knb[:, j * 64:(j + 1) * 64], identb, ["knb", "identb"], ["pT"])
        ws = i % 6
        for g in range(2):
            cp("act", KsT[0:64, g, i * 128:(i + 1) * 128], pT[0:64, g * 128:(g + 1) * 128], ["pT"], [("KsT", i)])
            cp("act", KwT[0:64, g, ws * 128:(ws + 1) * 128], pT[0:64, (2 + g) * 128:(3 + g) * 128], ["pT"], [("KwT", ws)])
            cp("dve", KwT[64:67, g, ws * 128:(ws + 1) * 128], KsT[64:67, g, i * 128:(i + 1) * 128], ["KsT"], [("KwT", ws)])
        cp("pool", Vs[:, i, :, 0:64], proj[:, C_VS:C_VS + 128].rearrange("p (h d) -> p h d", d=64), ["pj_kv"], [("Vs", i)])
        cp("pool", Vw[:, ws, :, 0:64], proj[:, C_VW:C_VW + 128].rearrange("p (h d) -> p h d", d=64), ["pj_w"], [("Vw", ws)])
        kb.dma(kv_p[l, i * 128:(i + 1) * 128, :], proj[:, C_KC:C_KC + 512], ["pj_kv"], [("kv_p", l, i)])
        r0 = i * 128 - (S - WROWS)
        if r0 >= 0:
            kb.dma(win_p[l, r0:r0 + 128, :], proj[:, C_KW:C_KW + 256], ["pj_w"], [("win_p", l, i)])
        act(gates, proj[:, C_GL:C_GL + 24], AF.Sigmoid, ["pj_w"], ["gates"])
        kb.op("dve", lambda e: e.memset(cst, 0.0), ["cst"], ["cst"])
        act(kcn, pC[0:8, 0:64], AF.Square, ["pC", "cst"], ["kcn", "cst"], accum_out=cst[:, 0:1])
        rstd(cst[:, 0:1], cst[:, 1:2], 64.0, ["cst"])
        ts("dve", kcn, pC[0:8, 0:64], cst[:, 1:2], None, ALU.mult, None, ["pC", "cst"], ["kcn"])
        tt("dve", kcnb, kcn, gk[0:8, 0, :], ALU.mult, ["kcn", "gsm"], ["kcnb"])
        cp("act", vcnb, pC[0:8, 64:128], ["pC"], ["vcnb"])
        tr(pT[0:64, 0:8], kcnb, identb[0:8, 0:8], ["kcnb", "identb"], ["pT"])
        cp("act", KcT[0:64, :, 4 * i:4 * i + 4], pT[0:64, 0:8].rearrange("d (h c) -> d h c", h=2), ["pT"], ["KcT"])
        for g in range(2):
            kb.dma(Vc[4 * i:4 * i + 4, g, 0, :], vcnb[4 * g:4 * g + 4, :], ["vcnb", "Vc"], ["Vc"])

        def cmp_scores(g):
            for h in range(4):
                mm(pC[:, h * 128:(h + 1) * 128], QT[0:67, (4 * g + h) * 128:(4 * g + h + 1) * 128], KcT[0:67, g, 0:128],
                   True, True, ["QT", "KcT"], ["pC"])

        def cmp_chain(g):
            off = 124 - 4 * i
            tt("dve", tmpa.rearrange("p (h c) -> p h c", h=4), pC.rearrange("p (h c) -> p h c", h=4),
               mcmp[:, off:off + 128].unsqueeze(1).to_broadcast([128, 4, 128]), ALU.add, ["pC", "mcmp"], ["tmpa"])
            act(pcm, tmpa, AF.Exp, ["tmpa"], ["pcm"])
            red("dve", sm[:, 16:20], pcm.rearrange("p (h c) -> p h c", h=4), AX.X, ALU.add, ["pcm"], ["sm"])
            ts("dve", sm[:, 16:20], sm[:, 16:20], 1e-30, None, ALU.max, None, ["sm"], ["sm"])
            kb.op("dve", lambda e: e.reciprocal(out=sm[:, 16:20], in_=sm[:, 16:20]), ["sm"], ["sm"])
            tt("dve", pcm.rearrange("p (h c) -> p h c", h=4), pcm.rearrange("p (h c) -> p h c", h=4),
               sm[:, 16:20].unsqueeze(2).to_broadcast([128, 4, 128]), ALU.mult, ["pcm", "sm"], ["pcm"])
            cp("pool", pnb, pcm, ["pcm"], ["PT0"])
            for h in range(4):
                tr(pT[:, (4 + h) * 128:(5 + h) * 128], pnb[:, h * 128:(h + 1) * 128], identb, ["PT0", "identb"], ["pT"])
            cp("act", pnT, pT[:, 512:1024], ["pT"], ["PT1"])
            if i < 8:
                return
            red("dve", score, pcm.rearrange("p (h b t) -> p b h t", h=4, t=2), AX.XY, ALU.add, ["pcm"], ["score"])
            tt("dve", score, score, bon[s], ALU.add, ["score", "bon%d" % s], ["score"])
            kb.op("dve", lambda e: e.max(out=m8[:, 0:8], in_=score), ["score"], ["m8"])
            kb.op("dve", lambda e: e.match_replace(out=score2, in_to_replace=m8[:, 0:8], in_values=score, imm_value=-1e30),
                  ["score", "m8"], ["score2"])
            kb.op("dve", lambda e: e.max(out=m8[:, 8:16], in_=score2), ["score2"], ["m8"])
            ts("dve", score2, score, m8[:, 15:16], None, ALU.is_ge, None, ["score", "m8"], ["score2"])
            ts("dve", selb, score2, -NEGB, NEGB, ALU.mult, ALU.add, ["score2"], ["selb"])
            tr(pT[0:64, 0:128], selb, identb, ["selb", "identb"], ["pT"])
            cp("act", selTg[g][0:64, :].rearrange("b (h q) -> b h q", h=4), pT[0:64, 0:128].unsqueeze(1).to_broadcast([64, 4, 128]),
               ["pT"], ["selT%d" % g])

        def o_cmp(g):
            for h in range(4):
                mm(pC[:, h * 64:(h + 1) * 64], pnT[:, h * 128:(h + 1) * 128], Vc[:, g, 0, :], True, True, ["PT1", "Vc"], ["pC"])
            oat3 = oat[:, g * 256:(g + 1) * 256].rearrange("p (h d) -> p h d", d=64)
            g3 = gates[:, g * 12:(g + 1) * 12].rearrange("p (h b) -> p h b", b=3)
            tt("dve", oat3, pC[:, 0:256].rearrange("p (h d) -> p h d", d=64),
               g3[:, :, 0:1].to_broadcast([128, 4, 64]), ALU.mult, ["pC", "gates"], ["pj_q"])

        def branch(g, br, pacc, pname):
            Qg = QT[0:128, g * 512:(g + 1) * 512]
            kts = list(range(0, i + 1)) if br == 0 else list(range(max(0, i - 4), i + 1))
            pend = None
            for idx, kt in enumerate(kts):
                b2 = idx % 2
                bank = pS[b2]
                bname = "pS%d" % b2
                if br == 0:
                    klhs = KsT[0:128, g, kt * 128:(kt + 1) * 128]
                    kres = ("KsT", kt)
                    vv = Vs[:, kt, g, :]
                    vres = ("Vs", kt)
                else:
                    wsl = kt % 6
                    klhs = KwT[0:128, g, wsl * 128:(wsl + 1) * 128]
                    kres = ("KwT", wsl)
                    vv = Vw[:, wsl, g, :]
                    vres = ("Vw", wsl)
                diag = (kt == i)
                wfirst = (br == 1 and kt == i - 4)
                use_sel = (br == 0 and i >= 8)
                nextra = (1 if use_sel else 0) + (1 if diag else 0) + (1 if wfirst else 0)
                mm(bank, klhs, Qg, True, nextra == 0, [kres, "KsT", "KwT", "QT"], [bname])
                if use_sel:
                    nextra -= 1
                    mm(bank, stairb[:, kt * 128:(kt + 1) * 128], selTg[g], False, nextra == 0, ["stairb", "selT%d" % g], [bname])
                if diag:
                    nextra -= 1
                    mm(bank, identb, cmaskb, False, nextra == 0, ["identb", "cmaskb"], [bname])
                if wfirst:
                    nextra -= 1
                    mm(bank, identb, wmaskb, False, nextra == 0, ["identb", "wmaskb"], [bname])
                act(PT2[b2], bank, AF.Exp, [bname], ["PTx%d" % b2])
                if pend is not None:
                    pend()

                def pv(b2=b2, vv=vv, vres=vres, first=(idx == 0), lastk=(idx == len(kts) - 1)):
                    for h in range(4):
                        mm(pacc[:, h, :], PT2[b2][:, h * 128:(h + 1) * 128], vv, first and h == 0, lastk,
                           ["PTx%d" % b2, vres, "Vs", "Vw"], [pname], skip=True)
                pend = pv
            pend()

        def evac(g, br, pacc, pname):
            g3 = gates[:, g * 12:(g + 1) * 12].rearrange("p (h b) -> p h b", b=3)
            ts("dve", ocf[:, 0:4], pacc[:, :, 64], 1e-30, None, ALU.max, None, [pname], ["ocf"])
            kb.op("dve", lambda e: e.reciprocal(out=ocf[:, 0:4], in_=ocf[:, 0:4]), ["ocf"], ["ocf"])
            tt("dve", ocf[:, 4:8], ocf[:, 0:4], g3[:, :, 1 + br], ALU.mult, ["ocf", "gates"], ["ocf"])
            tt("dve", tmpb[:, 0:256].rearrange("p (h d) -> p h d", d=64), pacc[:, :, 0:64],
               ocf[:, 4:8].unsqueeze(2).to_broadcast([128, 4, 64]), ALU.mult, [pname, "ocf"], ["tmpb"])
            tt("pool", oat[:, g * 256:(g + 1) * 256], oat[:, g * 256:(g + 1) * 256], tmpb[:, 0:256], ALU.add,
               ["pj_q", "tmpb"], ["pj_q"])

        for g in range(2):
            cmp_scores(g)
            branch(g, 1, pW, "pW")
            cmp_chain(g)
            o_cmp(g)
            evac(g, 1, pW, "pW")
        branch(0, 0, pO, "pO")
        branch(1, 0, pW, "pW")
        evac(0, 0, pO, "pO")
        evac(1, 0, pW, "pW")
        act(tmpa, proj[:, C_ZA:C_ZA + 512], AF.Silu, ["pj_za"], ["tmpa"])
        tt("dve", omix[:, 0:512], oat, tmpa, ALU.mult, ["pj_q", "tmpa"], ["omix"])
        tt("pool", u, proj[:, C_CG:C_CG + 512], proj[:, C_HV:C_HV + 512], ALU.mult, ["pj_cg", "pj_hv"], ["pj_cg"])
        tt("pool", v0, u, cw[:, 0, :], ALU.mult, ["pj_cg", "gsm"], ["tmpa"])
        tt("pool", v1, u, cw[:, 1, :], ALU.mult, ["pj_cg", "gsm"], ["pcm"])
        tt("pool", tmpb, u, cw[:, 2, :], ALU.mult, ["pj_cg", "gsm"], ["tmpb"])
        mm(pA[0], shm[:, 0, :], v0, True, False, ["shm", "tmpa"], ["pA0"])
        mm(pA[0], shm[:, 1, :], v1, False, True, ["shm", "pcm"], ["pA0"])
        tt("dve", tmpb, pA[0], tmpb, ALU.add, ["pA0", "tmpb"], ["tmpb"])
        if i > 0:
            tt("dve", tmpb, tmpb, carry, ALU.add, ["tmpb", "carry"], ["tmpb"])
        if i < NT - 1:
            mm(pA[1], shm[:, 2, :], v0, True, False, ["shm", "tmpa"], ["pA1"])
            mm(pA[1], shm[:, 3, :], v1, False, True, ["shm", "pcm"], ["pA1"])
            cp("act", carry, pA[1], ["pA1"], ["carry"])
        tt("dve", tmpb, tmpb, proj[:, C_BG:C_BG + 512], ALU.mult, ["tmpb", "pj_bg"], ["tmpb"])
        act(tmpa, proj[:, C_ZB:C_ZB + 512], AF.Silu, ["pj_zb"], ["tmpa"])
        tt("dve", omix[:, 512:1024], tmpb, tmpa, ALU.mult, ["tmpb", "tmpa"], ["omix"])
        if i == NT - 1:
            kb.dma(conv_p[l], u[126:128, :], ["pj_cg"], [("conv_p", l)])
        for c in range(8):
            tr(pT[:, c * 128:(c + 1) * 128], omix[:, c * 128:(c + 1) * 128], identb, ["omix", "identb"], ["pT"])
        cp("act", xT, pT, ["pT"], ["xT"])
        for cg in range(2):
            bn = "pA%d" % cg
            for kc in range(8):
                mm(pA[cg], xT[:, kc * 128:(kc + 1) * 128], W_out[:, kc, cg * 512:(cg + 1) * 512], kc == 0, kc == 7,
                   ["xT", "W_out"], [bn])
            tt("dve", X[:, cg * 512:(cg + 1) * 512], pA[cg], X[:, cg * 512:(cg + 1) * 512], ALU.add, [bn, xn], [xn])
        cp("dve", xb, X, [xn], ["omix"])
        for c in range(8):
            tr(pT[:, c * 128:(c + 1) * 128], xb[:, c * 128:(c + 1) * 128], identb, ["omix", "identb"], ["pT"])
        cp("act", xT, pT, ["pT"], ["xT"])
        kb.op("dve", lambda e: e.memset(st[:, 2:4], 0.0), ["st"], ["st"])
        act(junkb, X, AF.Square, [xn, "st"], ["pcm", "st"], accum_out=st[:, 2:3])
        rstd(st[:, 2:3], st[:, 3:4], 1024.0, ["st"])
        cp("pool", pb, pt[s], ["pt0"], ["pb"])
        for c in range(2):
            tr(pT[:, c * 128:(c + 1) * 128], pb[:, c * 128:(c + 1) * 128], identb, ["pb", "identb"], ["pT"])
        cp("act", pTT, pT[:, 0:256], ["pT"], ["pTT"])
        for cg in range(2):
            for kc in range(8):
                mm(pA[0], xT[:, kc * 128:(kc + 1) * 128], W_pg[:, kc, cg * 512:(cg + 1) * 512], kc == 0, kc == 7,
                   ["xT", "W_pg"], ["pA0"])
            act(sg, pA[0], AF.Sigmoid, ["pA0", "st"], ["tmpa"], scale=st[:, 3:4])
            for kc in range(2):
                mm(pA[1], pTT[:, kc * 128:(kc + 1) * 128], W_ple[:, kc, cg * 512:(cg + 1) * 512], kc == 0, kc == 1,
                   ["pTT", "W_ple"], ["pA1"])
            tt("dve", sg, pA[1], sg, ALU.mult, ["pA1", "tmpa"], ["tmpa"])
            tt("pool", X[:, cg * 512:(cg + 1) * 512], X[:, cg * 512:(cg + 1) * 512], sg, ALU.add, [xn, "tmpa"], [xn])
        dst = y_p if last_layer else xscr
        kb.dma(dst[i * 128:(i + 1) * 128, :], X, [xn], [("xd", i)])

    pg = [tmpb[:, 0:256], tmpb[:, 256:512], pcm[:, 0:256], pcm[:, 256:512]]
    pgn = ["pgq0", "pgq1", "pgq2", "pgq3"]
    pool2 = pool.rearrange("r (two c) -> (r two) c", two=2)
    idxB = sb("idxB", [128, 64], I32)

    def fence():
        allr = ["tmpb", "pcm", "knb", "pT", "pS0", "pS1", "PT0", "PT1"] + pgn
        allr += [("pSq", g_, r_) for g_ in range(2) for r_ in range(4)] + [("PTq", g_, r_) for g_ in range(2) for r_ in range(4)]
        allr += [("knbq", 0), ("knbq", 1), ("pTq", 0), ("pTq", 1)]
        kb.op("dve", lambda e: e.memset(cst[0:1, 4:5], 0.0), allr, allr)
    QTv = QT.rearrange("p (h q) -> p h q", h=8)
    pTv = pT.rearrange("p (h q) -> p h q", h=8)

    def gather_page(l, j, k, second):
        buf = pg[k]
        src = pool2
        off = (idxB if second else idxi)[:, j:j + 1]
        kb.op("pool", lambda e: e.indirect_dma_start(out=buf, out_offset=None, in_=src,
                                                     in_offset=bass.IndirectOffsetOnAxis(ap=off, axis=0)),
              ["idxi"], [pgn[k]], dma=True)

    def sample_tile(l, bl):
        P = 8
        X = xt[0]
        xn = "xt0"
        last_layer = (l == n_layers - 1)
        src = xs_in if l == 0 else xs_scr
        kb.dma(X[0:P, :], src[bl], [("xsd", bl)], [xn])
        kb.dma(pt[0][0:P, :], pps[l, bl], [], ["pt0"])
        kb.dma(idxi, ptab[bl].partition_broadcast(128), [], ["idxi"])
        cp("dve", idxf, idxi, ["idxi"], ["idxf"])
        ts("dve", idxf, idxf, 128.0, iota[:, 0:1], ALU.mult, ALU.add, ["idxf", "iota"], ["idxf"])
        ts("dve", idxf, idxf, float(l * n_pool * 128), 2.0, ALU.add, ALU.mult, ["idxf"], ["idxf"])
        cp("dve", idxi, idxf, ["idxf"], ["idxi"])
        ts("dve", idxf, idxf, 1.0, None, ALU.add, None, ["idxf"], ["idxf"])
        cp("dve", idxB, idxf, ["idxf"], ["idxi"])
        kb.op("dve", lambda e: e.memset(st, 0.0), ["st"], ["st"])
        act(omix[0:P, :], X[0:P, :], AF.Square, [xn, "st"], ["omix", "st"], accum_out=st[0:P, 0:1])
        rstd(st[0:P, 0:1], st[0:P, 1:2], 1024.0, ["st"])
        cp("dve", xb[0:P, :], X[0:P, :], [xn], ["omix"])
        for c in range(8):
            tr(pT[:, c * 128:c * 128 + P], xb[0:P, c * 128:(c + 1) * 128], identb[0:P, 0:P], ["omix", "identb"], ["pT"])
        cp("act", xT, pT, ["pT"], ["xT"])
        for gi, (c0, c1, rn) in enumerate(COL_GROUPS):
            w = c1 - c0
            bank = pA[gi % 2]
            bn = "pA%d" % (gi % 2)
            for kc in range(8):
                mm(bank[0:P, 0:w], xT[:, kc * 128:kc * 128 + P], W_in[:, kc, c0:c1], kc == 0, kc == 7, ["xT", "W_in"], [bn])
            act(proj[0:P, c0:c1], bank[0:P, 0:w], AF.Copy, [bn, "st"], [rn], scale=st[0:P, 1:2])
        q3 = proj[0:P, C_Q:C_Q + 512].rearrange("p (h d) -> p h d", d=64)
        ta3 = tmpa[0:P, :].rearrange("p (h d) -> p h d", d=64)
        tt("dve", tmpa[0:P, :], proj[0:P, C_Q:C_Q + 512], proj[0:P, C_Q:C_Q + 512], ALU.mult, ["pj_q"], ["tmpa"])
        red("dve", sm[0:P, 0:8], ta3, AX.X, ALU.add, ["tmpa"], ["sm"])
        rstd(sm[0:P, 0:8], sm[0:P, 0:8], 64.0, ["sm"])
        tt("dve", ta3, q3, sm[0:P, 0:8].unsqueeze(2).to_broadcast([P, 8, 64]), ALU.mult, ["pj_q", "sm"], ["tmpa"])
        tt("pool", qnb[0:P, :].rearrange("p (h d) -> p h d", d=64), ta3, gq8[0:P, :].unsqueeze(1).to_broadcast([P, 8, 64]),
           ALU.mult, ["tmpa", "gsm"], ["qnb"])
        for h in range(8):
            tr(pT[0:64, h * 128:h * 128 + P], qnb[0:P, h * 64:(h + 1) * 64], identb[0:P, 0:P], ["qnb", "identb"], ["pT"])
        for vq, qb_ in ((0, 64), (1, 32)):
            cp("act", QTv[0:64, :, vq * 8:vq * 8 + 8], pTv[0:64, :, 0:8], ["pT"], ["QT"])
            ts("dve", QTv[64:67, :, vq * 8:vq * 8 + 8], qaugf[64:67, :].unsqueeze(2).to_broadcast([3, 8, 8]),
               qmulf[64:67, qb_:qb_ + 1], None, ALU.mult, None, ["qaugf", "qmulf"], ["QT"])
        for which, (c0, gi_) in enumerate(((C_KS, 1), (C_KW, 2))):
            rn = "pj_kv" if which == 0 else "pj_w"
            k3 = proj[0:P, c0:c0 + 128].rearrange("p (h d) -> p h d", d=64)
            tt("dve", tmpb[0:P, 0:128], proj[0:P, c0:c0 + 128], proj[0:P, c0:c0 + 128], ALU.mult, [rn], ["tmpb"])
            red("dve", sm[0:P, 8 + 2 * which:10 + 2 * which], tmpb[0:P, 0:128].rearrange("p (h d) -> p h d", d=64),
                AX.X, ALU.add, ["tmpb"], ["sm"])
            rstd(sm[0:P, 8 + 2 * which:10 + 2 * which], sm[0:P, 8 + 2 * which:10 + 2 * which], 64.0, ["sm"])
            tt("dve", k3, k3, sm[0:P, 8 + 2 * which:10 + 2 * which].unsqueeze(2).to_broadcast([P, 2, 64]), ALU.mult, [rn, "sm"], [rn])
            tt("dve", k3, k3, gk[0:P, gi_, :].unsqueeze(1).to_broadcast([P, 2, 64]), ALU.mult, [rn, "gsm"], [rn])
            cp("pool", knb[0:P, which * 128:(which + 1) * 128], proj[0:P, c0:c0 + 128], [rn], ["knb"])
        for j in range(4):
            tr(pT[0:64, j * 128:j * 128 + P], knb[0:P, j * 64:(j + 1) * 64], identb[0:P, 0:P], ["knb", "identb"], ["pT"])
        for which in range(2):
            for g in range(2):
                j = which * 2 + g
                cp("act", KnT[0:64, which, g, :], pT[0:64, j * 128:j * 128 + 8], ["pT"], ["KnT"])
        cp("pool", Vn[:, 0, :, 0:64], proj[0:P, C_VS:C_VS + 128].rearrange("p (h d) -> p h d", d=64), ["pj_kv"], ["Vn"])
        cp("pool", Vn[:, 1, :, 0:64], proj[0:P, C_VW:C_VW + 128].rearrange("p (h d) -> p h d", d=64), ["pj_w"], ["Vn"])
        kb.dma(kv_s[l, bl], proj[0:P, C_KC:C_KC + 512], ["pj_kv"], [("kv_s", l, bl)])
        kb.dma(win_s[l, bl, 504:512, :], proj[0:P, C_KW:C_KW + 256], ["pj_w"], [("win_s", l, bl)])
        kb.dma(win_s[l, bl, 0:504, :], cw_in[l, bl, 8:512, :], [], [("win_s2", l, bl)])
        act(gates[0:P, :], proj[0:P, C_GL:C_GL + 24], AF.Sigmoid, ["pj_w"], ["gates"])

        fence()
        for g in range(2):
            for half in range(2):
                for jj in range(32):
                    j = half * 32 + jj
                    k = jj % 4
                    gather_page(l, j, k, False)
                    pv = pg[k].rearrange("p (a h d) -> p a h d", a=2, h=2)[:, :, g, :]
                    tt("dve", kcp[:, 0:128].rearrange("p (a d) -> p a d", a=2), pv, posrep, ALU.add, [pgn[k], "gsm"], ["kcp"])
                    for a in range(2):
                        tr(pT[0:64, a * 128:(a + 1) * 128], kcp[:, a * 64:(a + 1) * 64], identb, ["kcp", "identb"], ["pT"])
                    cp("act", KsT[0:64, :, jj * 128:(jj + 1) * 128], pT[0:64, 0:256].rearrange("d (a t) -> d a t", a=2),
                       ["pT"], [("KsT", jj)])
                allk = [("KsT", jj) for jj in range(32)]
                for kv in range(2):
                    for ll in range(32):
                        lhs = KsT[0:64, kv, :].rearrange("d (c l) -> d c l", l=32)[:, :, ll]
                        mm(pC[:, kv * 64:(kv + 1) * 64], lhs, W_phi[:, kv, ll, :], ll == 0, ll == 31, allk + ["W_phi"], ["pC"])
                kb.op("dve", lambda e: e.memset(st[:, 4:6], 0.0), ["st"], ["st"])
                act(tmpa[:, 0:64], pC[:, 0:64], AF.Square, ["pC", "st"], ["tmpa", "st"], accum_out=st[:, 4:5])
                rstd(st[:, 4:5], st[:, 5:6], 64.0, ["st"])
                ts("dve", tmpa[:, 0:64], pC[:, 0:64], st[:, 5:6], None, ALU.mult, None, ["pC", "st"], ["tmpa"])
                tt("dve", kcp[:, 0:64], tmpa[:, 0:64], gk[:, 0, :], ALU.mult, ["tmpa", "gsm"], ["kcp"])
                tr(pT[0:64, 0:128], kcp[:, 0:64], identb, ["kcp", "identb"], ["pT"])
                cp("act", KcT[0:64, g, half * 128:(half + 1) * 128], pT[0:64, 0:128], ["pT"], ["KcT"])
                cp("act", Vc[:, g, half, :], pC[:, 64:128], ["pC"], ["Vc"])

        fence()
        for g in range(2):
            pbuf = [tmpa, pcm]
            pbn = ["tmpa", "pcm"]
            for hp in range(2):
                bank = pC if hp == 0 else pS[0]
                bn = "pC" if hp == 0 else "pS0"
                for hh in range(2):
                    h = hp * 2 + hh
                    for half in range(2):
                        mm(bank[0:P, (hh * 2 + half) * 128:(hh * 2 + half + 1) * 128],
                           QTv[0:67, 4 * g + h, half * 8:half * 8 + 8], KcT[0:67, g, half * 128:(half + 1) * 128],
                           True, True, ["QT", "KcT"], [bn])
                act(pbuf[hp][0:P, :], bank[0:P, :], AF.Exp, [bn], [pbn[hp]])
                red("dve", sm[0:P, 16 + 2 * hp:18 + 2 * hp], pbuf[hp][0:P, :].rearrange("p (h c) -> p h c", h=2), AX.X, ALU.add,
                    [pbn[hp]], ["sm"])
            ts("dve", sm[0:P, 16:20], sm[0:P, 16:20], 1e-30, None, ALU.max, None, ["sm"], ["sm"])
            kb.op("dve", lambda e: e.reciprocal(out=sm[0:P, 16:20], in_=sm[0:P, 16:20]), ["sm"], ["sm"])
            for hp in range(2):
                pv3 = pbuf[hp][0:P, :].rearrange("p (h c) -> p h c", h=2)
                tt("dve", pv3, pv3, sm[0:P, 16 + 2 * hp:18 + 2 * hp].unsqueeze(2).to_broadcast([P, 2, 256]), ALU.mult,
                   [pbn[hp], "sm"], [pbn[hp]])
                dst = scS if hp == 0 else scS2
                red("dve", dst[:, 0:128], pbuf[hp][0:P, :].rearrange("p (h b t) -> p b h t", h=2, t=2), AX.XY, ALU.add,
                    [pbn[hp]], ["scS" if hp == 0 else "scS2"])
                cp("pool", omix[0:P, hp * 512:(hp + 1) * 512], pbuf[hp][0:P, :], [pbn[hp]], ["omix"])
            tt("dve", scS[:, 0:128], scS[:, 0:128], scS2[:, 0:128], ALU.add, ["scS", "scS2"], ["scS"])
            tt("dve", scS[:, 0:128], scS[:, 0:128], bonsS[:, 0:128], ALU.add, ["scS", "bonsS"], ["scS"])
            cp("dve", scS[:, 128:136], bonsS[:, 128:136], ["bonsS", "scS"], ["scS"])
            kb.op("dve", lambda e: e.max(out=m8[0:P, 0:8], in_=scS), ["scS"], ["m8"])
            kb.op("dve", lambda e: e.match_replace(out=scS2, in_to_replace=m8[0:P, 0:8], in_values=scS, imm_value=-1e30),
                  ["scS", "m8"], ["scS2"])
            kb.op("dve", lambda e: e.max(out=m8[0:P, 8:16], in_=scS2), ["scS2"], ["m8"])
            ts("dve", scS2, scS, m8[0:P, 15:16], None, ALU.is_ge, None, ["scS", "m8"], ["scS2"])
            ts("dve", selbS, scS2[:, 0:128], -NEGB, NEGB, ALU.mult, ALU.add, ["scS2"], ["selbS"])
            for half in range(2):
                tr(pT[0:64, half * 128:half * 128 + P], selbS[:, half * 64:(half + 1) * 64], identb[0:P, 0:P],
                   ["selbS", "identb"], ["pT"])
            for half in range(2):
                cp("act", selTS[:, g, half, :].rearrange("b (h q) -> b h q", h=4),
                   pT[0:64, half * 128:half * 128 + 8].unsqueeze(1).to_broadcast([64, 4, 8]), ["pT"], ["selTS"])
            for hc in range(8):
                tr(pT[:, hc * 128:hc * 128 + P], omix[0:P, hc * 128:(hc + 1) * 128], identb[0:P, 0:P], ["omix", "identb"], ["pT"])
            cp("act", PT[1][:, 0:64].rearrange("c (a q) -> c a q", a=8), pTv[:, :, 0:8], ["pT"], ["PT1"])
            for h in range(4):
                for half in range(2):
                    hc = h * 2 + half
                    mm(pC[0:P, h * 64:(h + 1) * 64], PT[1][:, hc * 8:(hc + 1) * 8], Vc[:, g, half, :],
                       h == 0 and half == 0, half == 1, ["PT1", "Vc"], ["pC"], skip=True)
            oat3 = oat[0:P, g * 256:(g + 1) * 256].rearrange("p (h d) -> p h d", d=64)
            g3 = gates[0:P, g * 12:(g + 1) * 12].rearrange("p (h b) -> p h b", b=3)
            tt("dve", oat3, pC[0:P, 0:256].rearrange("p (h d) -> p h d", d=64), g3[:, :, 0:1].to_broadcast([P, 4, 64]),
               ALU.mult, ["pC", "gates"], ["pj_q"])

        paccs = [pO, pW]
        pnames = ["pO", "pW"]

        rot = [0]

        def key_tile_s(g, klhs, kres, qoff, extra, nk, r):
            bank = pS[g][0:nk, r * 32:(r + 1) * 32]
            bn = ("pSq", g, r)
            Qv = QTv[0:67, 4 * g:4 * g + 4, qoff:qoff + 8]
            mm(bank, klhs, Qv, True, len(extra) == 0, [kres, "KsT", "KwT", "KnT", "QT"], [bn])
            for xi, (el, er, eres) in enumerate(extra):
                mm(bank, el, er, False, xi == len(extra) - 1, list(eres), [bn])
            act(PT[g][0:nk, r * 32:(r + 1) * 32], bank, AF.Exp, [bn], [("PTq", g, r)])

        def key_tile_v(g, vv, vres, first, nk, r):
            for h in range(4):
                mm(paccs[g][0:P, h, :], PT[g][0:nk, r * 32 + h * 8:r * 32 + (h + 1) * 8], vv, first and h == 0, False,
                   [("PTq", g, r), vres, "Vs", "Vw", "Vn"], [pnames[g]], skip=True)

        def key_tile(g, klhs, kres, qoff, extra, vv, vres, first, nk):
            r = rot[0] % 4
            rot[0] += 1
            key_tile_s(g, klhs, kres, qoff, extra, nk, r)
            key_tile_v(g, vv, vres, first, nk, r)

        def combine(br):
            for g in range(2):
                pacc = paccs[g]
                g3 = gates[0:P, g * 12:(g + 1) * 12].rearrange("p (h b) -> p h b", b=3)
                ts("dve", ocf[0:P, 0:4], pacc[0:P, :, 64], 1e-30, None, ALU.max, None, [pnames[g]], ["ocf"])
                kb.op("dve", lambda e: e.reciprocal(out=ocf[0:P, 0:4], in_=ocf[0:P, 0:4]), ["ocf"], ["ocf"])
                tt("dve", ocf[0:P, 4:8], ocf[0:P, 0:4], g3[:, :, 1 + br], ALU.mult, ["ocf", "gates"], ["ocf"])
                tt("dve", tmpa[0:P, 0:256].rearrange("p (h d) -> p h d", d=64), pacc[0:P, :, 0:64],
                   ocf[0:P, 4:8].unsqueeze(2).to_broadcast([P, 4, 64]), ALU.mult, [pnames[g], "ocf"], ["tmpa"])
                tt("pool", oat[0:P, g * 256:(g + 1) * 256], oat[0:P, g * 256:(g + 1) * 256], tmpa[0:P, 0:256], ALU.add,
                   ["pj_q", "tmpa"], ["pj_q"])

        fence()
        for j in range(64):
            k = j % 4
            half = j // 32
            jj = j % 32
            slot = j % 6
            gather_page(l, j, k, True)
            kq = j % 2
            cp("dve", knb[:, kq * 128:(kq + 1) * 128], pg[k][:, 0:128], [pgn[k]], [("knbq", kq)])
            for g in range(2):
                tr(pT[0:64, kq * 256 + g * 128:kq * 256 + (g + 1) * 128], knb[:, kq * 128 + g * 64:kq * 128 + (g + 1) * 64], identb,
                   [("knbq", kq), "identb"], [("pTq", kq)])
            cp("act", KwT[0:64, :, slot * 128:(slot + 1) * 128],
               pT[0:64, kq * 256:(kq + 1) * 256].rearrange("d (g t) -> d g t", g=2), [("pTq", kq)], [("KwT", slot)])
            cp("dve", KwT[64:67, :, slot * 128:(slot + 1) * 128], KsT[64:67, :, jj * 128:(jj + 1) * 128], ["KsT"],
               [("KwT", slot)])
            cp("dve", Vw[:, slot, :, 0:64], pg[k][:, 128:256].rearrange("p (h d) -> p h d", d=64), [pgn[k]], [("Vw", slot)])
            for g in range(2):
                key_tile_s(g, KwT[0:67, g, slot * 128:(slot + 1) * 128], ("KwT", slot), half * 8,
                           [(stairb[0:64, jj * 128:(jj + 1) * 128], selTS[:, g, half, :], ("stairb", "selTS"))], 128, j % 4)
            for g in range(2):
                key_tile_v(g, Vw[:, slot, g, :], ("Vw", slot), j == 0, 128, j % 4)
        for g in range(2):
            key_tile(g, KnT[0:67, 0, g, :], "KnT", 0, [(identb[0:8, 0:8], cmask8, ("identb", "cmask8"))],
                     Vn[:, 0, g, :], "Vn", False, 8)
        combine(0)
        wm8 = wmaskb.rearrange("k (h q) -> k h q", h=4)[:, :, 0:8]
        for r in range(4):
            k = r % 4
            slot = r
            kb.dma(pg[k], cw_in[l, bl, r * 128:(r + 1) * 128, :], [], [pgn[k]])
            kq = r % 2
            cp("dve", knb[:, kq * 128:(kq + 1) * 128], pg[k][:, 0:128], [pgn[k]], [("knbq", kq)])
            for g in range(2):
                tr(pT[0:64, kq * 256 + g * 128:kq * 256 + (g + 1) * 128], knb[:, kq * 128 + g * 64:kq * 128 + (g + 1) * 64], identb,
                   [("knbq", kq), "identb"], [("pTq", kq)])
            for g in range(2):
                cp("act", KwT[0:64, g, slot * 128:(slot + 1) * 128], pT[0:64, kq * 256 + g * 128:kq * 256 + (g + 1) * 128],
                   [("pTq", kq)], [("KwT", slot)])
                cp("dve", KwT[64:67, g, slot * 128:(slot + 1) * 128], KsT[64:67, g, (28 + r) * 128:(29 + r) * 128], ["KsT"],
                   [("KwT", slot)])
            cp("dve", Vw[:, slot, :, 0:64], pg[k][:, 128:256].rearrange("p (h d) -> p h d", d=64), [pgn[k]], [("Vw", slot)])
            for g in range(2):
                extra = [(identb, wm8, ("identb", "wmaskb"))] if r == 0 else []
                key_tile(g, KwT[0:67, g, slot * 128:(slot + 1) * 128], ("KwT", slot), 8, extra,
                         Vw[:, slot, g, :], ("Vw", slot), r == 0, 128)
        for g in range(2):
            key_tile(g, KnT[0:67, 1, g, :], "KnT", 0, [(identb[0:8, 0:8], cmask8, ("identb", "cmask8"))],
                     Vn[:, 1, g, :], "Vn", False, 8)
        combine(1)

        fence()
        act(tmpa[0:P, :], proj[0:P, C_ZA:C_ZA + 512], AF.Silu, ["pj_za"], ["tmpa"])
        tt("dve", omix[0:P, 0:512], oat[0:P, :], tmpa[0:P, :], ALU.mult, ["pj_q", "tmpa"], ["omix"])
        kb.op("pool", lambda e: e.memset(carry, 0.0), ["carry"], ["carry"])
        kb.dma(carry[126:128, :], sc_in[l, bl], ["carry"], ["carry"])
        tt("pool", u[0:P, :], proj[0:P, C_CG:C_CG + 512], proj[0:P, C_HV:C_HV + 512], ALU.mult, ["pj_cg", "pj_hv"], ["pj_cg"])
        tt("pool", tmpa[0:P, :], u[0:P, :], cw[0:P, 0, :], ALU.mult, ["pj_cg", "gsm"], ["tmpa"])
        tt("pool", pcm[0:P, :], u[0:P, :], cw[0:P, 1, :], ALU.mult, ["pj_cg", "gsm"], ["pcm"])
        tt("pool", tmpb[0:P, :], u[0:P, :], cw[0:P, 2, :], ALU.mult, ["pj_cg", "gsm"], ["tmpb"])
        tt("pool", tmpa[64:128, :], carry[64:128, :], cw[64:128, 0, :], ALU.mult, ["carry", "gsm"], ["tmpa"])
        tt("pool", pcm[64:128, :], carry[64:128, :], cw[64:128, 1, :], ALU.mult, ["carry", "gsm"], ["pcm"])
        mm(pA[0][0:P, :], shm[0:P, 0, 0:P], tmpa[0:P, :], True, False, ["shm", "tmpa"], ["pA0"])
        mm(pA[0][0:P, :], shm[0:P, 1, 0:P], pcm[0:P, :], False, False, ["shm", "pcm"], ["pA0"])
        mm(pA[0][0:P, :], shm[64:128, 2, 0:P], tmpa[64:128, :], False, False, ["shm", "tmpa"], ["pA0"])
        mm(pA[0][0:P, :], shm[64:128, 3, 0:P], pcm[64:128, :], False, True, ["shm", "pcm"], ["pA0"])
        tt("dve", tmpb[0:P, :], pA[0][0:P, :], tmpb[0:P, :], ALU.add, ["pA0", "tmpb"], ["tmpb"])
        tt("dve", tmpb[0:P, :], tmpb[0:P, :], proj[0:P, C_BG:C_BG + 512], ALU.mult, ["tmpb", "pj_bg"], ["tmpb"])
        act(tmpa[0:P, :], proj[0:P, C_ZB:C_ZB + 512], AF.Silu, ["pj_zb"], ["tmpa"])
        tt("dve", omix[0:P, 512:1024], tmpb[0:P, :], tmpa[0:P, :], ALU.mult, ["tmpb", "tmpa"], ["omix"])
        kb.dma(conv_s[l, bl], proj[6:8, C_CG:C_CG + 512], ["pj_cg"], [("conv_s", l, bl)])
        for c in range(8):
            tr(pT[:, c * 128:c * 128 + P], omix[0:P, c * 128:(c + 1) * 128], identb[0:P, 0:P], ["omix", "identb"], ["pT"])
        cp("act", xT, pT, ["pT"], ["xT"])
        for cg in range(2):
            bn = "pA%d" % cg
            for kc in range(8):
                mm(pA[cg][0:P, :], xT[:, kc * 128:kc * 128 + P], W_out[:, kc, cg * 512:(cg + 1) * 512], kc == 0, kc == 7,
                   ["xT", "W_out"], [bn])
            tt("dve", X[0:P, cg * 512:(cg + 1) * 512], pA[cg][0:P, :], X[0:P, cg * 512:(cg + 1) * 512], ALU.add, [bn, xn], [xn])
        kb.op("dve", lambda e: e.memset(st[:, 2:4], 0.0), ["st"], ["st"])
        act(omix[0:P, :], X[0:P, :], AF.Square, [xn, "st"], ["omix", "st"], accum_out=st[0:P, 2:3])
        rstd(st[0:P, 2:3], st[0:P, 3:4], 1024.0, ["st"])
        cp("dve", xb[0:P, :], X[0:P, :], [xn], ["omix"])
        for c in range(8):
            tr(pT[:, c * 128:c * 128 + P], xb[0:P, c * 128:(c + 1) * 128], identb[0:P, 0:P], ["omix", "identb"], ["pT"])
        cp("act", xT, pT, ["pT"], ["xT"])
        cp("pool", pb[0:P, :], pt[0][0:P, :], ["pt0"], ["pb"])
        for c in range(2):
            tr(pT[:, c * 128:c * 128 + P], pb[0:P, c * 128:(c + 1) * 128], identb[0:P, 0:P], ["pb", "identb"], ["pT"])
        cp("act", pTT, pT[:, 0:256], ["pT"], ["pTT"])
        for cg in range(2):
            for kc in range(8):
                mm(pA[0][0:P, :], xT[:, kc * 128:kc * 128 + P], W_pg[:, kc, cg * 512:(cg + 1) * 512], kc == 0, kc == 7,
                   ["xT", "W_pg"], ["pA0"])
            act(sg[0:P, :], pA[0][0:P, :], AF.Sigmoid, ["pA0", "st"], ["tmpa"], scale=st[0:P, 3:4])
            for kc in range(2):
                mm(pA[1][0:P, :], pTT[:, kc * 128:kc * 128 + P], W_ple[:, kc, cg * 512:(cg + 1) * 512], kc == 0, kc == 1,
                   ["pTT", "W_ple"], ["pA1"])
            tt("dve", sg[0:P, :], pA[1][0:P, :], sg[0:P, :], ALU.mult, ["pA1", "tmpa"], ["tmpa"])
            tt("pool", X[0:P, cg * 512:(cg + 1) * 512], X[0:P, cg * 512:(cg + 1) * 512], sg[0:P, :], ALU.add, [xn, "tmpa"], [xn])
        dst = y_s if last_layer else xs_scr
        kb.dma(dst[bl], X[0:P, :], [xn], [("xsd", bl)])

    for l in range(n_layers):
        load_layer(l)
        kb.op("dve", lambda e: e.memset(Vc, 0.0), ["Vc"], ["Vc"])
        kb.op("dve", lambda e: e.memset(KcT[0:64, :, :], 0.0), ["KcT"], ["KcT"])
        for i in range(NT):
            load_tile_inputs(l, i)
            prompt_tile(l, i)
        for bl in range(n_sb):
            sample_tile(l, bl)
    stats = kb.finalize()
    return nc, stats


def _prep_inputs(inp, n_layers, n_tiles, b, sbs=(0, 1, 2, 3)):
    S = n_tiles * 128
    sbs = list(sbs)
    c = _consts(n_tiles)
    f = lambda a: np.ascontiguousarray(np.asarray(a, dtype=np.float32))
    m = {
        "xp": f(inp["x_prompt"][b, :S]),
        "pp": f(inp["p_prompt"][:n_layers, b, :S]),
        "g_norm": f(inp["g_norm"][:n_layers]),
        "w_in": f(inp["w_in"][:n_layers]),
        "g_q": f(inp["g_q"][:n_layers]),
        "g_k": f(inp["g_k"][:n_layers]),
        "cmp_pos": f(inp["cmp_pos"][:n_layers]),
        "w_phi": f(inp["w_phi"][:n_layers]),
        "conv_w": f(inp["conv_w"][:n_layers]),
        "w_out": f(inp["w_out"][:n_layers]),
        "w_ple": f(inp["w_ple"][:n_layers]),
        "w_pg": f(inp["w_pg"][:n_layers]),
        "g_ple": f(inp["g_ple"][:n_layers]),
        "xs": f(inp["x_sample"][sbs]),
        "pps": f(inp["p_sample"][:n_layers][:, sbs]),
        "pool": np.asarray(inp["cache_kv"][:n_layers], dtype=np.float32).reshape(-1, 512),
        "cwin": f(inp["cache_win"][:n_layers][:, sbs]).reshape(n_layers, len(sbs), 512, 256),
        "scin": f(inp["state_conv"][:n_layers][:, sbs]),
        "ptab": np.ascontiguousarray(np.asarray(inp["page_table"], dtype=np.int32)[sbs]),
    }
    for k, v in c.items():
        m["c_" + k] = v
    return m


_PROG = {}


def kernel(**inputs):
    inp = {k: np.asarray(v) for k, v in inputs.items()}
    n_pool = inp["cache_kv"].shape[1]
    key = ("p", n_pool)
    if key not in _PROG:
        _PROG[key] = build_program(DEPTH, 32, n_pool=n_pool, n_sb=4)
    nc, _ = _PROG[key]
    in_maps = [_prep_inputs(inp, DEPTH, 32, c // 2, sbs=range(4 * c, 4 * c + 4)) for c in range(8)]
    res = run_bass_kernel_spmd(nc, in_maps, core_ids=list(range(8)))
    r = res.results
    y_p = np.stack([r[2 * b]["y_p"] for b in range(4)])
    kv_p = np.stack([r[2 * b]["kv_p"] for b in range(4)], 1).reshape(DEPTH, 4, SEQ, 4, 2, 64)
    win_p = np.stack([r[2 * b]["win_p"] for b in range(4)], 1).reshape(DEPTH, 4, 512, 2, 2, 64)
    conv_p = np.stack([r[2 * b]["conv_p"] for b in range(4)], 1)
    y_s = np.concatenate([r[c]["y_s"] for c in range(8)], 0)
    kv_s = np.concatenate([r[c]["kv_s"] for c in range(8)], 1).reshape(DEPTH, 32, 8, 4, 2, 64)
    win_s = np.concatenate([r[c]["win_s"] for c in range(8)], 1).reshape(DEPTH, 32, 512, 2, 2, 64)
    conv_s = np.concatenate([r[c]["conv_s"] for c in range(8)], 1)
    return (y_p, y_s, kv_p, win_p, conv_p, kv_s, win_s, conv_s)
```
